# Optimizing a Trainium2 kernel written in Bass

```python
import jax, jax.numpy as jnp
from jax import lax
import numpy as np


D_MODEL = 2048
BATCH = 4
SEQ = 2048
DEPTH = 1

MEM_LEN = 256
HEAD_DIM = 128
NSA_HEADS = 8
NSA_KV_HEADS = 2
NSA_GROUP = NSA_HEADS // NSA_KV_HEADS
NSA_WIDTH = NSA_HEADS * HEAD_DIM
KV_WIDTH = NSA_KV_HEADS * HEAD_DIM
N_BRANCH = 3
CONV_WIDTH = D_MODEL - NSA_WIDTH
CONV_KERNEL = 31
CMP_LEN = 32
CMP_STRIDE = 16
SEL_BLK = 64
N_SEL = 16
WINDOW = 512
Q_BLOCK = 128
SEL_Q_CHUNK = 64
MEM_HEADS = 4
MEM_WIDTH = MEM_HEADS * HEAD_DIM
FFN_HIDDEN = -(-8 * D_MODEL // (3 * 256)) * 256
IN_SIZES = (NSA_WIDTH, KV_WIDTH, KV_WIDTH, KV_WIDTH, KV_WIDTH, KV_WIDTH, KV_WIDTH, N_BRANCH * NSA_HEADS, 2 * CONV_WIDTH)
IN_WIDTH = sum(IN_SIZES)
NEG_INF = -1e30
FORCE_BONUS = 1e3

kernel_name = 'hymba_nsa_conformer_hybrid_block'


def rms_norm(x, g, eps=1e-6):
    xf = x.astype(jnp.float32)
    y = xf * lax.rsqrt(jnp.mean(xf * xf, axis=-1, keepdims=True) + eps)
    return (y * g.astype(jnp.float32)).astype(x.dtype)


def layer_norm(x, g, b, eps=1e-5):
    xf = x.astype(jnp.float32)
    mu = jnp.mean(xf, axis=-1, keepdims=True)
    var = jnp.mean(jnp.square(xf - mu), axis=-1, keepdims=True)
    y = (xf - mu) * lax.rsqrt(var + eps) * g.astype(jnp.float32) + b.astype(jnp.float32)
    return y.astype(x.dtype)


def masked_softmax(s, mask, axis=-1):
    p = jax.nn.softmax(jnp.where(mask, s, NEG_INF), axis=axis)
    return p * mask


def alibi_slopes(n):
    return jnp.exp2(-8.0 * (jnp.arange(n, dtype=jnp.float32) + 1.0) / n)


def compress(kv, pos_emb, w1, w2):
    B, T, G, dh = kv.shape
    n_cmp = (T - CMP_LEN) // CMP_STRIDE + 1
    idx = jnp.arange(n_cmp)[:, None] * CMP_STRIDE + jnp.arange(CMP_LEN)[None, :]
    blocks = kv[:, idx] + pos_emb[:, None, :]
    blocks = blocks.transpose(0, 1, 3, 2, 4).reshape(B, n_cmp, G, CMP_LEN * dh)
    return jax.nn.silu(blocks @ w1) @ w2


def compressed_attention(q, k, v, slopes):
    B, T, G, Hg, dh = q.shape
    n_cmp = k.shape[1]
    s = jnp.einsum('btghd,bcgd->bghtc', q, k).astype(jnp.float32) * dh ** -0.5
    t = jnp.arange(T)
    start = jnp.arange(n_cmp) * CMP_STRIDE
    mid = start.astype(jnp.float32) + (CMP_LEN - 1) / 2.0
    dist = t[:, None].astype(jnp.float32) - mid[None, :]
    mask = (start[None, :] + CMP_LEN - 1) <= t[:, None]
    s = s - slopes[None, :, :, None, None] * dist
    p = masked_softmax(s, mask)
    o = jnp.einsum('bghtc,bcgd->btghd', p.astype(v.dtype), v)
    return o, p


def select_blocks(p_cmp, T):
    n_cmp = p_cmp.shape[-1]
    n_blk = T // SEL_BLK
    n_sel = min(N_SEL, n_blk)
    cs = jnp.arange(n_cmp)[:, None] * CMP_STRIDE
    bs = jnp.arange(n_blk)[None, :] * SEL_BLK
    overlap = jnp.clip(jnp.minimum(cs + CMP_LEN, bs + SEL_BLK) - jnp.maximum(cs, bs), 0, None).astype(jnp.float32) / CMP_LEN
    imp = jnp.einsum('bghtc,cs->bgts', p_cmp, overlap)
    t = jnp.arange(T)[:, None]
    blk = jnp.arange(n_blk)[None, :]
    cur = t // SEL_BLK
    valid = blk * SEL_BLK <= t
    forced = (blk == 0) | (blk == cur) | (blk == cur - 1)
    score = jnp.where(valid, imp + FORCE_BONUS * forced, NEG_INF)
    _, idx = lax.top_k(score, n_sel)
    return idx


def selected_attention(q, k, v, idx, slopes):
    B, T, G, Hg, dh = q.shape
    n_blk = T // SEL_BLK
    n_sel = idx.shape[-1]
    n_ch = T // SEL_Q_CHUNK
    kb = k.reshape(B, n_blk, SEL_BLK, G, dh).transpose(0, 3, 1, 2, 4)
    vb = v.reshape(B, n_blk, SEL_BLK, G, dh).transpose(0, 3, 1, 2, 4)
    q_ch = q.reshape(B, n_ch, SEL_Q_CHUNK, G, Hg, dh).transpose(1, 0, 2, 3, 4, 5)
    i_ch = idx.reshape(B, G, n_ch, SEL_Q_CHUNK, n_sel).transpose(2, 0, 1, 3, 4)
    t_ch = jnp.arange(T).reshape(n_ch, SEL_Q_CHUNK)
    bi = jnp.arange(B)[:, None, None, None]
    gi = jnp.arange(G)[None, :, None, None]
    offs = jnp.arange(SEL_BLK)

    def chunk(args):
        qc, ic, tc = args
        ks = kb[bi, gi, ic]
        vs = vb[bi, gi, ic]
        s = jnp.einsum('bqghd,bgqnkd->bghqnk', qc, ks).astype(jnp.float32) * dh ** -0.5
        pos = ic[..., None] * SEL_BLK + offs
        dist = (tc[:, None, None] - pos)[:, :, None]
        s = s - slopes[None, :, :, None, None, None] * dist.astype(jnp.float32)
        p = masked_softmax(s, dist >= 0, axis=(-2, -1))
        return jnp.einsum('bghqnk,bgqnkd->bqghd', p.astype(vs.dtype), vs)

    o = lax.map(chunk, (q_ch, i_ch, t_ch))
    return o.transpose(1, 0, 2, 3, 4, 5).reshape(B, T, G, Hg, dh)


def window_attention(q, k, v, slopes):
    B, T, G, Hg, dh = q.shape
    nb = T // Q_BLOCK
    span = WINDOW + Q_BLOCK
    pad = ((0, 0), (WINDOW, 0), (0, 0), (0, 0))
    kp = jnp.pad(k, pad)
    vp = jnp.pad(v, pad)
    idx = jnp.arange(nb)[:, None] * Q_BLOCK + jnp.arange(span)[None, :]
    kw = kp[:, idx]
    vw = vp[:, idx]
    qb = q.reshape(B, nb, Q_BLOCK, G, Hg, dh)
    s = jnp.einsum('bnqghd,bnkgd->bnghqk', qb, kw).astype(jnp.float32) * dh ** -0.5
    t = jnp.arange(nb)[:, None] * Q_BLOCK + jnp.arange(Q_BLOCK)[None, :]
    spos = idx - WINDOW
    dist = t[:, :, None] - spos[:, None, :]
    mask = (dist >= 0) & (dist < WINDOW) & (spos[:, None, :] >= 0)
    s = s - slopes[None, None, :, :, None, None] * dist[None, :, None, None].astype(jnp.float32)
    p = masked_softmax(s, mask[None, :, None, None])
    o = jnp.einsum('bnghqk,bnkgd->bnqghd', p.astype(vw.dtype), vw)
    return o.reshape(B, T, G, Hg, dh)


def conformer_conv(u, conv_w, conv_b, ln_g, ln_b):
    a, b = jnp.split(u, 2, axis=-1)
    h = a * jax.nn.sigmoid(b)
    C = h.shape[-1]
    h = lax.conv_general_dilated(h, conv_w[:, None, :].astype(h.dtype), (1,), [(CONV_KERNEL - 1, 0)],
                                 dimension_numbers=('NWC', 'WIO', 'NWC'), feature_group_count=C) + conv_b
    return jax.nn.silu(layer_norm(h, ln_g, ln_b))


def memory_cross_attention(hq, hm, w_mq, w_mk, w_mv, mq_norm, mk_norm, w_mo):
    B, T, _ = hq.shape
    M = hm.shape[1]
    q = rms_norm((hq @ w_mq).reshape(B, T, MEM_HEADS, HEAD_DIM), mq_norm)
    k = rms_norm((hm @ w_mk).reshape(B, M, MEM_HEADS, HEAD_DIM), mk_norm)
    v = (hm @ w_mv).reshape(B, M, MEM_HEADS, HEAD_DIM)
    s = jnp.einsum('bthd,bmhd->bhtm', q, k).astype(jnp.float32) * HEAD_DIM ** -0.5
    p = jax.nn.softmax(s, axis=-1)
    o = jnp.einsum('bhtm,bmhd->bthd', p.astype(v.dtype), v).reshape(B, T, MEM_WIDTH)
    return o @ w_mo


def swiglu(h, w_gate, w_up, w_down):
    return (jax.nn.silu(h @ w_gate) * (h @ w_up)) @ w_down


def setup_inputs(seed: int = 0) -> dict:
    key = jax.random.key(seed)
    ks = jax.random.split(key, 40)
    f32 = jnp.float32

    def nrm(k, shape, scale):
        return jax.random.normal(k, shape, f32) * scale

    def gain(k, n):
        return 1.0 + 0.01 * jax.random.normal(k, (DEPTH, n), f32)

    L, dh = DEPTH, HEAD_DIM
    return {
        'x': nrm(ks[0], (BATCH, SEQ, D_MODEL), 1.0),
        'mem': nrm(ks[1], (BATCH, MEM_LEN, D_MODEL), 1.0),
        'norm_mix': gain(ks[2], D_MODEL),
        'w_in': nrm(ks[3], (L, D_MODEL, IN_WIDTH), D_MODEL ** -0.5),
        'gate_b': nrm(ks[4], (L, N_BRANCH * NSA_HEADS), 0.01),
        'q_norm': gain(ks[5], dh),
        'k_norm_cmp': gain(ks[6], dh),
        'k_norm_slc': gain(ks[7], dh),
        'k_norm_win': gain(ks[8], dh),
        'cmp_pos_k': nrm(ks[9], (L, CMP_LEN, dh), 0.1),
        'cmp_pos_v': nrm(ks[10], (L, CMP_LEN, dh), 0.1),
        'cmp_k_w1': nrm(ks[11], (L, CMP_LEN * dh, dh), (CMP_LEN * dh) ** -0.5),
        'cmp_k_w2': nrm(ks[12], (L, dh, dh), dh ** -0.5),
        'cmp_v_w1': nrm(ks[13], (L, CMP_LEN * dh, dh), (CMP_LEN * dh) ** -0.5),
        'cmp_v_w2': nrm(ks[14], (L, dh, dh), dh ** -0.5),
        'conv_w': nrm(ks[15], (L, CONV_KERNEL, CONV_WIDTH), CONV_KERNEL ** -0.5),
        'conv_b': nrm(ks[16], (L, CONV_WIDTH), 0.01),
        'conv_ln_g': gain(ks[17], CONV_WIDTH),
        'conv_ln_b': nrm(ks[18], (L, CONV_WIDTH), 0.01),
        'w_out': nrm(ks[19], (L, NSA_WIDTH + CONV_WIDTH, D_MODEL), (NSA_WIDTH + CONV_WIDTH) ** -0.5),
        'norm_mem_q': gain(ks[20], D_MODEL),
        'norm_mem_kv': gain(ks[21], D_MODEL),
        'w_mq': nrm(ks[22], (L, D_MODEL, MEM_WIDTH), D_MODEL ** -0.5),
        'w_mk': nrm(ks[23], (L, D_MODEL, MEM_WIDTH), D_MODEL ** -0.5),
        'w_mv': nrm(ks[24], (L, D_MODEL, MEM_WIDTH), D_MODEL ** -0.5),
        'mq_norm': gain(ks[25], dh),
        'mk_norm': gain(ks[26], dh),
        'w_mo': nrm(ks[27], (L, MEM_WIDTH, D_MODEL), MEM_WIDTH ** -0.5),
        'norm_ffn': gain(ks[28], D_MODEL),
        'w_gate': nrm(ks[29], (L, D_MODEL, FFN_HIDDEN), D_MODEL ** -0.5),
        'w_up': nrm(ks[30], (L, D_MODEL, FFN_HIDDEN), D_MODEL ** -0.5),
        'w_down': nrm(ks[31], (L, FFN_HIDDEN, D_MODEL), FFN_HIDDEN ** -0.5),
    }


def reference(x, mem, norm_mix, w_in, gate_b, q_norm, k_norm_cmp, k_norm_slc, k_norm_win,
              cmp_pos_k, cmp_pos_v, cmp_k_w1, cmp_k_w2, cmp_v_w1, cmp_v_w2,
              conv_w, conv_b, conv_ln_g, conv_ln_b, w_out,
              norm_mem_q, norm_mem_kv, w_mq, w_mk, w_mv, mq_norm, mk_norm, w_mo,
              norm_ffn, w_gate, w_up, w_down):
    B, T, _ = x.shape
    G, Hg, dh = NSA_KV_HEADS, NSA_GROUP, HEAD_DIM
    slopes = alibi_slopes(NSA_HEADS).reshape(G, Hg)
    splits = np.cumsum(IN_SIZES)[:-1].tolist()
    for l in range(DEPTH):
        h = rms_norm(x, norm_mix[l])
        z = h @ w_in[l]
        q, kc, vc, ksl, vsl, kw, vw, g, u = jnp.split(z, splits, axis=-1)
        q = rms_norm(q.reshape(B, T, NSA_HEADS, dh), q_norm[l]).reshape(B, T, G, Hg, dh)
        kv_shape = (B, T, G, dh)
        k_cmp = rms_norm(compress(kc.reshape(kv_shape), cmp_pos_k[l], cmp_k_w1[l], cmp_k_w2[l]), k_norm_cmp[l])
        v_cmp = compress(vc.reshape(kv_shape), cmp_pos_v[l], cmp_v_w1[l], cmp_v_w2[l])
        o_cmp, p_cmp = compressed_attention(q, k_cmp, v_cmp, slopes)
        sel_idx = select_blocks(p_cmp, T)
        o_slc = selected_attention(q, rms_norm(ksl.reshape(kv_shape), k_norm_slc[l]), vsl.reshape(kv_shape), sel_idx, slopes)
        o_win = window_attention(q, rms_norm(kw.reshape(kv_shape), k_norm_win[l]), vw.reshape(kv_shape), slopes)
        gates = jax.nn.sigmoid(g + gate_b[l]).reshape(B, T, G, Hg, N_BRANCH)
        o_nsa = (gates[..., 0:1] * o_cmp + gates[..., 1:2] * o_slc + gates[..., 2:3] * o_win).reshape(B, T, NSA_WIDTH)
        o_conv = conformer_conv(u, conv_w[l], conv_b[l], conv_ln_g[l], conv_ln_b[l])
        x = x + jnp.concatenate([o_nsa, o_conv], axis=-1) @ w_out[l]
        x = x + memory_cross_attention(rms_norm(x, norm_mem_q[l]), rms_norm(mem, norm_mem_kv[l]),
                                       w_mq[l], w_mk[l], w_mv[l], mq_norm[l], mk_norm[l], w_mo[l])
        x = x + swiglu(rms_norm(x, norm_ffn[l]), w_gate[l], w_up[l], w_down[l])
    return x
```

```python
import numpy as np
import ml_dtypes
from contextlib import ExitStack
import concourse.bass as bass
import concourse.mybir as mybir
from concourse.bass_utils import run_bass_kernel_spmd

F32 = mybir.dt.float32
BF16 = mybir.dt.bfloat16
AF = mybir.ActivationFunctionType
ALU = mybir.AluOpType
NPBF = ml_dtypes.bfloat16

D = 2048
TQ = 1024
TC = 2048
NH = 8
DH = 128
IN_W = 4632
FFN = 5632
BIG = 32768.0
NV = 129
NVC = 161

C_Q, C_KC, C_VC, C_KSL, C_VSL, C_KW, C_VW, C_G, C_U = 0, 1024, 1280, 1536, 1792, 2048, 2304, 2560, 2584

PV_NMIX, PV_NMQ, PV_NMKV, PV_NFFN = 0, 16, 32, 48
PV_QN, PV_KNC, PV_KNS, PV_KNW, PV_MQN, PV_MKN = 64, 65, 66, 67, 68, 69
PV_CB, PV_LNG, PV_LNB, PV_CW = 70, 78, 86, 94
PV_GB = 94 + 248
PV_POSK = PV_GB + 24
PV_POSV = PV_POSK + 32
NPV = PV_POSV + 32

CB_LBS, CB_LBW, CB_LBC, CB_CS, CB_WS, CB_CM, CB_OV, CB_ID, CB_ONE = 0, 2048, 4096, 4224, 5120, 6016, 7040, 7073, 7201
NCB = 7329


class Sem:
    n = 0

    def __init__(self, h):
        self.h = h
        Sem.n += 1
        self.id = Sem.n


class Tok:
    __slots__ = ("sem", "val", "eng")

    def __init__(self, sem, val, eng):
        self.sem, self.val, self.eng = sem, val, eng


def _add(d, tok):
    cur = d.get(tok.sem.id)
    if cur is None or cur.val < tok.val:
        d[tok.sem.id] = tok


class Buf:
    def __init__(self, name, fence):
        self.name = name
        self.w = {}
        self.wf = {}
        self.r = dict(fence)
        self.dsem = None
        self.dcnt = 0
        self.excl = False


class Eng:
    EPOCH = 12000

    def __init__(self, K, eng, name, is_pe=False):
        self.K, self.eng, self.name, self.is_pe = K, eng, name, is_pe
        self.seen = {}
        self.ep = 0
        self.nwait = 0
        self.nins = 0
        self._new()

    def _new(self):
        self.sem = self.K.new_sem(f"e{self.name}{self.ep}")
        self.ep += 1
        self.cnt = 0

    def wait(self, tok):
        if tok is None:
            return
        if self.is_pe and tok.eng is self:
            return
        if self.seen.get(tok.sem.id, 0) >= tok.val:
            return
        self.eng.wait_ge(tok.sem.h, tok.val)
        self.nwait += 1
        self.seen[tok.sem.id] = tok.val

    def issue(self, ins):
        if self.cnt >= self.EPOCH:
            self._new()
        ins.then_inc(self.sem.h, 1)
        self.cnt += 1
        self.nins += 1
        return Tok(self.sem, self.cnt, self)


class Scope:
    def __init__(self, K):
        self.K = K
        self.blocks = []

    def close(self):
        for off, n in self.blocks:
            self.K.arena_release(off, n)
        self.blocks = []


class Rot:
    def __init__(self, items):
        self.items = list(items)
        self.i = 0

    def next(self):
        it = self.items[self.i % len(self.items)]
        self.i += 1
        return it


class Kern:
    def __init__(self, taps=(), stop=None):
        self.stop = stop
        self.taps = list(taps)
        self.tap_tensors = {}
        self.nc = bass.Bass("TRN2", target_bir_lowering=False)
        self.nsem = 0
        self.fence = {}
        self.es = ExitStack()

    def new_sem(self, name):
        self.nsem += 1
        return Sem(self.nc.alloc_semaphore(name=f"s{self.nsem}_{name}"))

    def buf(self, name):
        return Buf(name, self.fence)

    def free(self, bufs):
        for b in bufs:
            for t in b.w.values():
                _add(self.fence, t)
            for t in b.r.values():
                _add(self.fence, t)

    ARENA_BYTES = 198 * 1024

    def arena_init(self):
        self.arena = self.es.enter_context(self.nc.sbuf_tensor("arena", [128, self.ARENA_BYTES // 2], BF16))
        self.afree = [(0, self.ARENA_BYTES)]
        self.apeak = 0

    def arena_release(self, off, n):
        self.afree.append((off, n))
        self.afree.sort()
        m = []
        for o, l in self.afree:
            if m and m[-1][0] + m[-1][1] == o:
                m[-1] = (m[-1][0], m[-1][1] + l)
            else:
                m.append((o, l))
        self.afree = m

    def scope(self):
        return Scope(self)

    def tile(self, S, name, shape, dt):
        esz = 4 if dt == F32 else 2
        nel = int(np.prod(shape[1:]))
        nb = (nel * esz + 63) // 64 * 64
        for i, (o, l) in enumerate(self.afree):
            if l >= nb:
                self.afree[i] = (o + nb, l - nb)
                if l == nb:
                    self.afree.pop(i)
                break
        else:
            raise RuntimeError(f"arena OOM allocating {name} {shape} ({nb}B); free={self.afree}")
        S.blocks.append((o, nb))
        self.apeak = max(self.apeak, o + nb)
        v = self.arena[:, o // 2:o // 2 + nel * esz // 2]
        if dt == F32:
            v = v.bitcast(F32)
        if len(shape) == 3:
            v = v.rearrange("p (a b) -> p a b", a=shape[1])
        elif len(shape) == 4:
            v = v.rearrange("p (a b c) -> p a b c", a=shape[1], b=shape[2])
        return v

    def _waits(self, E, r, w, pw):
        for b in r:
            for t in b.w.values():
                E.wait(t)
            if b.excl:
                for t in b.r.values():
                    if t.eng is not E:
                        E.wait(t)
        for b in w:
            for t in b.w.values():
                E.wait(t)
            for t in b.r.values():
                E.wait(t)
        for b in pw:
            for t in b.wf.values():
                E.wait(t)
            for t in b.r.values():
                E.wait(t)

    def _upd(self, tok, r, w, pw):
        for b in w:
            b.w = {tok.sem.id: tok}
            b.wf = {tok.sem.id: tok}
            b.r = {}
        for b in pw:
            _add(b.w, tok)
        for b in r:
            if b not in w and b not in pw:
                _add(b.r, tok)

    def op(self, E, fn, r=(), w=(), pw=()):
        self._waits(E, r, w, pw)
        ins = fn()
        tok = E.issue(ins)
        self._upd(tok, r, w, pw)
        return tok

    def dma(self, Q, out, in_, r=(), w=(), pw=(), owner=None, **kw):
        self._waits(Q, r, w, pw)
        if owner is None:
            owner = (list(w) + list(pw) + list(r))[0]
        if owner.dsem is None:
            owner.dsem = self.new_sem("d" + owner.name)
        ins = Q.eng.dma_start(out=out, in_=in_, **kw)
        owner.dcnt += 16
        ins.then_inc(owner.dsem.h, 16)
        tok = Tok(owner.dsem, owner.dcnt, None)
        self._upd(tok, r, w, pw)
        return tok

    def tap(self, name, ap, bufs, shape, dt=F32):
        if name not in self.taps:
            return
        t = self.nc.dram_tensor("tap_" + name, list(shape), dt, kind="ExternalOutput").ap()
        self.tap_tensors[name] = t
        tb = self.buf("tap" + name)
        self.dma(self.SP, t, ap, r=list(bufs), owner=tb)
        self.outtoks.append((tb.dsem, tb.dcnt))

    def mark(self, name):
        self.marks.append((name, self.PE.nins, self.ACT.nins, self.DVE.nins))

    def fin(self):
        for sem, val in self.outtoks:
            self.SP.eng.wait_ge(sem.h, val)
        return self.nc

    def build(self):
        nc = self.nc
        es = self.es
        dr = lambda n, s, dt=F32: nc.dram_tensor(n, list(s), dt, kind="ExternalInput").ap()
        self.xc = dr("xc", [TC, D])
        self.memb = dr("memb", [256, D])
        self.w_in = dr("w_in_b", [18 * 128, 4096])
        self.w_g = dr("w_g", [D, 24])
        self.w_out = dr("w_out_b", [8 * 128, 4096])
        self.w_mq = dr("w_mq_b", [2 * 128, 4096])
        self.w_mk = dr("w_mk_b", [2 * 128, 4096])
        self.w_mv = dr("w_mv_b", [2 * 128, 4096])
        self.w_mo = dr("w_mo", [512, D])
        self.w_gate = dr("w_gate_b", [22 * 128, 4096])
        self.w_up = dr("w_up_b", [22 * 128, 4096])
        self.w_down = dr("w_down", [FFN, D])
        self.w1k = dr("w1k_b", [128, 4096])
        self.w1v = dr("w1v_b", [128, 4096])
        self.w2k = dr("w2k", [128, 128])
        self.w2v = dr("w2v", [128, 128])
        self.pvec_d = dr("pvec", [128, NPV])
        self.cbf_d = dr("cbf", [128, NCB], BF16)
        self.rb_d = dr("rb", [128, 16 * 512], BF16)
        self.bonus_d = dr("bonus", [128, 256])
        self.y = nc.dram_tensor("y", [TQ, D], F32, kind="ExternalOutput").ap()
        self.outtoks = []
        self.marks = []

        self.PE = Eng(self, nc.tensor, "pe", is_pe=True)
        self.ACT = Eng(self, nc.scalar, "act")
        self.DVE = Eng(self, nc.vector, "dve")
        self.POOL = Eng(self, nc.gpsimd, "pool")
        self.SP = Eng(self, nc.sync, "sp")
        PE, ACT, DVE, POOL, SP = self.PE, self.ACT, self.DVE, self.POOL, self.SP
        op, dma = self.op, self.dma

        self.arena_init()
        self.bank = []
        self.bankb = []
        for i in range(8):
            self.bank.append(es.enter_context(nc.psum_tensor(f"bank{i}", [128, 512], F32)))
            self.bankb.append(self.buf(f"bank{i}"))
            self.bankb[-1].excl = True
        bank, bankb = self.bank, self.bankb

        G = self.scope()
        pvec = self.tile(G, "pvec", [128, NPV], F32)
        pvb = self.buf("pvec")
        cid = self.tile(G, "cid", [128, 256], BF16)
        cidb = self.buf("cid")
        eps = self.tile(G, "eps", [128, 4], F32)
        epsb = self.buf("eps")
        dma(SP, pvec[:, :], self.pvec_d, w=[pvb])
        dma(SP, cid[:, :], self.cbf_d[:, CB_ID:CB_ID + 256], w=[cidb])
        op(DVE, lambda: nc.vector.memset(eps[:, 0:1], 1e-6), w=[epsb])
        op(DVE, lambda: nc.vector.memset(eps[:, 1:2], 1e-5), pw=[epsb])
        op(DVE, lambda: nc.vector.tensor_scalar(out=eps[:, 2:3], in0=pvec[:, PV_QN:PV_QN + 1], scalar1=DH ** -0.5, scalar2=None, op0=ALU.mult), r=[pvb], pw=[epsb])
        op(DVE, lambda: nc.vector.tensor_scalar(out=eps[:, 3:4], in0=pvec[:, PV_MQN:PV_MQN + 1], scalar1=DH ** -0.5, scalar2=None, op0=ALU.mult), r=[pvb], pw=[epsb])
        ident = cid[:, 0:128]
        ones = cid[:, 128:256]

        NSLOT = 6
        wslot = [self.tile(G, f"wslot{i}", [128, 4096], BF16) for i in range(NSLOT)]
        wslotb = [self.buf(f"wslot{i}") for i in range(NSLOT)]
        self.wq = []
        self.wnext_issue = 0

        def wcol(w, i):
            return w[i * 128:(i + 1) * 128, :]

        def wrow(w, r0, n=256):
            return w[r0:r0 + n, :].rearrange("(k p) n -> p k n", p=128)

        def w1v_(w):
            return w

        self.wreleased = set()

        def wissue():
            while self.wnext_issue < len(self.wq):
                i = self.wnext_issue
                if i >= NSLOT and (i - NSLOT) not in self.wreleased:
                    break
                src, shp = self.wq[i]
                s_ = i % NSLOT
                if len(src.shape) == 2:
                    dst = wslot[s_][:, :]
                else:
                    dst = wslot[s_][:, :].rearrange("p (k n) -> p k n", k=shp[0])
                dma(POOL, dst, src, w=[wslotb[s_]])
                self.wnext_issue += 1

        self.wcur = 0
        self.wheld = []

        def wget():
            i = self.wcur
            self.wcur += 1
            wissue()
            assert i < self.wnext_issue, "weight block not issuable: too many blocks held"
            self.wheld.append(i)
            s_ = i % NSLOT
            shp = self.wq[i][1]
            return wslot[s_][:, :].rearrange("p (k n) -> p k n", k=shp[0]), wslotb[s_]

        def wrel(n=None, newest=False):
            n = len(self.wheld) if n is None else n
            for _ in range(n):
                self.wreleased.add(self.wheld.pop(-1 if newest else 0))
            wissue()

        kvblocks = [(C_KC, "kc"), (C_VC, "vc"), (C_KSL, "ksl"), (C_KW, "kw"), (C_VSL, "vsl"), (C_VW, "vw")]
        for i in range(18):
            self.wq.append((wcol(self.w_in, i), (16, 256)))
        self.wq.append((w1v_(self.w1k), (32, 128)))
        self.wq.append((w1v_(self.w1v), (32, 128)))
        for w in (self.w_mk, self.w_mv):
            for i in range(2):
                self.wq.append((wcol(w, i), (16, 256)))
        for i in range(8):
            self.wq.append((wcol(self.w_out, i), (16, 256)))
        for i in range(2):
            self.wq.append((wcol(self.w_mq, i), (16, 256)))
        for i in range(2):
            self.wq.append((wrow(self.w_mo, 256 * i), (2, 2048)))
        for j in range(22):
            self.wq.append((wcol(self.w_gate, j), (16, 256)))
            self.wq.append((wcol(self.w_up, j), (16, 256)))
            self.wq.append((wrow(self.w_down, 256 * j), (2, 2048)))

        wg = self.tile(G, "wg", [128, 16, 24], BF16)
        wgb = self.buf("wg")
        dma(POOL, wg[:, :, :], self.w_g.rearrange("(k p) n -> p k n", p=128), w=[wgb])
        w2 = self.tile(G, "w2", [128, 2, 128], BF16)
        w2b = self.buf("w2")
        dma(POOL, w2[:, 0, :], self.w2k, w=[w2b])
        dma(POOL, w2[:, 1, :], self.w2v, pw=[w2b])
        wissue()
        if self.stop == "init":
            self.tap("w0", wslot[0][:, :], [wslotb[0]], [128, 4096], BF16)
            self.tap("w4", wslot[4][:, :], [wslotb[4]], [128, 4096], BF16)
            self.tap("wg", wg[:, :, :], [wgb], [128, 16, 24], BF16)
            self.tap("pvec", pvec[:, :], [pvb], [128, NPV], F32)
            self.tap("cid", cid[:, :], [cidb], [128, 256], BF16)
            self.tap("eps", eps[:, :], [epsb], [128, 4], F32)
            return self.fin()

        rot_proj = Rot([0, 1, 2, 3])
        rot_ssq = Rot([4, 5])
        rot_tr = Rot([6, 7])
        self.flip = 0

        def evac_engine():
            self.flip ^= 1
            return ACT if self.flip else DVE

        def copy_op(E, out, in_, r, w=(), pw=()):
            if E is ACT:
                return op(ACT, lambda: nc.scalar.copy(out=out, in_=in_), r=r, w=w, pw=pw)
            return op(E, lambda: E.eng.tensor_copy(out=out, in_=in_), r=r, w=w, pw=pw)

        def norm_tile_a(xap, xbufs, xs_, xsb_, ss_, ssb_):
            op(ACT, lambda: nc.scalar.activation(out=xs_[:, :], in_=xap, func=AF.Square, accum_out=ss_[:, 0:1]), r=xbufs, w=[xsb_, ssb_])
            op(ACT, lambda: nc.scalar.activation(out=ss_[:, 1:2], in_=ss_[:, 0:1], func=AF.Sqrt, scale=1.0 / D, bias=eps[:, 0:1]), r=[epsb], w=[ssb_])
            op(DVE, lambda: nc.vector.reciprocal(out=ss_[:, 2:3], in_=ss_[:, 1:2]), w=[ssb_])
            op(DVE, lambda: nc.vector.tensor_scalar(out=xs_[:, :], in0=xap, scalar1=ss_[:, 2:3], scalar2=None, op0=ALU.mult), r=list(xbufs) + [ssb_], w=[xsb_])

        def norm_tile_b(gcol, dstT, t0, dstbuf, xs_, xsb_, tail=None):
            for half in range(2):
                bi = rot_tr.next()
                bv = bank[bi][:, :].bitcast(BF16)
                for c8 in range(8):
                    c = half * 8 + c8
                    op(PE, lambda: nc.tensor.transpose(out=bv[:, c8 * 128:(c8 + 1) * 128], in_=xs_[:, c * 128:(c + 1) * 128], identity=ident), r=[xsb_, cidb], w=[bankb[bi]])
                src = bv[:, 0:1024].rearrange("p (c t) -> p c t", c=8)
                dst = dstT[:, half * 8:(half + 1) * 8, t0:t0 + 128]
                g = pvec[:, gcol + half * 8:gcol + half * 8 + 8].unsqueeze(2).broadcast_to([128, 8, 128])
                op(DVE, lambda: nc.vector.tensor_tensor(out=dst, in0=src, in1=g, op=ALU.mult), r=[bankb[bi], pvb], pw=[dstbuf])
                if tail is not None:
                    tl, tlb = tail
                    op(POOL, lambda: nc.gpsimd.tensor_copy(out=tl[:, half * 8:(half + 1) * 8, :], in_=dstT[:, half * 8:(half + 1) * 8, t0 + 96:t0 + 128]), r=[dstbuf], pw=[tlb])

        def norm_tile(xap, xbufs, gcol, dstT, t0, dstbuf, xs_, xsb_, ss_, ssb_, tail=None):
            norm_tile_a(xap, xbufs, xs_, xsb_, ss_, ssb_)
            norm_tile_b(gcol, dstT, t0, dstbuf, xs_, xsb_, tail=tail)

        def rmsnorm_T(S, get_x, ntiles, gcol, dstT, dstbuf_of, tcol_of, tail=None):
            xs = [self.tile(S, f"xs{j}", [128, D], BF16) for j in range(2)]
            xsb = [self.buf(f"xs{j}") for j in range(2)]
            ss = [self.tile(S, f"ss{j}", [128, 4], F32) for j in range(2)]
            ssb = [self.buf(f"ss{j}") for j in range(2)]
            for i in range(ntiles):
                xap, xbufs = get_x(i)
                j = i % 2
                norm_tile(xap, xbufs, gcol, dstT, tcol_of(i), dstbuf_of(i), xs[j], xsb[j], ss[j], ssb[j], tail=tail if i == ntiles - 1 else None)
            self.free(xsb + ssb)

        P1 = self.scope()
        sqt = [self.tile(G, f"sqt{j}", [128, 512], BF16) for j in range(2)]
        sqtb = [self.buf(f"sqt{j}") for j in range(2)]
        rtt = [self.tile(G, f"rtt{j}", [128, 512], F32) for j in range(2)]
        rttb = [self.buf(f"rtt{j}") for j in range(2)]
        rot_sq = Rot([0, 1])

        def pnorm(psi, n, gain_ap, gain_bufs, dst, dstb):
            j = rot_sq.next()
            ps = bank[psi]
            op(ACT, lambda: nc.scalar.activation(out=sqt[j][:, 0:n], in_=ps[:, 0:n], func=AF.Square), r=[bankb[psi]], w=[sqtb[j]])

            def tail():
                si = rot_ssq.next()
                op(PE, lambda: nc.tensor.matmul(bank[si][:, 0:n], lhsT=ones, rhs=sqt[j][:, 0:n], start=True, stop=True), r=[sqtb[j], cidb], w=[bankb[si]])
                op(ACT, lambda: nc.scalar.activation(out=rtt[j][:, 0:n], in_=bank[si][:, 0:n], func=AF.Sqrt, scale=1.0 / DH, bias=eps[:, 0:1]), r=[bankb[si], epsb], w=[rttb[j]])
                op(DVE, lambda: nc.vector.reciprocal(out=rtt[j][:, 0:n], in_=rtt[j][:, 0:n]), w=[rttb[j]])
                op(DVE, lambda: nc.vector.scalar_tensor_tensor(out=dst, in0=ps[:, 0:n], scalar=gain_ap, in1=rtt[j][:, 0:n], op0=ALU.mult, op1=ALU.mult), r=[bankb[psi], rttb[j]] + list(gain_bufs), pw=[dstb])
            self.deferred.append(tail)

        self.deferred = []

        def flush(keep=0):
            while len(self.deferred) > keep:
                self.deferred.pop(0)()

        def proj_fm(wt, wb, cc, actT, actbufs, t0, n):
            psi = rot_proj.next()
            for kc in range(16):
                op(PE, lambda: nc.tensor.matmul(bank[psi][:, 0:n], lhsT=wt[:, kc, cc * 128:(cc + 1) * 128], rhs=actT[:, kc, t0:t0 + n], start=(kc == 0), stop=(kc == 15)), r=[wb] + list(actbufs), w=[bankb[psi]])
            flush(0)
            return psi

        def proj_tm(wt, wb, actT, actbufs, t0, ncols, kcs=16):
            psi = rot_proj.next()
            for kc in range(kcs):
                op(PE, lambda: nc.tensor.matmul(bank[psi][:, 0:ncols], lhsT=actT[:, kc, t0:t0 + 128], rhs=wt[:, kc, 0:ncols], start=(kc == 0), stop=(kc == kcs - 1)), r=[wb] + list(actbufs), w=[bankb[psi]])
            return psi

        A = self.scope()
        cbf = self.tile(A, "cbf", [128, CB_ID], BF16)
        cbb = self.buf("cbf")
        dma(SP, cbf[:, :], self.cbf_d[:, 0:CB_ID], w=[cbb])
        KnT = self.tile(A, "KnT", [128, 4, TC], BF16)
        KnTb = [[self.buf(f"KnT{i}_{c}") for c in range(4)] for i in range(4)]
        Vaug = self.tile(A, "Vaug", [128, 16, 4, NV], BF16)
        Vaugb = [self.buf(f"Vaug{i}") for i in range(16)]
        CK = self.scope()
        kcvcT = self.tile(CK, "kcvcT", [128, 4, TC], BF16)
        kcvcb = [self.buf(f"kcvc{i}") for i in range(4)]
        vab = self.buf("vaug_init")
        op(POOL, lambda: nc.gpsimd.memset(Vaug[:, :, :, :], 1.0), w=Vaugb)

        if self.stop == "p0a":
            self.tap("Vaug", Vaug[:, :, :, :], Vaugb, [128, 16, 4, NV], BF16)
            self.tap("cbf", cbf[:, :], [cbb], [128, CB_ID], BF16)
            return self.fin()
        hTs = [self.tile(P1, f"hT{c}", [128, 16, 512], BF16) for c in range(2)]
        hTb = [self.buf(f"hT{c}") for c in range(2)]
        htail = self.tile(P1, "htail", [128, 16, 32], BF16)
        htailb = self.buf("htail")

        S0 = self.scope()
        xt = [self.tile(S0, f"xt{j}", [128, D], F32) for j in range(2)]
        xtb = [self.buf(f"xt{j}") for j in range(2)]
        xs0 = [self.tile(S0, f"xs{j}", [128, D], BF16) for j in range(2)]
        xs0b = [self.buf(f"xs{j}") for j in range(2)]
        ss0 = [self.tile(S0, f"ss{j}", [128, 4], F32) for j in range(2)]
        ss0b = [self.buf(f"ss{j}") for j in range(2)]

        def xload(gt):
            dma(SP, xt[gt % 2][:, :], self.xc[gt * 128:(gt + 1) * 128, :], w=[xtb[gt % 2]])

        def norm_a(gt):
            if gt + 1 < 16:
                xload(gt + 1)
            j = gt % 2
            norm_tile_a(xt[j][:, :], [xtb[j]], xs0[j], xs0b[j], ss0[j], ss0b[j])

        def norm_b(gt):
            q_, i_ = gt // 4, gt % 4
            j = gt % 2
            norm_tile_b(PV_NMIX, hTs[q_ % 2], i_ * 128, hTb[q_ % 2], xs0[j], xs0b[j], tail=(htail, htailb) if gt == 7 else None)

        kvw = []

        def kv_block(q_, c0, kind, k_):
            hT_, hb_ = hTs[q_ % 2], hTb[q_ % 2]
            if q_ == 0:
                kvw.append(wget())
            wt, wb = kvw[k_]
            tg = q_ * 512
            if kind in ("kc", "vc", "ksl", "kw"):
                for cc in range(2):
                    psi = proj_fm(wt, wb, cc, hT_, [hb_], 0, 512)
                    if kind in ("kc", "vc"):
                        idx = (0 if kind == "kc" else 2) + cc
                        copy_op(evac_engine(), kcvcT[:, idx, tg:tg + 512], bank[psi][:, :], r=[bankb[psi]], pw=[kcvcb[idx]])
                    else:
                        idx = (0 if kind == "ksl" else 2) + cc
                        gcol = PV_KNS if kind == "ksl" else PV_KNW
                        pnorm(psi, 512, pvec[:, gcol:gcol + 1], [pvb], KnT[:, idx, tg:tg + 512], KnTb[idx][q_])
                flush(0)
            else:
                vk = 0 if kind == "vsl" else 2
                for tt in range(4):
                    psi = proj_tm(wt, wb, hT_, [hb_], tt * 128, 256)
                    kt = q_ * 4 + tt
                    E = evac_engine()
                    src = bank[psi][:, 0:256].rearrange("p (g d) -> p g d", g=2)
                    copy_op(E, Vaug[:, kt, vk:vk + 2, 0:128], src, r=[bankb[psi]], pw=[Vaugb[kt]])
            if q_ == 3:
                wrel(1)

        self.mark("p1_start")
        xload(0)
        norm_a(0)
        for gt in range(4):
            norm_b(gt)
            norm_a(gt + 1)
        for q_ in range(4):
            for k_, (c0, kind) in enumerate(kvblocks):
                kv_block(q_, c0, kind, k_)
                if q_ < 3 and k_ < 4:
                    gt = (q_ + 1) * 4 + k_
                    norm_b(gt)
                    if gt + 1 < 16 and k_ < 3:
                        norm_a(gt + 1)
            if q_ < 2:
                norm_a((q_ + 2) * 4)
            flush(0)
            if self.stop == "p0" and q_ == 0:
                return self.fin()
        S0.close()
        self.free(xtb + xs0b + ss0b)
        qnT = self.tile(A, "qnT", [128, NH, TQ], BF16)
        qnTb = [[self.buf(f"qnT{h}_{c}") for c in range(2)] for h in range(NH)]
        gates = self.tile(A, "gates", [128, 8, 24], F32)
        gatesb = self.buf("gates")
        HG = self.scope()
        hglu = self.tile(HG, "hglu", [128, 8, 1056], BF16)
        hglub = [self.buf(f"hglu{c}") for c in range(8)]

        self.mark("q_proj")
        for blk in range(4):
            wt, wb = wget()
            for cc in range(2):
                h = blk * 2 + cc
                for tc in range(2):
                    psi = proj_fm(wt, wb, cc, hTs[tc], [hTb[tc]], 0, 512)
                    pnorm(psi, 512, eps[:, 2:3], [epsb], qnT[:, h, tc * 512:(tc + 1) * 512], qnTb[h][tc])
            flush(0)
            wrel()
        gtmp = self.tile(P1, "gtmp", [128, 24], F32)
        gtmpb = self.buf("gtmp")
        for tt in range(8):
            psi = rot_proj.next()
            for kc in range(16):
                op(PE, lambda: nc.tensor.matmul(bank[psi][:, 0:24], lhsT=hTs[tt // 4][:, kc, (tt % 4) * 128:(tt % 4 + 1) * 128], rhs=wg[:, kc, :], start=(kc == 0), stop=(kc == 15)), r=[wgb, hTb[tt // 4]], w=[bankb[psi]])
            op(DVE, lambda: nc.vector.tensor_tensor(out=gtmp[:, :], in0=bank[psi][:, 0:24], in1=pvec[:, PV_GB:PV_GB + 24], op=ALU.add), r=[bankb[psi], pvb], w=[gtmpb])
            op(ACT, lambda: nc.scalar.activation(out=gates[:, tt, :], in_=gtmp[:, :], func=AF.Sigmoid), r=[gtmpb], pw=[gatesb])
        self.mark("u_proj")
        sg = [self.tile(P1, f"sg{j}", [128, 512], F32) for j in range(2)]
        sgb = [self.buf(f"sg{j}") for j in range(2)]
        rot_sg = Rot([0, 1])
        for j4 in range(4):
            wa, wab = wget()
            wbt, wbb = wget()
            for cc in range(2):
                ch = j4 * 2 + cc
                for seg in range(3):
                    if seg == 0:
                        actT, actb, t0, n, off = htail, [htailb], 0, 32, 0
                    else:
                        actT, actb, t0, n, off = hTs[seg - 1], [hTb[seg - 1]], 0, 512, 32 + (seg - 1) * 512
                    pa = proj_fm(wa, wab, cc, actT, actb, t0, n)
                    pb = proj_fm(wbt, wbb, cc, actT, actb, t0, n)
                    j = rot_sg.next()
                    op(ACT, lambda: nc.scalar.activation(out=sg[j][:, 0:n], in_=bank[pb][:, 0:n], func=AF.Sigmoid), r=[bankb[pb]], w=[sgb[j]])
                    op(DVE, lambda: nc.vector.tensor_tensor(out=hglu[:, ch, off:off + n], in0=bank[pa][:, 0:n], in1=sg[j][:, 0:n], op=ALU.mult), r=[bankb[pa], sgb[j]], pw=[hglub[ch]])
            wrel()
        self.tap("KnT", KnT[:, :, :], [b for l in KnTb for b in l], [128, 4, TC], BF16)
        self.tap("qnT", qnT[:, :, :], [b for l in qnTb for b in l], [128, NH, TQ], BF16)
        self.tap("kcvcT", kcvcT[:, :, :], kcvcb, [128, 4, TC], BF16)
        self.tap("Vaug", Vaug[:, :, :, :], Vaugb, [128, 16, 4, NV], BF16)
        self.tap("gates", gates[:, :, :], [gatesb], [128, 8, 24], F32)
        self.tap("hglu", hglu[:, :, :], hglub, [128, 8, 1056], BF16)
        P1.close()
        self.free(hTb + [htailb, gtmpb] + sgb)
        if self.stop == "p1":
            return self.fin()

        self.mark("compress")
        kcmpT = self.tile(A, "kcmpT", [128, 2, 128], BF16)
        kcmpb = self.buf("kcmpT")
        vcaug = self.tile(A, "vcaug", [128, 2, NVC], BF16)
        vcaugb = self.buf("vcaug")
        C1 = self.scope()
        posbf = self.tile(C1, "posbf", [128, 64], BF16)
        posbfb = self.buf("posbf")
        posb = self.tile(C1, "posb", [128, 2], F32)
        posbb = self.buf("posb")
        h1 = [self.tile(C1, f"h1_{j}", [128, 128], BF16) for j in range(2)]
        h1b = [self.buf(f"h1_{j}") for j in range(2)]
        op(DVE, lambda: nc.vector.memset(kcmpT[:, :, :], 0.0), w=[kcmpb])
        op(DVE, lambda: nc.vector.memset(vcaug[:, :, :], 0.0), w=[vcaugb])
        for g in range(2):
            op(DVE, lambda: nc.vector.tensor_copy(out=vcaug[:, g, 128:NVC], in_=cbf[:, CB_OV:CB_OV + 33]), r=[cbb], pw=[vcaugb])
        op(DVE, lambda: nc.vector.tensor_copy(out=posbf[:, :], in_=pvec[:, PV_POSK:PV_POSK + 64]), r=[pvb], w=[posbfb])
        rot_h1 = Rot([0, 1])
        for kind in range(2):
            w1, w1b = wget()
            pbi = rot_proj.next()
            for l in range(32):
                op(PE, lambda: nc.tensor.matmul(bank[pbi][:, 0:1], lhsT=w1[:, l, :], rhs=posbf[:, kind * 32 + l:kind * 32 + l + 1], start=(l == 0), stop=(l == 31)), r=[w1b, posbfb], w=[bankb[pbi]])
            op(DVE, lambda: nc.vector.tensor_copy(out=posb[:, kind:kind + 1], in_=bank[pbi][:, 0:1]), r=[bankb[pbi]], pw=[posbb])
            for g in range(2):
                psi = rot_proj.next()
                src = kcvcT[:, kind * 2 + g, :]
                for l in range(32):
                    op(PE, lambda: nc.tensor.matmul(bank[psi][:, 0:127], lhsT=w1[:, l, :], rhs=src[:, l:l + 16 * 126 + 1:16], start=(l == 0), stop=(l == 31)), r=[w1b, kcvcb[kind * 2 + g]], w=[bankb[psi]])
                j = rot_h1.next()
                op(ACT, lambda: nc.scalar.activation(out=h1[j][:, 0:127], in_=bank[psi][:, 0:127], func=AF.Silu, bias=posb[:, kind:kind + 1]), r=[bankb[psi], posbb], w=[h1b[j]])
                ps2 = rot_proj.next()
                if kind == 0:
                    op(PE, lambda: nc.tensor.matmul(bank[ps2][:, 0:127], lhsT=w2[:, 0, :], rhs=h1[j][:, 0:127], start=True, stop=True), r=[w2b, h1b[j]], w=[bankb[ps2]])
                    pnorm(ps2, 127, pvec[:, PV_KNC:PV_KNC + 1], [pvb], kcmpT[:, g, 0:127], kcmpb)
                    flush(0)
                else:
                    op(PE, lambda: nc.tensor.matmul(bank[ps2][0:127, 0:128], lhsT=h1[j][:, 0:127], rhs=w2[:, 1, :], start=True, stop=True), r=[w2b, h1b[j]], w=[bankb[ps2]])
                    copy_op(evac_engine(), vcaug[0:127, g, 0:128], bank[ps2][0:127, 0:128], r=[bankb[ps2]], pw=[vcaugb])
            wrel()
        self.tap("kcmpT", kcmpT[:, :, :], [kcmpb], [128, 2, 128], BF16)
        self.tap("vcaug", vcaug[:, :, :], [vcaugb], [128, 2, NVC], BF16)
        C1.close()
        self.free([posbfb, posbb] + h1b)
        if self.stop == "p1c":
            return self.fin()
        CK.close()
        self.free(kcvcb)

        self.mark("attn")
        CT = self.scope()
        catA = self.tile(CT, "catA", [128, 8, TQ], BF16)
        catTb = [[self.buf(f"catT{c}_{t}") for t in range(2)] for c in range(16)]
        AT = self.scope()
        RB = self.tile(AT, "RB", [128, 16, 512], BF16)
        RBb = [self.buf(f"RB{i}") for i in range(16)]
        dma(SP, RB[:, :, :], self.rb_d.rearrange("p (i n) -> p i n", i=16), w=RBb, owner=RBb[0])
        bonus = self.tile(AT, "bonus", [128, 8, 32], F32)
        bonusb = self.buf("bonus")
        dma(SP, bonus[:, :, :], self.bonus_d.rearrange("p (i n) -> p i n", i=8), w=[bonusb])
        Pt = [self.tile(AT, f"Pt{j}", [128, 512], BF16) for j in range(3)]
        Ptb = [self.buf(f"Pt{j}") for j in range(3)]
        rot_P = Rot([0, 1, 2])
        rot_S = Rot([0, 1, 6])
        rot_O = Rot([(2, 3), (4, 5)])
        ocomb = [self.tile(AT, f"ocomb{j}", [128, 4, 4, 128], F32) for j in range(1)]
        ocombb = [self.buf(f"ocomb{j}") for j in range(1)]
        ocb = [self.tile(AT, f"ocb{j}", [128, 4, 4, 128], BF16) for j in range(1)]
        ocbb = [self.buf(f"ocb{j}") for j in range(1)]
        imp = self.tile(AT, "imp", [128, 4, 32], F32)
        impb = self.buf("imp")
        sc2 = self.tile(AT, "sc2", [128, 4, 32], F32)
        sc2b = self.buf("sc2")
        m8 = self.tile(AT, "m8", [128, 16], F32)
        m8b = self.buf("m8")
        selbf = self.tile(AT, "selbf", [128, 4, 32], BF16)
        selbfb = self.buf("selbf")
        rdt = [self.tile(AT, f"rd{j}", [128, 8], F32) for j in range(4)]
        rdb = [self.buf(f"rd{j}") for j in range(4)]
        rot_rd = Rot([0, 1, 2, 3])

        def attn_tile(Kt, Kb, Qt, Qb, LBap, RBi, mask_ap):
            si = rot_S.next()
            n_extra = 1 if mask_ap is not None else 0
            op(PE, lambda: nc.tensor.matmul(bank[si][:, :], lhsT=Kt, rhs=Qt, start=True, stop=False), r=list(Kb) + list(Qb), w=[bankb[si]])
            op(PE, lambda: nc.tensor.matmul(bank[si][:, :], lhsT=LBap, rhs=RB[:, RBi, :], start=False, stop=(n_extra == 0)), r=[cbb, RBb[RBi]], w=[bankb[si]])
            if mask_ap is not None:
                op(PE, lambda: nc.tensor.matmul(bank[si][:, :], lhsT=ident, rhs=mask_ap, start=False, stop=True), r=[cbb, cidb], w=[bankb[si]])
            pj = rot_P.next()
            op(ACT, lambda: nc.scalar.activation(out=Pt[pj][:, :], in_=bank[si][:, :], func=AF.Exp), r=[bankb[si]], w=[Ptb[pj]])
            return pj

        def pv(pj, Obanks, Vap, Vb, nv, first, last):
            for j in range(4):
                ob = Obanks[j // 2]
                o0 = (j % 2) * nv
                op(PE, lambda: nc.tensor.matmul(bank[ob][:, o0:o0 + nv], lhsT=Pt[pj][:, j * 128:(j + 1) * 128], rhs=Vap, start=(first and j % 2 == 0), stop=last, skip_group_check=True), r=[Ptb[pj]] + list(Vb), w=[bankb[ob]])

        def finish(Obanks, nv, h, tc, br, oc, mode):
            hl = h % 4
            rj = rot_rd.next()
            rd = rdt[rj]
            for b2 in range(2):
                ob = Obanks[b2]
                op(DVE, lambda: nc.vector.tensor_scalar(out=rd[:, 2 * b2:2 * b2 + 2], in0=bank[ob][:, 128:128 + nv + 1:nv], scalar1=1e-30, scalar2=None, op0=ALU.max), r=[bankb[ob]], pw=[rdb[rj]])
            op(DVE, lambda: nc.vector.reciprocal(out=rd[:, 0:4], in_=rd[:, 0:4]), w=[rdb[rj]])
            gcol = h * 3 + br
            op(DVE, lambda: nc.vector.tensor_tensor(out=rd[:, 4:8], in0=rd[:, 0:4], in1=gates[:, tc * 4:tc * 4 + 4, gcol], op=ALU.mult), r=[gatesb], w=[rdb[rj]])
            for j in range(4):
                ob = Obanks[j // 2]
                o0 = (j % 2) * nv
                src = bank[ob][:, o0:o0 + 128]
                f = rd[:, 4 + j:5 + j]
                if mode == "first":
                    op(DVE, lambda: nc.vector.tensor_scalar(out=ocomb[oc][:, j, hl, :], in0=src, scalar1=f, scalar2=None, op0=ALU.mult), r=[bankb[ob], rdb[rj]], pw=[ocombb[oc]])
                elif mode == "add":
                    op(DVE, lambda: nc.vector.scalar_tensor_tensor(out=ocomb[oc][:, j, hl, :], in0=src, scalar=f, in1=ocomb[oc][:, j, hl, :], op0=ALU.mult, op1=ALU.add), r=[bankb[ob], rdb[rj]], w=[ocombb[oc]])
                else:
                    op(DVE, lambda: nc.vector.scalar_tensor_tensor(out=ocb[oc][:, j, hl, :], in0=src, scalar=f, in1=ocomb[oc][:, j, hl, :], op0=ALU.mult, op1=ALU.add), r=[bankb[ob], rdb[rj], ocombb[oc]], pw=[ocbb[oc]])
            return rj

        LBS = lambda i: cbf[:, CB_LBS + i * 128:CB_LBS + (i + 1) * 128]
        LBW = lambda i: cbf[:, CB_LBW + i * 128:CB_LBW + (i + 1) * 128]
        LBC = cbf[:, CB_LBC:CB_LBC + 128]
        CSm = lambda m: cbf[:, CB_CS + 384 - 128 * m:CB_CS + 384 - 128 * m + 512]
        WSm = lambda m: cbf[:, CB_WS + 384 - 128 * m:CB_WS + 384 - 128 * m + 512]

        def run_stream(jobs, L=2):
            pend = []
            for n in range(len(jobs) + L):
                if n < len(jobs):
                    pend.append((jobs[n], jobs[n][0]()))
                if n >= L:
                    job, pj = pend.pop(0)
                    job[1](pj)

        for g in range(2):
            for tc in range(2):
                oc = 0
                Q = lambda h: qnT[:, h, tc * 512:(tc + 1) * 512]
                jobs = []
                for hl in range(4):
                    h = 4 * g + hl
                    Ob = rot_O.next()

                    def sc(h=h):
                        return attn_tile(kcmpT[:, g, :], [kcmpb], Q(h), [qnTb[h][tc]], LBC, h * 2 + tc, cbf[:, CB_CM + tc * 512:CB_CM + (tc + 1) * 512])

                    def pvf(pj, h=h, hl=hl, Ob=Ob):
                        pv(pj, Ob, vcaug[:, g, :], [vcaugb], NVC, True, True)
                        rj = finish(Ob, NVC, h, tc, 0, oc, "first")
                        rd = rdt[rj]
                        for j in range(4):
                            ob = Ob[j // 2]
                            o0 = (j % 2) * NVC + 129
                            if hl == 0:
                                op(DVE, lambda: nc.vector.tensor_scalar(out=imp[:, j, :], in0=bank[ob][:, o0:o0 + 32], scalar1=rd[:, j:j + 1], scalar2=None, op0=ALU.mult), r=[bankb[ob], rdb[rj]], w=[impb] if j == 0 else [], pw=[impb] if j else [])
                            else:
                                op(DVE, lambda: nc.vector.scalar_tensor_tensor(out=imp[:, j, :], in0=bank[ob][:, o0:o0 + 32], scalar=rd[:, j:j + 1], in1=imp[:, j, :], op0=ALU.mult, op1=ALU.add), r=[bankb[ob], rdb[rj]], w=[impb])
                    jobs.append((sc, pvf))
                for hl in range(4):
                    h = 4 * g + hl
                    Ob = rot_O.next()
                    ks = list(range(4 + 4 * tc, 12 + 4 * tc))
                    for n_, i in enumerate(ks):
                        m = i - (4 + 4 * tc)
                        mask = WSm(m) if m < 4 else CSm(m - 4)

                        def sc(h=h, i=i, mask=mask):
                            return attn_tile(KnT[:, 2 + g, i * 128:(i + 1) * 128], [KnTb[2 + g][i // 4]], Q(h), [qnTb[h][tc]], LBW(i), h * 2 + tc, mask)

                        def pvf(pj, h=h, i=i, n_=n_, Ob=Ob, last=(n_ == len(ks) - 1)):
                            pv(pj, Ob, Vaug[:, i, 2 + g, :], [Vaugb[i]], NV, n_ == 0, last)
                            if last:
                                finish(Ob, NV, h, tc, 2, oc, "add")
                        jobs.append((sc, pvf))
                run_stream(jobs[:4])
                op(DVE, lambda: nc.vector.tensor_tensor(out=imp[:, :, :], in0=imp[:, :, :], in1=bonus[:, tc * 4:tc * 4 + 4, :], op=ALU.add), r=[bonusb], w=[impb])
                tb = rot_tr.next()
                tbv = bank[tb][:, :].bitcast(BF16)
                for j in range(4):
                    op(DVE, lambda: nc.vector.max(out=m8[:, 0:8], in_=imp[:, j, :]), r=[impb], w=[m8b])
                    op(DVE, lambda: nc.vector.match_replace(out=sc2[:, j, :], in_to_replace=m8[:, 0:8], in_values=imp[:, j, :], imm_value=-3.0e38), r=[impb, m8b], w=[sc2b])
                    op(DVE, lambda: nc.vector.max(out=m8[:, 8:16], in_=sc2[:, j, :]), r=[sc2b], w=[m8b])
                    op(DVE, lambda: nc.vector.tensor_scalar(out=selbf[:, j, :], in0=imp[:, j, :], scalar1=m8[:, 15:16], scalar2=None, op0=ALU.is_ge), r=[impb, m8b], w=[selbfb])
                if g == 0 and tc == 1:
                    self.tap("imp", imp[:, :, :], [impb], [128, 4, 32], F32)
                    self.tap("selbf", selbf[:, :, :], [selbfb], [128, 4, 32], BF16)
                run_stream(jobs[4:])
                for j in range(4):
                    op(PE, lambda: nc.tensor.transpose(out=tbv[0:32, j * 128:(j + 1) * 128], in_=selbf[:, j, :], identity=ident), r=[selbfb, cidb], w=[bankb[tb]])
                for hl in range(4):
                    h = 4 * g + hl
                    copy_op(ACT, RB[0:32, h * 2 + tc, :], tbv[0:32, 0:512], r=[bankb[tb]], w=[RBb[h * 2 + tc]])
                jobs = []
                for hl in range(4):
                    h = 4 * g + hl
                    Ob = rot_O.next()
                    ks = list(range(0, 12 + 4 * tc))
                    for n_, i in enumerate(ks):
                        m = i - (8 + 4 * tc)
                        mask = CSm(m) if m >= 0 else None

                        def sc(h=h, i=i, mask=mask):
                            return attn_tile(KnT[:, g, i * 128:(i + 1) * 128], [KnTb[g][i // 4]], Q(h), [qnTb[h][tc]], LBS(i), h * 2 + tc, mask)

                        def pvf(pj, h=h, i=i, n_=n_, Ob=Ob, last=(n_ == len(ks) - 1)):
                            pv(pj, Ob, Vaug[:, i, g, :], [Vaugb[i]], NV, n_ == 0, last)
                            if last:
                                finish(Ob, NV, h, tc, 1, oc, "last")
                        jobs.append((sc, pvf))
                run_stream(jobs)
                for hl in range(4):
                    h = 4 * g + hl
                    tb2 = rot_tr.next()
                    tv = bank[tb2][:, :].bitcast(BF16)
                    for j in range(4):
                        op(PE, lambda: nc.tensor.transpose(out=tv[:, j * 128:(j + 1) * 128], in_=ocb[oc][:, j, hl, :], identity=ident), r=[ocbb[oc], cidb], w=[bankb[tb2]])
                    copy_op(evac_engine(), catA[:, h, tc * 512:(tc + 1) * 512], tv[:, 0:512], r=[bankb[tb2]], w=[catTb[h][tc]])
        self.tap("catT_nsa", catA[:, :, :], [catTb[c][t] for c in range(8) for t in range(2)], [128, 8, TQ], BF16)
        if self.stop == "p2":
            return self.fin()
        AT.close()
        self.free(RBb + [bonusb, impb, sc2b, m8b, selbfb] + Ptb + ocombb + ocbb + rdb)
        A.close()
        self.free([b for l in KnTb for b in l] + Vaugb + [b for l in qnTb for b in l] + [gatesb, kcmpb, vcaugb, cbb])
        catB = self.tile(CT, "catB", [128, 8, TQ], BF16)

        self.mark("conv")
        CV = self.scope()
        dg = [self.tile(CV, f"dg{j}", [128, 31, 128], BF16) for j in range(2)]
        dgb = [self.buf(f"dg{j}") for j in range(2)]
        cv = self.tile(CV, "cv", [128, 8, TQ], F32)
        cvb_ = [[self.buf(f"cv{c}_{t}") for t in range(2)] for c in range(8)]
        cvs = [self.tile(CV, f"cvs{j}", [128, 2, 512], BF16) for j in range(2)]
        cvsb = [self.buf(f"cvs{j}") for j in range(2)]
        rot_cvs = Rot([0, 1])
        stat_bank = {(0, 0): 4, (0, 1): 5, (1, 0): 6, (1, 1): 7}
        mean = self.tile(CV, "mean", [128, 512], F32)
        meanb = self.buf("mean")
        rstd = self.tile(CV, "rstd", [128, 512], F32)
        rstdb = self.buf("rstd")
        t1 = [self.tile(CV, f"t1_{j}", [128, 512], F32) for j in range(2)]
        t1b = [self.buf(f"t1_{j}") for j in range(2)]
        ident_b = ident.unsqueeze(1).broadcast_to([128, 31, 128])

        def ln_prep(tc):
            sb0, sb1 = stat_bank[(tc, 0)], stat_bank[(tc, 1)]
            op(DVE, lambda: nc.vector.tensor_scalar(out=mean[:, :], in0=bank[sb0][:, :], scalar1=1.0 / 1024, scalar2=None, op0=ALU.mult), r=[bankb[sb0]], w=[meanb])
            op(DVE, lambda: nc.vector.tensor_tensor(out=rstd[:, :], in0=mean[:, :], in1=mean[:, :], op=ALU.mult), r=[meanb], w=[rstdb])
            op(DVE, lambda: nc.vector.scalar_tensor_tensor(out=rstd[:, :], in0=bank[sb1][:, :], scalar=1.0 / 1024, in1=rstd[:, :], op0=ALU.mult, op1=ALU.subtract), r=[bankb[sb1]], w=[rstdb])
            op(ACT, lambda: nc.scalar.activation(out=rstd[:, :], in_=rstd[:, :], func=AF.Sqrt, bias=eps[:, 1:2]), r=[epsb], w=[rstdb])
            op(DVE, lambda: nc.vector.reciprocal(out=rstd[:, :], in_=rstd[:, :]), w=[rstdb])

        def ln_chunk(tc, ch):
            j = ch % 2
            op(DVE, lambda: nc.vector.tensor_tensor(out=t1[j][:, :], in0=cv[:, ch, tc * 512:(tc + 1) * 512], in1=mean[:, :], op=ALU.subtract), r=[cvb_[ch][tc], meanb], w=[t1b[j]])
            op(DVE, lambda: nc.vector.tensor_tensor(out=t1[j][:, :], in0=t1[j][:, :], in1=rstd[:, :], op=ALU.mult), r=[rstdb], w=[t1b[j]])
            op(ACT, lambda: nc.scalar.activation(out=catB[:, ch, tc * 512:(tc + 1) * 512], in_=t1[j][:, :], func=AF.Silu, scale=pvec[:, PV_LNG + ch:PV_LNG + ch + 1], bias=pvec[:, PV_LNB + ch:PV_LNB + ch + 1]), r=[t1b[j], pvb], w=[catTb[8 + ch][tc]])

        pend = []

        def build_dg(n):
            ch_ = n % 8
            wv = pvec[:, PV_CW + ch_ * 31:PV_CW + ch_ * 31 + 31].unsqueeze(2).broadcast_to([128, 31, 128])
            op(DVE, lambda: nc.vector.tensor_tensor(out=dg[n % 2][:, :, :], in0=ident_b, in1=wv, op=ALU.mult), r=[cidb, pvb], w=[dgb[n % 2]])

        it_ = 0
        build_dg(0)
        for tc in range(2):
            for ch in range(8):
                dj = it_ % 2
                if it_ + 1 < 16:
                    build_dg(it_ + 1)
                it_ += 1
                psi = rot_proj.next()
                for j in range(31):
                    o = 2 + j + tc * 512
                    op(PE, lambda: nc.tensor.matmul(bank[psi][:, :], lhsT=dg[dj][:, j, :], rhs=hglu[:, ch, o:o + 512], start=(j == 0), stop=(j == 30)), r=[dgb[dj], hglub[ch]], w=[bankb[psi]])
                while pend:
                    pend.pop(0)()
                op(ACT, lambda: nc.scalar.activation(out=cv[:, ch, tc * 512:(tc + 1) * 512], in_=bank[psi][:, :], func=AF.Identity, bias=pvec[:, PV_CB + ch:PV_CB + ch + 1]), r=[bankb[psi], pvb], w=[cvb_[ch][tc]])
                sj = rot_cvs.next()
                op(DVE, lambda: nc.vector.tensor_copy(out=cvs[sj][:, 0, :], in_=cv[:, ch, tc * 512:(tc + 1) * 512]), r=[cvb_[ch][tc]], w=[cvsb[sj]])
                op(ACT, lambda: nc.scalar.activation(out=cvs[sj][:, 1, :], in_=cv[:, ch, tc * 512:(tc + 1) * 512], func=AF.Square), r=[cvb_[ch][tc]], pw=[cvsb[sj]])

                def stats(tc=tc, ch=ch, sj=sj):
                    for kind in range(2):
                        sb_ = stat_bank[(tc, kind)]
                        op(PE, lambda: nc.tensor.matmul(bank[sb_][:, :], lhsT=ones, rhs=cvs[sj][:, kind, :], start=(ch == 0), stop=(ch == 7)), r=[cvsb[sj], cidb], w=[bankb[sb_]])
                pend.append(stats)
                if tc == 1:
                    ln_chunk(0, ch)
            while pend:
                pend.pop(0)()
            if tc == 0:
                ln_prep(0)
        ln_prep(1)
        for ch in range(8):
            ln_chunk(1, ch)
        self.tap("cv", cv[:, :, :], [b for l in cvb_ for b in l], [128, 8, TQ], F32)
        self.tap("catB", catB[:, :, :], [catTb[c][t] for c in range(8, 16) for t in range(2)], [128, 8, TQ], BF16)
        if self.stop == "p3":
            return self.fin()
        CV.close()
        self.free(dgb + [b for l in cvb_ for b in l] + cvsb + [meanb, rstdb] + t1b)
        HG.close()
        self.free(hglub)

        self.mark("memkv")
        MK = self.scope()
        hmT = self.tile(MK, "hmT", [128, 16, 256], BF16)
        hmTb = self.buf("hmT")
        kmnT = self.tile(MK, "kmnT", [128, 4, 256], BF16)
        kmnTb = [self.buf(f"kmnT{h}") for h in range(4)]
        vmaug = self.tile(MK, "vmaug", [128, 2, 4, NV], BF16)
        vmaugb = self.buf("vmaug")
        S1m = self.scope()
        mt_ = [self.tile(S1m, f"memt{j}", [128, D], F32) for j in range(2)]
        mtb = [self.buf(f"memt{j}") for j in range(2)]
        for j in range(2):
            dma(SP, mt_[j][:, :], self.memb[j * 128:(j + 1) * 128, :], w=[mtb[j]])
        rmsnorm_T(S1m, lambda i: (mt_[i][:, :], [mtb[i]]), 2, PV_NMKV, hmT, lambda i: hmTb, lambda i: i * 128)
        S1m.close()
        self.free(mtb)
        op(POOL, lambda: nc.gpsimd.memset(vmaug[:, :, :, :], 1.0), w=[vmaugb])
        for blk in range(2):
            wt, wb = wget()
            for cc in range(2):
                h = blk * 2 + cc
                psi = proj_fm(wt, wb, cc, hmT, [hmTb], 0, 256)
                pnorm(psi, 256, pvec[:, PV_MKN:PV_MKN + 1], [pvb], kmnT[:, h, :], kmnTb[h])
            flush(0)
            wrel()
        for blk in range(2):
            wt, wb = wget()
            for mt in range(2):
                psi = proj_tm(wt, wb, hmT, [hmTb], mt * 128, 256)
                src = bank[psi][:, 0:256].rearrange("p (g d) -> p g d", g=2)
                copy_op(evac_engine(), vmaug[:, mt, blk * 2:blk * 2 + 2, 0:128], src, r=[bankb[psi]], pw=[vmaugb])
            wrel()

        self.mark("w_out")
        xres = self.tile(G, "xres", [128, 8, D], F32)
        xrb = [[self.buf(f"xres{t}_{c}") for c in range(8)] for t in range(8)]
        for tt in range(8):
            dma(SP, xres[:, tt, :], self.xc[TQ + tt * 128:TQ + (tt + 1) * 128, :], w=xrb[tt], owner=xrb[tt][0])
        for cb in range(8):
            wt, wb = wget()
            for tt in range(8):
                psi = rot_proj.next()
                for kc in range(16):
                    cat_ = catA if kc < 8 else catB
                    op(PE, lambda: nc.tensor.matmul(bank[psi][:, 0:256], lhsT=cat_[:, kc % 8, tt * 128:(tt + 1) * 128], rhs=wt[:, kc, 0:256], start=(kc == 0), stop=(kc == 15)), r=[wb, catTb[kc][tt // 4]], w=[bankb[psi]])
                dst = xres[:, tt, cb * 256:(cb + 1) * 256]
                op(DVE, lambda: nc.vector.tensor_tensor(out=dst, in0=bank[psi][:, 0:256], in1=dst, op=ALU.add), r=[bankb[psi]], w=[xrb[tt][cb]])
            wrel()
        self.tap("x1", xres[:, :, :], [b for l in xrb for b in l], [128, 8, D], F32)
        if self.stop == "p4":
            return self.fin()
        CT.close()
        self.free([b for l in catTb for b in l])

        self.mark("mem")
        M = self.scope()
        h2T = self.tile(M, "h2T", [128, 16, TQ], BF16)
        h2Tb = [self.buf(f"h2T{c}") for c in range(2)]
        S1 = self.scope()
        rmsnorm_T(S1, lambda i: (xres[:, i, :], xrb[i]), 8, PV_NMQ, h2T, lambda i: h2Tb[i // 4], lambda i: i * 128)
        S1.close()
        qmnT = self.tile(M, "qmnT", [128, 4, TQ], BF16)
        qmnTb = [[self.buf(f"qmnT{h}_{t}") for t in range(2)] for h in range(4)]
        omT = self.tile(M, "omT", [128, 4, TQ], BF16)
        omTb = [[self.buf(f"omT{h}_{t}") for t in range(2)] for h in range(4)]
        for blk in range(2):
            wt, wb = wget()
            for cc in range(2):
                h = blk * 2 + cc
                for tc in range(2):
                    psi = proj_fm(wt, wb, cc, h2T, [h2Tb[tc]], tc * 512, 512)
                    pnorm(psi, 512, eps[:, 3:4], [epsb], qmnT[:, h, tc * 512:(tc + 1) * 512], qmnTb[h][tc])
            flush(0)
            wrel()
        Pm = [self.tile(M, f"Pm{j}", [128, 512], BF16) for j in range(3)]
        Pmb = [self.buf(f"Pm{j}") for j in range(3)]
        omem = [self.tile(M, f"omem{j}", [128, 4, 128], BF16) for j in range(2)]
        omemb = [self.buf(f"omem{j}") for j in range(2)]
        rdm = [self.tile(M, f"rdm{j}", [128, 4], F32) for j in range(2)]
        rdmb = [self.buf(f"rdm{j}") for j in range(2)]
        jobs = []
        it = 0
        for h in range(4):
            for tc in range(2):
                Ob = rot_O.next()
                oj = it % 2
                it += 1
                for mt in range(2):
                    def sc(h=h, tc=tc, mt=mt):
                        si = rot_S.next()
                        op(PE, lambda: nc.tensor.matmul(bank[si][:, :], lhsT=kmnT[:, h, mt * 128:(mt + 1) * 128], rhs=qmnT[:, h, tc * 512:(tc + 1) * 512], start=True, stop=True), r=[kmnTb[h], qmnTb[h][tc]], w=[bankb[si]])
                        pj = rot_P.next()
                        op(ACT, lambda: nc.scalar.activation(out=Pm[pj][:, :], in_=bank[si][:, :], func=AF.Exp), r=[bankb[si]], w=[Pmb[pj]])
                        return pj

                    def pvf(pj, h=h, tc=tc, mt=mt, Ob=Ob, oj=oj):
                        for j in range(4):
                            ob = Ob[j // 2]
                            o0 = (j % 2) * NV
                            op(PE, lambda: nc.tensor.matmul(bank[ob][:, o0:o0 + NV], lhsT=Pm[pj][:, j * 128:(j + 1) * 128], rhs=vmaug[:, mt, h, :], start=(mt == 0 and j % 2 == 0), stop=(mt == 1), skip_group_check=True), r=[Pmb[pj], vmaugb], w=[bankb[ob]])
                        if mt == 0:
                            return
                        for b2 in range(2):
                            ob = Ob[b2]
                            op(DVE, lambda: nc.vector.tensor_scalar(out=rdm[oj][:, 2 * b2:2 * b2 + 2], in0=bank[ob][:, 128:128 + NV + 1:NV], scalar1=1e-30, scalar2=None, op0=ALU.max), r=[bankb[ob]], pw=[rdmb[oj]] if b2 else [], w=[rdmb[oj]] if not b2 else [])
                        op(DVE, lambda: nc.vector.reciprocal(out=rdm[oj][:, :], in_=rdm[oj][:, :]), w=[rdmb[oj]])
                        for j in range(4):
                            ob = Ob[j // 2]
                            o0 = (j % 2) * NV
                            op(DVE, lambda: nc.vector.tensor_scalar(out=omem[oj][:, j, :], in0=bank[ob][:, o0:o0 + 128], scalar1=rdm[oj][:, j:j + 1], scalar2=None, op0=ALU.mult), r=[bankb[ob], rdmb[oj]], w=[omemb[oj]] if j == 0 else [], pw=[omemb[oj]] if j else [])
                        tb2 = rot_tr.next()
                        tv = bank[tb2][:, :].bitcast(BF16)
                        for j in range(4):
                            op(PE, lambda: nc.tensor.transpose(out=tv[:, j * 128:(j + 1) * 128], in_=omem[oj][:, j, :], identity=ident), r=[omemb[oj], cidb], w=[bankb[tb2]])
                        copy_op(evac_engine(), omT[:, h, tc * 512:(tc + 1) * 512], tv[:, 0:512], r=[bankb[tb2]], w=[omTb[h][tc]])
                    jobs.append((sc, pvf))
        run_stream(jobs)
        wmo = [wget(), wget()]
        for tt in range(8):
            for cb4 in range(4):
                psi = rot_proj.next()
                for kc in range(4):
                    wt, wb = wmo[kc // 2]
                    op(PE, lambda: nc.tensor.matmul(bank[psi][:, :], lhsT=omT[:, kc, tt * 128:(tt + 1) * 128], rhs=wt[:, kc % 2, cb4 * 512:(cb4 + 1) * 512], start=(kc == 0), stop=(kc == 3)), r=[wb, omTb[kc][tt // 4]], w=[bankb[psi]])
                dst = xres[:, tt, cb4 * 512:(cb4 + 1) * 512]
                op(DVE, lambda: nc.vector.tensor_tensor(out=dst, in0=bank[psi][:, :], in1=dst, op=ALU.add), r=[bankb[psi]], w=[xrb[tt][2 * cb4], xrb[tt][2 * cb4 + 1]])
        wrel()
        self.tap("x2", xres[:, :, :], [b for l in xrb for b in l], [128, 8, D], F32)
        if self.stop == "p5":
            return self.fin()
        M.close()
        self.free(h2Tb + [b for l in qmnTb for b in l] + [b for l in omTb for b in l] + Pmb + omemb + rdmb)
        MK.close()
        self.free([hmTb, vmaugb] + kmnTb)

        self.mark("ffn")
        Fz = self.scope()
        h3T = self.tile(Fz, "h3T", [128, 16, TQ], BF16)
        h3Tb = [self.buf(f"h3T{c}") for c in range(2)]
        S3 = self.scope()
        rmsnorm_T(S3, lambda i: (xres[:, i, :], xrb[i]), 8, PV_NFFN, h3T, lambda i: h3Tb[i // 4], lambda i: i * 128)
        S3.close()
        actT = [self.tile(Fz, f"actT{j}", [128, 2, TQ], BF16) for j in range(2)]
        actTb = [[[self.buf(f"actT{j}_{c}_{t}") for t in range(2)] for c in range(2)] for j in range(2)]
        sgt = [self.tile(Fz, f"sgt{j}", [128, 512], BF16) for j in range(2)]
        sgtb = [self.buf(f"sgt{j}") for j in range(2)]
        rot_g = Rot([4, 5])
        rot_u = Rot([6, 7])
        rot_sgt = Rot([0, 1])
        dheld = []
        for jb in range(22):
            wgt, wgb_ = wget()
            wut, wub = wget()
            aj = jb % 2
            for cc in range(2):
                for tc in range(2):
                    pg = rot_g.next()
                    pu = rot_u.next()
                    for kc in range(16):
                        op(PE, lambda: nc.tensor.matmul(bank[pg][:, :], lhsT=wgt[:, kc, cc * 128:(cc + 1) * 128], rhs=h3T[:, kc, tc * 512:(tc + 1) * 512], start=(kc == 0), stop=(kc == 15)), r=[wgb_, h3Tb[tc]], w=[bankb[pg]])
                    for kc in range(16):
                        op(PE, lambda: nc.tensor.matmul(bank[pu][:, :], lhsT=wut[:, kc, cc * 128:(cc + 1) * 128], rhs=h3T[:, kc, tc * 512:(tc + 1) * 512], start=(kc == 0), stop=(kc == 15)), r=[wub, h3Tb[tc]], w=[bankb[pu]])
                    sj = rot_sgt.next()
                    op(ACT, lambda: nc.scalar.activation(out=sgt[sj][:, :], in_=bank[pg][:, :], func=AF.Silu), r=[bankb[pg]], w=[sgtb[sj]])
                    op(DVE, lambda: nc.vector.tensor_tensor(out=actT[aj][:, cc, tc * 512:(tc + 1) * 512], in0=bank[pu][:, :], in1=sgt[sj][:, :], op=ALU.mult), r=[bankb[pu], sgtb[sj]], w=[actTb[aj][cc][tc]])
            wrel(2, newest=True)
            wdt, wdb = wget()
            dheld.append((aj, wdt, wdb))
            if jb % 2 == 0:
                continue
            for tt in range(8):
                for cb4 in range(4):
                    psi = rot_proj.next()
                    n_ = 0
                    for (a_, wd_, wdb_) in dheld:
                        for cc in range(2):
                            op(PE, lambda: nc.tensor.matmul(bank[psi][:, :], lhsT=actT[a_][:, cc, tt * 128:(tt + 1) * 128], rhs=wd_[:, cc, cb4 * 512:(cb4 + 1) * 512], start=(n_ == 0), stop=(n_ == 3)), r=[wdb_, actTb[a_][cc][tt // 4]], w=[bankb[psi]])
                            n_ += 1
                    dst = xres[:, tt, cb4 * 512:(cb4 + 1) * 512]
                    op(DVE, lambda: nc.vector.tensor_tensor(out=dst, in0=bank[psi][:, :], in1=dst, op=ALU.add), r=[bankb[psi]], w=[xrb[tt][2 * cb4], xrb[tt][2 * cb4 + 1]])
            dheld = []
            wrel()
        self.mark("out")
        for tt in range(8):
            ob_ = self.buf(f"out{tt}")
            dma(SP, self.y[tt * 128:(tt + 1) * 128, :], xres[:, tt, :], r=xrb[tt], owner=ob_)
            self.outtoks.append((ob_.dsem, ob_.dcnt))
        for sem, val in self.outtoks:
            SP.eng.wait_ge(sem.h, val)
        return nc


def _bf(a):
    return np.ascontiguousarray(a.astype(np.float32)).astype(NPBF)


def _consts(s):
    first_real = 0 if s == 1 else 1024
    k = np.arange(TC)
    lbs = np.zeros((128, TC), np.float32)
    lbs[k // 64, k] = BIG
    lbs[32, :] = 16 * (k // 16)
    lbs[33, :] = k % 16
    lbs[34, :] = 1.0
    lbs[35, :] = 1.0
    lbs[36, :] = -BIG
    lbs[37, :] = np.where(k < first_real, -BIG, 0.0)
    lbw = lbs.copy()
    lbw[0:32, :] = 0.0
    lbw[36, :] = 0.0
    c = np.arange(128)
    lbc = np.zeros((128, 128), np.float32)
    lbc[32, :] = 16 * c
    lbc[33, :] = 15.5
    lbc[34, :] = 1.0
    lbc[35, :] = 1.0
    kk = np.arange(128)[:, None]
    x = np.arange(896)[None, :]
    cs = np.where((x - 384) < kk, -BIG, 0.0)
    ws = np.where((x - 384) >= kk, -BIG, 0.0)
    t_ctx = 1024 + np.arange(TQ)[None, :]
    cc = np.arange(128)[:, None]
    cm = np.where((16 * cc + 31 > t_ctx) | (cc >= 127) | (16 * cc < first_real), -BIG, 0.0)
    cs_ = np.arange(128)[:, None] * 16
    bs_ = np.arange(32)[None, :] * 64
    ov = np.clip(np.minimum(cs_ + 32, bs_ + 64) - np.maximum(cs_, bs_), 0, None).astype(np.float32) / 32.0
    ovaug = np.concatenate([np.ones((128, 1), np.float32), ov], axis=1)
    ovaug[127, :] = 0.0
    ident = np.eye(128, dtype=np.float32)
    ones = np.ones((128, 128), np.float32)
    cbf = np.concatenate([lbs, lbw, lbc, cs, ws, cm, ovaug, ident, ones], axis=1)
    assert cbf.shape[1] == NCB
    rb = np.zeros((128, 16, 512), np.float32)
    slopes = 2.0 ** (-(np.arange(NH) + 1.0))
    for h in range(NH):
        for tc in range(2):
            t = 1024 + tc * 512 + np.arange(512)
            i = h * 2 + tc
            rb[0:32, i, :] = 1.0
            rb[32, i, :] = slopes[h]
            rb[33, i, :] = slopes[h]
            rb[34, i, :] = -slopes[h] * (16 * (t // 16))
            rb[35, i, :] = -slopes[h] * (t % 16)
            rb[36, i, :] = 1.0
            rb[37, i, :] = 1.0
    tq = 1024 + np.arange(TQ)[:, None]
    blk = np.arange(32)[None, :]
    cur = tq // 64
    valid = (blk * 64 <= tq) & (blk * 64 >= first_real)
    forced = (blk == first_real // 64) | (blk == cur) | (blk == cur - 1)
    bon = np.where(valid, 1000.0 * forced, -1e30).astype(np.float32)
    bon = bon.reshape(8, 128, 32).transpose(1, 0, 2).reshape(128, 256)
    return _bf(cbf), _bf(rb.reshape(128, 16 * 512)), np.ascontiguousarray(bon)


def _pvec(I):
    pv = np.zeros((128, NPV), np.float32)
    f = lambda a: np.asarray(a, np.float32)
    pv[:, PV_NMIX:PV_NMIX + 16] = f(I["norm_mix"])[0].reshape(16, 128).T
    pv[:, PV_NMQ:PV_NMQ + 16] = f(I["norm_mem_q"])[0].reshape(16, 128).T
    pv[:, PV_NMKV:PV_NMKV + 16] = f(I["norm_mem_kv"])[0].reshape(16, 128).T
    pv[:, PV_NFFN:PV_NFFN + 16] = f(I["norm_ffn"])[0].reshape(16, 128).T
    pv[:, PV_QN] = f(I["q_norm"])[0]
    pv[:, PV_KNC] = f(I["k_norm_cmp"])[0]
    pv[:, PV_KNS] = f(I["k_norm_slc"])[0]
    pv[:, PV_KNW] = f(I["k_norm_win"])[0]
    pv[:, PV_MQN] = f(I["mq_norm"])[0]
    pv[:, PV_MKN] = f(I["mk_norm"])[0]
    pv[:, PV_CB:PV_CB + 8] = f(I["conv_b"])[0].reshape(8, 128).T
    pv[:, PV_LNG:PV_LNG + 8] = f(I["conv_ln_g"])[0].reshape(8, 128).T
    pv[:, PV_LNB:PV_LNB + 8] = f(I["conv_ln_b"])[0].reshape(8, 128).T
    cw = f(I["conv_w"])[0]
    pv[:, PV_CW:PV_CW + 248] = cw.reshape(31, 8, 128).transpose(2, 1, 0).reshape(128, 248)
    pv[:, PV_GB:PV_GB + 24] = f(I["gate_b"])[0][None, :]
    pv[:, PV_POSK:PV_POSK + 32] = f(I["cmp_pos_k"])[0].T
    pv[:, PV_POSV:PV_POSV + 32] = f(I["cmp_pos_v"])[0].T
    return pv


def make_in_maps(I):
    f = lambda a: np.ascontiguousarray(np.asarray(a, np.float32))
    x = f(I["x"])
    mem = f(I["mem"])
    w_in = f(I["w_in"])[0]

    def blk(W, cols):
        out = np.empty((len(cols), 128, 16, 256), np.float32)
        for i, c0 in enumerate(cols):
            out[i] = W[:, c0:c0 + 256].reshape(16, 128, 256).transpose(1, 0, 2)
        return out.reshape(len(cols) * 128, 4096)

    in_cols = [C_KC, C_VC, C_KSL, C_KW, C_VSL, C_VW] + [C_Q + 256 * i for i in range(4)]
    for j in range(4):
        in_cols += [C_U + 256 * j, C_U + 1024 + 256 * j]
    w1b = lambda W: np.ascontiguousarray(f(W)[0].reshape(32, 128, 128).transpose(1, 0, 2).reshape(128, 4096))
    shared = {
        "w_in_b": blk(w_in, in_cols), "w_g": np.ascontiguousarray(w_in[:, C_G:C_G + 24]),
        "w_out_b": blk(f(I["w_out"])[0], [256 * i for i in range(8)]),
        "w_mq_b": blk(f(I["w_mq"])[0], [0, 256]), "w_mk_b": blk(f(I["w_mk"])[0], [0, 256]), "w_mv_b": blk(f(I["w_mv"])[0], [0, 256]),
        "w_mo": f(I["w_mo"])[0],
        "w_gate_b": blk(f(I["w_gate"])[0], [256 * i for i in range(22)]), "w_up_b": blk(f(I["w_up"])[0], [256 * i for i in range(22)]),
        "w_down": f(I["w_down"])[0],
        "w1k_b": w1b(I["cmp_k_w1"]), "w1v_b": w1b(I["cmp_v_w1"]), "w2k": f(I["cmp_k_w2"])[0], "w2v": f(I["cmp_v_w2"])[0],
        "pvec": _pvec(I),
    }
    cs = [_consts(0), _consts(1)]
    maps = []
    for core in range(8):
        b, s = core // 2, core % 2
        if s == 1:
            xc = x[b]
        else:
            xc = np.concatenate([np.zeros((TQ, D), np.float32), x[b, :TQ]], axis=0)
        m = dict(shared)
        m["xc"] = np.ascontiguousarray(xc)
        m["memb"] = mem[b]
        m["cbf"], m["rb"], m["bonus"] = cs[s]
        maps.append(m)
    return maps


_CACHE = {}


def kernel(**inputs):
    if "k" not in _CACHE:
        K = Kern()
        K.build()
        _CACHE["k"] = K
    K = _CACHE["k"]
    maps = make_in_maps(inputs)
    res = run_bass_kernel_spmd(K.nc, maps, core_ids=list(range(8)))
    out = np.zeros((4, 2048, D), np.float32)
    for core in range(8):
        b, s = core // 2, core % 2
        out[b, s * TQ:(s + 1) * TQ] = np.asarray(res.results[core]["y"], np.float32).reshape(TQ, D)
    return out
```

```python
import numpy as np
import ml_dtypes
from contextlib import ExitStack
import concourse.bass as bass
import concourse.mybir as mybir
from concourse.bass_utils import run_bass_kernel_spmd

F32 = mybir.dt.float32
BF16 = mybir.dt.bfloat16
AF = mybir.ActivationFunctionType
ALU = mybir.AluOpType
NPBF = ml_dtypes.bfloat16

D = 2048
TQ = 1024
TC = 2048
NH = 8
DH = 128
IN_W = 4632
FFN = 5632
BIG = 32768.0
NV = 129
NVC = 161

C_Q, C_KC, C_VC, C_KSL, C_VSL, C_KW, C_VW, C_G, C_U = 0, 1024, 1280, 1536, 1792, 2048, 2304, 2560, 2584

PV_NMIX, PV_NMQ, PV_NMKV, PV_NFFN = 0, 16, 32, 48
PV_QN, PV_KNC, PV_KNS, PV_KNW, PV_MQN, PV_MKN = 64, 65, 66, 67, 68, 69
PV_CB, PV_LNG, PV_LNB, PV_CW = 70, 78, 86, 94
PV_GB = 94 + 248
PV_POSK = PV_GB + 24
PV_POSV = PV_POSK + 32
NPV = PV_POSV + 32

CB_LBS, CB_LBW, CB_LBC, CB_CS, CB_WS, CB_CM, CB_OV, CB_ID, CB_ONE = 0, 2048, 4096, 4224, 5120, 6016, 7040, 7073, 7201
NCB = 7329


class Sem:
    n = 0

    def __init__(self, h):
        self.h = h
        Sem.n += 1
        self.id = Sem.n


class Tok:
    __slots__ = ("sem", "val", "eng")

    def __init__(self, sem, val, eng):
        self.sem, self.val, self.eng = sem, val, eng


def _add(d, tok):
    cur = d.get(tok.sem.id)
    if cur is None or cur.val < tok.val:
        d[tok.sem.id] = tok


class Buf:
    def __init__(self, name, fence):
        self.name = name
        self.w = {}
        self.wf = {}
        self.r = dict(fence)
        self.dsem = None
        self.dcnt = 0
        self.excl = False


class Eng:
    EPOCH = 12000

    def __init__(self, K, eng, name, is_pe=False):
        self.K, self.eng, self.name, self.is_pe = K, eng, name, is_pe
        self.seen = {}
        self.ep = 0
        self.nwait = 0
        self.nins = 0
        self._new()

    def _new(self):
        self.sem = self.K.new_sem(f"e{self.name}{self.ep}")
        self.ep += 1
        self.cnt = 0

    def wait(self, tok):
        if tok is None:
            return
        if self.is_pe and tok.eng is self:
            return
        if self.seen.get(tok.sem.id, 0) >= tok.val:
            return
        self.eng.wait_ge(tok.sem.h, tok.val)
        self.nwait += 1
        self.seen[tok.sem.id] = tok.val

    def issue(self, ins):
        if self.cnt >= self.EPOCH:
            self._new()
        ins.then_inc(self.sem.h, 1)
        self.cnt += 1
        self.nins += 1
        return Tok(self.sem, self.cnt, self)


class Scope:
    def __init__(self, K):
        self.K = K
        self.blocks = []

    def close(self):
        for off, n in self.blocks:
            self.K.arena_release(off, n)
        self.blocks = []


class Rot:
    def __init__(self, items):
        self.items = list(items)
        self.i = 0

    def next(self):
        it = self.items[self.i % len(self.items)]
        self.i += 1
        return it


class Kern:
    def __init__(self, taps=(), stop=None):
        self.stop = stop
        self.taps = list(taps)
        self.tap_tensors = {}
        self.nc = bass.Bass("TRN2", target_bir_lowering=False)
        self.nsem = 0
        self.fence = {}
        self.es = ExitStack()

    def new_sem(self, name):
        self.nsem += 1
        return Sem(self.nc.alloc_semaphore(name=f"s{self.nsem}_{name}"))

    def buf(self, name):
        return Buf(name, self.fence)

    def free(self, bufs):
        for b in bufs:
            for t in b.w.values():
                _add(self.fence, t)
            for t in b.r.values():
                _add(self.fence, t)

    ARENA_BYTES = 198 * 1024

    def arena_init(self):
        self.arena = self.es.enter_context(self.nc.sbuf_tensor("arena", [128, self.ARENA_BYTES // 2], BF16))
        self.afree = [(0, self.ARENA_BYTES)]
        self.apeak = 0

    def arena_release(self, off, n):
        self.afree.append((off, n))
        self.afree.sort()
        m = []
        for o, l in self.afree:
            if m and m[-1][0] + m[-1][1] == o:
                m[-1] = (m[-1][0], m[-1][1] + l)
            else:
                m.append((o, l))
        self.afree = m

    def scope(self):
        return Scope(self)

    def tile(self, S, name, shape, dt):
        esz = 4 if dt == F32 else 2
        nel = int(np.prod(shape[1:]))
        nb = (nel * esz + 63) // 64 * 64
        for i, (o, l) in enumerate(self.afree):
            if l >= nb:
                self.afree[i] = (o + nb, l - nb)
                if l == nb:
                    self.afree.pop(i)
                break
        else:
            raise RuntimeError(f"arena OOM allocating {name} {shape} ({nb}B); free={self.afree}")
        S.blocks.append((o, nb))
        self.apeak = max(self.apeak, o + nb)
        v = self.arena[:, o // 2:o // 2 + nel * esz // 2]
        if dt == F32:
            v = v.bitcast(F32)
        if len(shape) == 3:
            v = v.rearrange("p (a b) -> p a b", a=shape[1])
        elif len(shape) == 4:
            v = v.rearrange("p (a b c) -> p a b c", a=shape[1], b=shape[2])
        return v

    def _waits(self, E, r, w, pw):
        for b in r:
            for t in b.w.values():
                E.wait(t)
            if b.excl:
                for t in b.r.values():
                    if t.eng is not E:
                        E.wait(t)
        for b in w:
            for t in b.w.values():
                E.wait(t)
            for t in b.r.values():
                E.wait(t)
        for b in pw:
            for t in b.wf.values():
                E.wait(t)
            for t in b.r.values():
                E.wait(t)

    def _upd(self, tok, r, w, pw):
        for b in w:
            b.w = {tok.sem.id: tok}
            b.wf = {tok.sem.id: tok}
            b.r = {}
        for b in pw:
            _add(b.w, tok)
        for b in r:
            if b not in w and b not in pw:
                _add(b.r, tok)

    def op(self, E, fn, r=(), w=(), pw=()):
        self._waits(E, r, w, pw)
        ins = fn()
        tok = E.issue(ins)
        self._upd(tok, r, w, pw)
        return tok

    def dma(self, Q, out, in_, r=(), w=(), pw=(), owner=None, **kw):
        self._waits(Q, r, w, pw)
        if owner is None:
            owner = (list(w) + list(pw) + list(r))[0]
        if owner.dsem is None:
            owner.dsem = self.new_sem("d" + owner.name)
        ins = Q.eng.dma_start(out=out, in_=in_, **kw)
        owner.dcnt += 16
        ins.then_inc(owner.dsem.h, 16)
        tok = Tok(owner.dsem, owner.dcnt, None)
        self._upd(tok, r, w, pw)
        return tok

    def tap(self, name, ap, bufs, shape, dt=F32):
        if name not in self.taps:
            return
        t = self.nc.dram_tensor("tap_" + name, list(shape), dt, kind="ExternalOutput").ap()
        self.tap_tensors[name] = t
        tb = self.buf("tap" + name)
        self.dma(self.SP, t, ap, r=list(bufs), owner=tb)
        self.outtoks.append((tb.dsem, tb.dcnt))

    def mark(self, name):
        self.marks.append((name, self.PE.nins, self.ACT.nins, self.DVE.nins))

    def fin(self):
        for sem, val in self.outtoks:
            self.SP.eng.wait_ge(sem.h, val)
        return self.nc

    def build(self):
        nc = self.nc
        es = self.es
        dr = lambda n, s, dt=F32: nc.dram_tensor(n, list(s), dt, kind="ExternalInput").ap()
        self.xc = dr("xc", [TC, D])
        self.memb = dr("memb", [256, D])
        self.w_in = dr("w_in_b", [18 * 128, 4096])
        self.w_g = dr("w_g", [D, 24])
        self.w_out = dr("w_out_b", [8 * 128, 4096])
        self.w_mq = dr("w_mq_b", [2 * 128, 4096])
        self.w_mk = dr("w_mk_b", [2 * 128, 4096])
        self.w_mv = dr("w_mv_b", [2 * 128, 4096])
        self.w_mo = dr("w_mo", [512, D])
        self.w_gate = dr("w_gate_b", [22 * 128, 4096])
        self.w_up = dr("w_up_b", [22 * 128, 4096])
        self.w_down = dr("w_down", [FFN, D])
        self.w1k = dr("w1k_b", [128, 4096])
        self.w1v = dr("w1v_b", [128, 4096])
        self.w2k = dr("w2k", [128, 128])
        self.w2v = dr("w2v", [128, 128])
        self.pvec_d = dr("pvec", [128, NPV])
        self.cbf_d = dr("cbf", [128, NCB], BF16)
        self.rb_d = dr("rb", [128, 16 * 512], BF16)
        self.bonus_d = dr("bonus", [128, 256])
        self.y = nc.dram_tensor("y", [TQ, D], F32, kind="ExternalOutput").ap()
        self.outtoks = []
        self.marks = []

        self.PE = Eng(self, nc.tensor, "pe", is_pe=True)
        self.ACT = Eng(self, nc.scalar, "act")
        self.DVE = Eng(self, nc.vector, "dve")
        self.POOL = Eng(self, nc.gpsimd, "pool")
        self.SP = Eng(self, nc.sync, "sp")
        PE, ACT, DVE, POOL, SP = self.PE, self.ACT, self.DVE, self.POOL, self.SP
        op, dma = self.op, self.dma

        self.arena_init()
        self.bank = []
        self.bankb = []
        for i in range(8):
            self.bank.append(es.enter_context(nc.psum_tensor(f"bank{i}", [128, 512], F32)))
            self.bankb.append(self.buf(f"bank{i}"))
            self.bankb[-1].excl = True
        bank, bankb = self.bank, self.bankb

        G = self.scope()
        pvec = self.tile(G, "pvec", [128, NPV], F32)
        pvb = self.buf("pvec")
        cid = self.tile(G, "cid", [128, 256], BF16)
        cidb = self.buf("cid")
        eps = self.tile(G, "eps", [128, 4], F32)
        epsb = self.buf("eps")
        dma(SP, pvec[:, :], self.pvec_d, w=[pvb])
        dma(SP, cid[:, :], self.cbf_d[:, CB_ID:CB_ID + 256], w=[cidb])
        op(DVE, lambda: nc.vector.memset(eps[:, 0:1], 1e-6), w=[epsb])
        op(DVE, lambda: nc.vector.memset(eps[:, 1:2], 1e-5), pw=[epsb])
        op(DVE, lambda: nc.vector.tensor_scalar(out=eps[:, 2:3], in0=pvec[:, PV_QN:PV_QN + 1], scalar1=DH ** -0.5, scalar2=None, op0=ALU.mult), r=[pvb], pw=[epsb])
        op(DVE, lambda: nc.vector.tensor_scalar(out=eps[:, 3:4], in0=pvec[:, PV_MQN:PV_MQN + 1], scalar1=DH ** -0.5, scalar2=None, op0=ALU.mult), r=[pvb], pw=[epsb])
        ident = cid[:, 0:128]
        ones = cid[:, 128:256]

        NSLOT = 6
        wslot = [self.tile(G, f"wslot{i}", [128, 4096], BF16) for i in range(NSLOT)]
        wslotb = [self.buf(f"wslot{i}") for i in range(NSLOT)]
        self.wq = []
        self.wnext_issue = 0

        def wcol(w, i):
            return w[i * 128:(i + 1) * 128, :]

        def wrow(w, r0, n=256):
            return w[r0:r0 + n, :].rearrange("(k p) n -> p k n", p=128)

        def w1v_(w):
            return w

        self.wreleased = set()

        def wissue():
            while self.wnext_issue < len(self.wq):
                i = self.wnext_issue
                if i >= NSLOT and (i - NSLOT) not in self.wreleased:
                    break
                src, shp = self.wq[i]
                s_ = i % NSLOT
                if len(src.shape) == 2:
                    dst = wslot[s_][:, :].rearrange("p (a b) -> p a b", a=4)
                    src = src.rearrange("p (a b) -> p a b", a=4)
                else:
                    dst = wslot[s_][:, :].rearrange("p (k n) -> p k n", k=shp[0])
                dma(POOL, dst, src, w=[wslotb[s_]])
                self.wnext_issue += 1

        self.wcur = 0
        self.wheld = []

        def wget():
            i = self.wcur
            self.wcur += 1
            wissue()
            assert i < self.wnext_issue, "weight block not issuable: too many blocks held"
            self.wheld.append(i)
            s_ = i % NSLOT
            shp = self.wq[i][1]
            return wslot[s_][:, :].rearrange("p (k n) -> p k n", k=shp[0]), wslotb[s_]

        def wrel(n=None, newest=False):
            n = len(self.wheld) if n is None else n
            for _ in range(n):
                self.wreleased.add(self.wheld.pop(-1 if newest else 0))
            wissue()

        kvblocks = [(C_KC, "kc"), (C_VC, "vc"), (C_KSL, "ksl"), (C_KW, "kw"), (C_VSL, "vsl"), (C_VW, "vw")]
        for i in range(18):
            self.wq.append((wcol(self.w_in, i), (16, 256)))
        self.wq.append((w1v_(self.w1k), (32, 128)))
        self.wq.append((w1v_(self.w1v), (32, 128)))
        for w in (self.w_mk, self.w_mv):
            for i in range(2):
                self.wq.append((wcol(w, i), (16, 256)))
        for i in range(8):
            self.wq.append((wcol(self.w_out, i), (16, 256)))
        for i in range(2):
            self.wq.append((wcol(self.w_mq, i), (16, 256)))
        for i in range(2):
            self.wq.append((wrow(self.w_mo, 256 * i), (2, 2048)))
        for j in range(22):
            self.wq.append((wcol(self.w_gate, j), (16, 256)))
            self.wq.append((wcol(self.w_up, j), (16, 256)))
            self.wq.append((wrow(self.w_down, 256 * j), (2, 2048)))

        wg = self.tile(G, "wg", [128, 16, 24], BF16)
        wgb = self.buf("wg")
        dma(POOL, wg[:, :, :], self.w_g.rearrange("(k p) n -> p k n", p=128), w=[wgb])
        w2 = self.tile(G, "w2", [128, 2, 128], BF16)
        w2b = self.buf("w2")
        dma(POOL, w2[:, 0, :], self.w2k, w=[w2b])
        dma(POOL, w2[:, 1, :], self.w2v, pw=[w2b])
        wissue()
        if self.stop == "init":
            self.tap("w0", wslot[0][:, :], [wslotb[0]], [128, 4096], BF16)
            self.tap("w4", wslot[4][:, :], [wslotb[4]], [128, 4096], BF16)
            self.tap("wg", wg[:, :, :], [wgb], [128, 16, 24], BF16)
            self.tap("pvec", pvec[:, :], [pvb], [128, NPV], F32)
            self.tap("cid", cid[:, :], [cidb], [128, 256], BF16)
            self.tap("eps", eps[:, :], [epsb], [128, 4], F32)
            return self.fin()

        rot_proj = Rot([0, 1, 2, 3])
        rot_ssq = Rot([4, 5])
        rot_tr = Rot([6, 7])
        self.flip = 0

        def evac_engine():
            self.flip ^= 1
            return ACT if self.flip else DVE

        def copy_op(E, out, in_, r, w=(), pw=()):
            if E is ACT:
                return op(ACT, lambda: nc.scalar.copy(out=out, in_=in_), r=r, w=w, pw=pw)
            return op(E, lambda: E.eng.tensor_copy(out=out, in_=in_), r=r, w=w, pw=pw)

        def norm_tile_a(xap, xbufs, xs_, xsb_, ss_, ssb_):
            op(ACT, lambda: nc.scalar.activation(out=xs_[:, :], in_=xap, func=AF.Square, accum_out=ss_[:, 0:1]), r=xbufs, w=[xsb_, ssb_])
            op(ACT, lambda: nc.scalar.activation(out=ss_[:, 1:2], in_=ss_[:, 0:1], func=AF.Sqrt, scale=1.0 / D, bias=eps[:, 0:1]), r=[epsb], w=[ssb_])
            op(DVE, lambda: nc.vector.reciprocal(out=ss_[:, 2:3], in_=ss_[:, 1:2]), w=[ssb_])
            op(DVE, lambda: nc.vector.tensor_scalar(out=xs_[:, :], in0=xap, scalar1=ss_[:, 2:3], scalar2=None, op0=ALU.mult), r=list(xbufs) + [ssb_], w=[xsb_])

        def norm_tile_b(gcol, dstT, t0, dstbuf, xs_, xsb_, tail=None):
            for half in range(2):
                bi = rot_tr.next()
                bv = bank[bi][:, :].bitcast(BF16)
                for c8 in range(8):
                    c = half * 8 + c8
                    op(PE, lambda: nc.tensor.transpose(out=bv[:, c8 * 128:(c8 + 1) * 128], in_=xs_[:, c * 128:(c + 1) * 128], identity=ident), r=[xsb_, cidb], w=[bankb[bi]])
                src = bv[:, 0:1024].rearrange("p (c t) -> p c t", c=8)
                dst = dstT[:, half * 8:(half + 1) * 8, t0:t0 + 128]
                g = pvec[:, gcol + half * 8:gcol + half * 8 + 8].unsqueeze(2).broadcast_to([128, 8, 128])
                op(DVE, lambda: nc.vector.tensor_tensor(out=dst, in0=src, in1=g, op=ALU.mult), r=[bankb[bi], pvb], pw=[dstbuf])
                if tail is not None:
                    tl, tlb = tail
                    op(POOL, lambda: nc.gpsimd.tensor_copy(out=tl[:, half * 8:(half + 1) * 8, :], in_=dstT[:, half * 8:(half + 1) * 8, t0 + 96:t0 + 128]), r=[dstbuf], pw=[tlb])

        def norm_tile(xap, xbufs, gcol, dstT, t0, dstbuf, xs_, xsb_, ss_, ssb_, tail=None):
            norm_tile_a(xap, xbufs, xs_, xsb_, ss_, ssb_)
            norm_tile_b(gcol, dstT, t0, dstbuf, xs_, xsb_, tail=tail)

        def rmsnorm_T(S, get_x, ntiles, gcol, dstT, dstbuf_of, tcol_of, tail=None):
            xs = [self.tile(S, f"xs{j}", [128, D], BF16) for j in range(2)]
            xsb = [self.buf(f"xs{j}") for j in range(2)]
            ss = [self.tile(S, f"ss{j}", [128, 4], F32) for j in range(2)]
            ssb = [self.buf(f"ss{j}") for j in range(2)]
            for i in range(ntiles):
                xap, xbufs = get_x(i)
                j = i % 2
                norm_tile(xap, xbufs, gcol, dstT, tcol_of(i), dstbuf_of(i), xs[j], xsb[j], ss[j], ssb[j], tail=tail if i == ntiles - 1 else None)
            self.free(xsb + ssb)

        P1 = self.scope()
        sqt = [self.tile(G, f"sqt{j}", [128, 512], BF16) for j in range(2)]
        sqtb = [self.buf(f"sqt{j}") for j in range(2)]
        rtt = [self.tile(G, f"rtt{j}", [128, 512], F32) for j in range(2)]
        rttb = [self.buf(f"rtt{j}") for j in range(2)]
        rot_sq = Rot([0, 1])

        def pnorm(psi, n, gain_ap, gain_bufs, dst, dstb):
            j = rot_sq.next()
            ps = bank[psi]
            op(ACT, lambda: nc.scalar.activation(out=sqt[j][:, 0:n], in_=ps[:, 0:n], func=AF.Square), r=[bankb[psi]], w=[sqtb[j]])

            def tail():
                si = rot_ssq.next()
                op(PE, lambda: nc.tensor.matmul(bank[si][:, 0:n], lhsT=ones, rhs=sqt[j][:, 0:n], start=True, stop=True), r=[sqtb[j], cidb], w=[bankb[si]])
                op(ACT, lambda: nc.scalar.activation(out=rtt[j][:, 0:n], in_=bank[si][:, 0:n], func=AF.Sqrt, scale=1.0 / DH, bias=eps[:, 0:1]), r=[bankb[si], epsb], w=[rttb[j]])
                op(DVE, lambda: nc.vector.reciprocal(out=rtt[j][:, 0:n], in_=rtt[j][:, 0:n]), w=[rttb[j]])
                op(DVE, lambda: nc.vector.scalar_tensor_tensor(out=dst, in0=ps[:, 0:n], scalar=gain_ap, in1=rtt[j][:, 0:n], op0=ALU.mult, op1=ALU.mult), r=[bankb[psi], rttb[j]] + list(gain_bufs), pw=[dstb])
            self.deferred.append(tail)

        self.deferred = []

        def flush(keep=0):
            while len(self.deferred) > keep:
                self.deferred.pop(0)()

        def proj_fm(wt, wb, cc, actT, actbufs, t0, n):
            psi = rot_proj.next()
            for kc in range(16):
                op(PE, lambda: nc.tensor.matmul(bank[psi][:, 0:n], lhsT=wt[:, kc, cc * 128:(cc + 1) * 128], rhs=actT[:, kc, t0:t0 + n], start=(kc == 0), stop=(kc == 15)), r=[wb] + list(actbufs), w=[bankb[psi]])
            flush(0)
            return psi

        def proj_tm(wt, wb, actT, actbufs, t0, ncols, kcs=16):
            psi = rot_proj.next()
            for kc in range(kcs):
                op(PE, lambda: nc.tensor.matmul(bank[psi][:, 0:ncols], lhsT=actT[:, kc, t0:t0 + 128], rhs=wt[:, kc, 0:ncols], start=(kc == 0), stop=(kc == kcs - 1)), r=[wb] + list(actbufs), w=[bankb[psi]])
            return psi

        A = self.scope()
        cbf = self.tile(A, "cbf", [128, CB_ID], BF16)
        cbb = self.buf("cbf")
        dma(SP, cbf[:, :], self.cbf_d[:, 0:CB_ID], w=[cbb])
        KnT = self.tile(A, "KnT", [128, 4, TC], BF16)
        KnTb = [[self.buf(f"KnT{i}_{c}") for c in range(4)] for i in range(4)]
        Vaug = self.tile(A, "Vaug", [128, 16, 4, NV], BF16)
        Vaugb = [self.buf(f"Vaug{i}") for i in range(16)]
        CK = self.scope()
        kcvcT = self.tile(CK, "kcvcT", [128, 4, TC], BF16)
        kcvcb = [self.buf(f"kcvc{i}") for i in range(4)]
        vab = self.buf("vaug_init")
        op(POOL, lambda: nc.gpsimd.memset(Vaug[:, :, :, :], 1.0), w=Vaugb)

        if self.stop == "p0a":
            self.tap("Vaug", Vaug[:, :, :, :], Vaugb, [128, 16, 4, NV], BF16)
            self.tap("cbf", cbf[:, :], [cbb], [128, CB_ID], BF16)
            return self.fin()
        hTs = [self.tile(P1, f"hT{c}", [128, 16, 512], BF16) for c in range(2)]
        hTb = [self.buf(f"hT{c}") for c in range(2)]
        htail = self.tile(P1, "htail", [128, 16, 32], BF16)
        htailb = self.buf("htail")

        S0 = self.scope()
        xt = [self.tile(S0, f"xt{j}", [128, D], F32) for j in range(2)]
        xtb = [self.buf(f"xt{j}") for j in range(2)]
        xs0 = [self.tile(S0, f"xs{j}", [128, D], BF16) for j in range(2)]
        xs0b = [self.buf(f"xs{j}") for j in range(2)]
        ss0 = [self.tile(S0, f"ss{j}", [128, 4], F32) for j in range(2)]
        ss0b = [self.buf(f"ss{j}") for j in range(2)]

        def xload(gt):
            dma(SP, xt[gt % 2][:, :], self.xc[gt * 128:(gt + 1) * 128, :], w=[xtb[gt % 2]])

        def norm_a(gt):
            if gt + 1 < 16:
                xload(gt + 1)
            j = gt % 2
            norm_tile_a(xt[j][:, :], [xtb[j]], xs0[j], xs0b[j], ss0[j], ss0b[j])

        def norm_b(gt):
            q_, i_ = gt // 4, gt % 4
            j = gt % 2
            norm_tile_b(PV_NMIX, hTs[q_ % 2], i_ * 128, hTb[q_ % 2], xs0[j], xs0b[j], tail=(htail, htailb) if gt == 7 else None)

        kvw = []

        def kv_block(q_, c0, kind, k_):
            hT_, hb_ = hTs[q_ % 2], hTb[q_ % 2]
            if q_ == 0:
                kvw.append(wget())
            wt, wb = kvw[k_]
            tg = q_ * 512
            if kind in ("kc", "vc", "ksl", "kw"):
                for cc in range(2):
                    psi = proj_fm(wt, wb, cc, hT_, [hb_], 0, 512)
                    if kind in ("kc", "vc"):
                        idx = (0 if kind == "kc" else 2) + cc
                        copy_op(evac_engine(), kcvcT[:, idx, tg:tg + 512], bank[psi][:, :], r=[bankb[psi]], pw=[kcvcb[idx]])
                    else:
                        idx = (0 if kind == "ksl" else 2) + cc
                        gcol = PV_KNS if kind == "ksl" else PV_KNW
                        pnorm(psi, 512, pvec[:, gcol:gcol + 1], [pvb], KnT[:, idx, tg:tg + 512], KnTb[idx][q_])
                flush(0)
            else:
                vk = 0 if kind == "vsl" else 2
                for tt in range(4):
                    psi = proj_tm(wt, wb, hT_, [hb_], tt * 128, 256)
                    kt = q_ * 4 + tt
                    E = evac_engine()
                    src = bank[psi][:, 0:256].rearrange("p (g d) -> p g d", g=2)
                    copy_op(E, Vaug[:, kt, vk:vk + 2, 0:128], src, r=[bankb[psi]], pw=[Vaugb[kt]])
            if q_ == 3:
                wrel(1)

        self.mark("p1_start")
        xload(0)
        norm_a(0)
        for gt in range(4):
            norm_b(gt)
            norm_a(gt + 1)
        for q_ in range(4):
            for k_, (c0, kind) in enumerate(kvblocks):
                kv_block(q_, c0, kind, k_)
                if q_ < 3 and k_ < 4:
                    gt = (q_ + 1) * 4 + k_
                    norm_b(gt)
                    if gt + 1 < 16 and k_ < 3:
                        norm_a(gt + 1)
            if q_ < 2:
                norm_a((q_ + 2) * 4)
            flush(0)
            if self.stop == "p0" and q_ == 0:
                return self.fin()
        S0.close()
        self.free(xtb + xs0b + ss0b)
        qnT = self.tile(A, "qnT", [128, NH, TQ], BF16)
        qnTb = [[self.buf(f"qnT{h}_{c}") for c in range(2)] for h in range(NH)]
        gates = self.tile(A, "gates", [128, 8, 24], F32)
        gatesb = self.buf("gates")
        HG = self.scope()
        hglu = self.tile(HG, "hglu", [128, 8, 1056], BF16)
        hglub = [self.buf(f"hglu{c}") for c in range(8)]

        self.mark("q_proj")
        for blk in range(4):
            wt, wb = wget()
            for cc in range(2):
                h = blk * 2 + cc
                for tc in range(2):
                    psi = proj_fm(wt, wb, cc, hTs[tc], [hTb[tc]], 0, 512)
                    pnorm(psi, 512, eps[:, 2:3], [epsb], qnT[:, h, tc * 512:(tc + 1) * 512], qnTb[h][tc])
            flush(0)
            wrel()
        gtmp = self.tile(P1, "gtmp", [128, 24], F32)
        gtmpb = self.buf("gtmp")
        for tt in range(8):
            psi = rot_proj.next()
            for kc in range(16):
                op(PE, lambda: nc.tensor.matmul(bank[psi][:, 0:24], lhsT=hTs[tt // 4][:, kc, (tt % 4) * 128:(tt % 4 + 1) * 128], rhs=wg[:, kc, :], start=(kc == 0), stop=(kc == 15)), r=[wgb, hTb[tt // 4]], w=[bankb[psi]])
            op(DVE, lambda: nc.vector.tensor_tensor(out=gtmp[:, :], in0=bank[psi][:, 0:24], in1=pvec[:, PV_GB:PV_GB + 24], op=ALU.add), r=[bankb[psi], pvb], w=[gtmpb])
            op(ACT, lambda: nc.scalar.activation(out=gates[:, tt, :], in_=gtmp[:, :], func=AF.Sigmoid), r=[gtmpb], pw=[gatesb])
        self.mark("u_proj")
        sg = [self.tile(P1, f"sg{j}", [128, 512], F32) for j in range(2)]
        sgb = [self.buf(f"sg{j}") for j in range(2)]
        rot_sg = Rot([0, 1])
        for j4 in range(4):
            wa, wab = wget()
            wbt, wbb = wget()
            for cc in range(2):
                ch = j4 * 2 + cc
                for seg in range(3):
                    if seg == 0:
                        actT, actb, t0, n, off = htail, [htailb], 0, 32, 0
                    else:
                        actT, actb, t0, n, off = hTs[seg - 1], [hTb[seg - 1]], 0, 512, 32 + (seg - 1) * 512
                    pa = proj_fm(wa, wab, cc, actT, actb, t0, n)
                    pb = proj_fm(wbt, wbb, cc, actT, actb, t0, n)
                    j = rot_sg.next()
                    op(ACT, lambda: nc.scalar.activation(out=sg[j][:, 0:n], in_=bank[pb][:, 0:n], func=AF.Sigmoid), r=[bankb[pb]], w=[sgb[j]])
                    op(DVE, lambda: nc.vector.tensor_tensor(out=hglu[:, ch, off:off + n], in0=bank[pa][:, 0:n], in1=sg[j][:, 0:n], op=ALU.mult), r=[bankb[pa], sgb[j]], pw=[hglub[ch]])
            wrel()
        self.tap("KnT", KnT[:, :, :], [b for l in KnTb for b in l], [128, 4, TC], BF16)
        self.tap("qnT", qnT[:, :, :], [b for l in qnTb for b in l], [128, NH, TQ], BF16)
        self.tap("kcvcT", kcvcT[:, :, :], kcvcb, [128, 4, TC], BF16)
        self.tap("Vaug", Vaug[:, :, :, :], Vaugb, [128, 16, 4, NV], BF16)
        self.tap("gates", gates[:, :, :], [gatesb], [128, 8, 24], F32)
        self.tap("hglu", hglu[:, :, :], hglub, [128, 8, 1056], BF16)
        P1.close()
        self.free(hTb + [htailb, gtmpb] + sgb)
        if self.stop == "p1":
            return self.fin()

        self.mark("compress")
        kcmpT = self.tile(A, "kcmpT", [128, 2, 128], BF16)
        kcmpb = self.buf("kcmpT")
        vcaug = self.tile(A, "vcaug", [128, 2, NVC], BF16)
        vcaugb = self.buf("vcaug")
        C1 = self.scope()
        posbf = self.tile(C1, "posbf", [128, 64], BF16)
        posbfb = self.buf("posbf")
        posb = self.tile(C1, "posb", [128, 2], F32)
        posbb = self.buf("posb")
        h1 = [self.tile(C1, f"h1_{j}", [128, 128], BF16) for j in range(2)]
        h1b = [self.buf(f"h1_{j}") for j in range(2)]
        op(DVE, lambda: nc.vector.memset(kcmpT[:, :, :], 0.0), w=[kcmpb])
        op(DVE, lambda: nc.vector.memset(vcaug[:, :, :], 0.0), w=[vcaugb])
        for g in range(2):
            op(DVE, lambda: nc.vector.tensor_copy(out=vcaug[:, g, 128:NVC], in_=cbf[:, CB_OV:CB_OV + 33]), r=[cbb], pw=[vcaugb])
        op(DVE, lambda: nc.vector.tensor_copy(out=posbf[:, :], in_=pvec[:, PV_POSK:PV_POSK + 64]), r=[pvb], w=[posbfb])
        rot_h1 = Rot([0, 1])
        for kind in range(2):
            w1, w1b = wget()
            pbi = rot_proj.next()
            for l in range(32):
                op(PE, lambda: nc.tensor.matmul(bank[pbi][:, 0:1], lhsT=w1[:, l, :], rhs=posbf[:, kind * 32 + l:kind * 32 + l + 1], start=(l == 0), stop=(l == 31)), r=[w1b, posbfb], w=[bankb[pbi]])
            op(DVE, lambda: nc.vector.tensor_copy(out=posb[:, kind:kind + 1], in_=bank[pbi][:, 0:1]), r=[bankb[pbi]], pw=[posbb])
            for g in range(2):
                psi = rot_proj.next()
                src = kcvcT[:, kind * 2 + g, :]
                for l in range(32):
                    op(PE, lambda: nc.tensor.matmul(bank[psi][:, 0:127], lhsT=w1[:, l, :], rhs=src[:, l:l + 16 * 126 + 1:16], start=(l == 0), stop=(l == 31)), r=[w1b, kcvcb[kind * 2 + g]], w=[bankb[psi]])
                j = rot_h1.next()
                op(ACT, lambda: nc.scalar.activation(out=h1[j][:, 0:127], in_=bank[psi][:, 0:127], func=AF.Silu, bias=posb[:, kind:kind + 1]), r=[bankb[psi], posbb], w=[h1b[j]])
                ps2 = rot_proj.next()
                if kind == 0:
                    op(PE, lambda: nc.tensor.matmul(bank[ps2][:, 0:127], lhsT=w2[:, 0, :], rhs=h1[j][:, 0:127], start=True, stop=True), r=[w2b, h1b[j]], w=[bankb[ps2]])
                    pnorm(ps2, 127, pvec[:, PV_KNC:PV_KNC + 1], [pvb], kcmpT[:, g, 0:127], kcmpb)
                    flush(0)
                else:
                    op(PE, lambda: nc.tensor.matmul(bank[ps2][0:127, 0:128], lhsT=h1[j][:, 0:127], rhs=w2[:, 1, :], start=True, stop=True), r=[w2b, h1b[j]], w=[bankb[ps2]])
                    copy_op(evac_engine(), vcaug[0:127, g, 0:128], bank[ps2][0:127, 0:128], r=[bankb[ps2]], pw=[vcaugb])
            wrel()
        self.tap("kcmpT", kcmpT[:, :, :], [kcmpb], [128, 2, 128], BF16)
        self.tap("vcaug", vcaug[:, :, :], [vcaugb], [128, 2, NVC], BF16)
        C1.close()
        self.free([posbfb, posbb] + h1b)
        if self.stop == "p1c":
            return self.fin()
        CK.close()
        self.free(kcvcb)

        self.mark("attn")
        CT = self.scope()
        catA = self.tile(CT, "catA", [128, 8, TQ], BF16)
        catTb = [[self.buf(f"catT{c}_{t}") for t in range(2)] for c in range(16)]
        AT = self.scope()
        RB = self.tile(AT, "RB", [128, 16, 512], BF16)
        RBb = [self.buf(f"RB{i}") for i in range(16)]
        dma(SP, RB[:, :, :], self.rb_d.rearrange("p (i n) -> p i n", i=16), w=RBb, owner=RBb[0])
        bonus = self.tile(AT, "bonus", [128, 8, 32], F32)
        bonusb = self.buf("bonus")
        dma(SP, bonus[:, :, :], self.bonus_d.rearrange("p (i n) -> p i n", i=8), w=[bonusb])
        Pt = [self.tile(AT, f"Pt{j}", [128, 512], BF16) for j in range(3)]
        Ptb = [self.buf(f"Pt{j}") for j in range(3)]
        rot_P = Rot([0, 1, 2])
        rot_S = Rot([0, 1, 6])
        rot_O = Rot([(2, 3), (4, 5)])
        ocomb = [self.tile(AT, f"ocomb{j}", [128, 4, 4, 128], F32) for j in range(1)]
        ocombb = [self.buf(f"ocomb{j}") for j in range(1)]
        ocb = [self.tile(AT, f"ocb{j}", [128, 4, 4, 128], BF16) for j in range(1)]
        ocbb = [self.buf(f"ocb{j}") for j in range(1)]
        imp = self.tile(AT, "imp", [128, 4, 32], F32)
        impb = self.buf("imp")
        sc2 = self.tile(AT, "sc2", [128, 4, 32], F32)
        sc2b = self.buf("sc2")
        m8 = self.tile(AT, "m8", [128, 16], F32)
        m8b = self.buf("m8")
        selbf = self.tile(AT, "selbf", [128, 4, 32], BF16)
        selbfb = self.buf("selbf")
        rdt = [self.tile(AT, f"rd{j}", [128, 8], F32) for j in range(4)]
        rdb = [self.buf(f"rd{j}") for j in range(4)]
        rot_rd = Rot([0, 1, 2, 3])

        def attn_tile(Kt, Kb, Qt, Qb, LBap, RBi, mask_ap):
            si = rot_S.next()
            n_extra = 1 if mask_ap is not None else 0
            op(PE, lambda: nc.tensor.matmul(bank[si][:, :], lhsT=Kt, rhs=Qt, start=True, stop=False), r=list(Kb) + list(Qb), w=[bankb[si]])
            op(PE, lambda: nc.tensor.matmul(bank[si][:, :], lhsT=LBap, rhs=RB[:, RBi, :], start=False, stop=(n_extra == 0)), r=[cbb, RBb[RBi]], w=[bankb[si]])
            if mask_ap is not None:
                op(PE, lambda: nc.tensor.matmul(bank[si][:, :], lhsT=ident, rhs=mask_ap, start=False, stop=True), r=[cbb, cidb], w=[bankb[si]])
            pj = rot_P.next()
            op(ACT, lambda: nc.scalar.activation(out=Pt[pj][:, :], in_=bank[si][:, :], func=AF.Exp), r=[bankb[si]], w=[Ptb[pj]])
            return pj

        def pv(pj, Obanks, Vap, Vb, nv, first, last):
            for j in range(4):
                ob = Obanks[j // 2]
                o0 = (j % 2) * nv
                op(PE, lambda: nc.tensor.matmul(bank[ob][:, o0:o0 + nv], lhsT=Pt[pj][:, j * 128:(j + 1) * 128], rhs=Vap, start=(first and j % 2 == 0), stop=last, skip_group_check=True), r=[Ptb[pj]] + list(Vb), w=[bankb[ob]])

        def finish(Obanks, nv, h, tc, br, oc, mode):
            hl = h % 4
            rj = rot_rd.next()
            rd = rdt[rj]
            for b2 in range(2):
                ob = Obanks[b2]
                op(DVE, lambda: nc.vector.tensor_scalar(out=rd[:, 2 * b2:2 * b2 + 2], in0=bank[ob][:, 128:128 + nv + 1:nv], scalar1=1e-30, scalar2=None, op0=ALU.max), r=[bankb[ob]], pw=[rdb[rj]])
            op(DVE, lambda: nc.vector.reciprocal(out=rd[:, 0:4], in_=rd[:, 0:4]), w=[rdb[rj]])
            gcol = h * 3 + br
            op(DVE, lambda: nc.vector.tensor_tensor(out=rd[:, 4:8], in0=rd[:, 0:4], in1=gates[:, tc * 4:tc * 4 + 4, gcol], op=ALU.mult), r=[gatesb], w=[rdb[rj]])
            for j in range(4):
                ob = Obanks[j // 2]
                o0 = (j % 2) * nv
                src = bank[ob][:, o0:o0 + 128]
                f = rd[:, 4 + j:5 + j]
                if mode == "first":
                    op(DVE, lambda: nc.vector.tensor_scalar(out=ocomb[oc][:, j, hl, :], in0=src, scalar1=f, scalar2=None, op0=ALU.mult), r=[bankb[ob], rdb[rj]], pw=[ocombb[oc]])
                elif mode == "add":
                    op(DVE, lambda: nc.vector.scalar_tensor_tensor(out=ocomb[oc][:, j, hl, :], in0=src, scalar=f, in1=ocomb[oc][:, j, hl, :], op0=ALU.mult, op1=ALU.add), r=[bankb[ob], rdb[rj]], w=[ocombb[oc]])
                else:
                    op(DVE, lambda: nc.vector.scalar_tensor_tensor(out=ocb[oc][:, j, hl, :], in0=src, scalar=f, in1=ocomb[oc][:, j, hl, :], op0=ALU.mult, op1=ALU.add), r=[bankb[ob], rdb[rj], ocombb[oc]], pw=[ocbb[oc]])
            return rj

        LBS = lambda i: cbf[:, CB_LBS + i * 128:CB_LBS + (i + 1) * 128]
        LBW = lambda i: cbf[:, CB_LBW + i * 128:CB_LBW + (i + 1) * 128]
        LBC = cbf[:, CB_LBC:CB_LBC + 128]
        CSm = lambda m: cbf[:, CB_CS + 384 - 128 * m:CB_CS + 384 - 128 * m + 512]
        WSm = lambda m: cbf[:, CB_WS + 384 - 128 * m:CB_WS + 384 - 128 * m + 512]

        def run_stream(jobs, L=2):
            pend = []
            for n in range(len(jobs) + L):
                if n < len(jobs):
                    pend.append((jobs[n], jobs[n][0]()))
                if n >= L:
                    job, pj = pend.pop(0)
                    job[1](pj)

        for g in range(2):
            for tc in range(2):
                oc = 0
                Q = lambda h: qnT[:, h, tc * 512:(tc + 1) * 512]
                jobs = []
                for hl in range(4):
                    h = 4 * g + hl
                    Ob = rot_O.next()

                    def sc(h=h):
                        return attn_tile(kcmpT[:, g, :], [kcmpb], Q(h), [qnTb[h][tc]], LBC, h * 2 + tc, cbf[:, CB_CM + tc * 512:CB_CM + (tc + 1) * 512])

                    def pvf(pj, h=h, hl=hl, Ob=Ob):
                        pv(pj, Ob, vcaug[:, g, :], [vcaugb], NVC, True, True)
                        rj = finish(Ob, NVC, h, tc, 0, oc, "first")
                        rd = rdt[rj]
                        for j in range(4):
                            ob = Ob[j // 2]
                            o0 = (j % 2) * NVC + 129
                            if hl == 0:
                                op(DVE, lambda: nc.vector.tensor_scalar(out=imp[:, j, :], in0=bank[ob][:, o0:o0 + 32], scalar1=rd[:, j:j + 1], scalar2=None, op0=ALU.mult), r=[bankb[ob], rdb[rj]], w=[impb] if j == 0 else [], pw=[impb] if j else [])
                            else:
                                op(DVE, lambda: nc.vector.scalar_tensor_tensor(out=imp[:, j, :], in0=bank[ob][:, o0:o0 + 32], scalar=rd[:, j:j + 1], in1=imp[:, j, :], op0=ALU.mult, op1=ALU.add), r=[bankb[ob], rdb[rj]], w=[impb])
                    jobs.append((sc, pvf))
                for hl in range(4):
                    h = 4 * g + hl
                    Ob = rot_O.next()
                    ks = list(range(4 + 4 * tc, 12 + 4 * tc))
                    for n_, i in enumerate(ks):
                        m = i - (4 + 4 * tc)
                        mask = WSm(m) if m < 4 else CSm(m - 4)

                        def sc(h=h, i=i, mask=mask):
                            return attn_tile(KnT[:, 2 + g, i * 128:(i + 1) * 128], [KnTb[2 + g][i // 4]], Q(h), [qnTb[h][tc]], LBW(i), h * 2 + tc, mask)

                        def pvf(pj, h=h, i=i, n_=n_, Ob=Ob, last=(n_ == len(ks) - 1)):
                            pv(pj, Ob, Vaug[:, i, 2 + g, :], [Vaugb[i]], NV, n_ == 0, last)
                            if last:
                                finish(Ob, NV, h, tc, 2, oc, "add")
                        jobs.append((sc, pvf))
                run_stream(jobs[:4])
                op(DVE, lambda: nc.vector.tensor_tensor(out=imp[:, :, :], in0=imp[:, :, :], in1=bonus[:, tc * 4:tc * 4 + 4, :], op=ALU.add), r=[bonusb], w=[impb])
                tb = rot_tr.next()
                tbv = bank[tb][:, :].bitcast(BF16)
                for j in range(4):
                    op(DVE, lambda: nc.vector.max(out=m8[:, 0:8], in_=imp[:, j, :]), r=[impb], w=[m8b])
                    op(DVE, lambda: nc.vector.match_replace(out=sc2[:, j, :], in_to_replace=m8[:, 0:8], in_values=imp[:, j, :], imm_value=-3.0e38), r=[impb, m8b], w=[sc2b])
                    op(DVE, lambda: nc.vector.max(out=m8[:, 8:16], in_=sc2[:, j, :]), r=[sc2b], w=[m8b])
                    op(DVE, lambda: nc.vector.tensor_scalar(out=selbf[:, j, :], in0=imp[:, j, :], scalar1=m8[:, 15:16], scalar2=None, op0=ALU.is_ge), r=[impb, m8b], w=[selbfb])
                if g == 0 and tc == 1:
                    self.tap("imp", imp[:, :, :], [impb], [128, 4, 32], F32)
                    self.tap("selbf", selbf[:, :, :], [selbfb], [128, 4, 32], BF16)
                run_stream(jobs[4:])
                for j in range(4):
                    op(PE, lambda: nc.tensor.transpose(out=tbv[0:32, j * 128:(j + 1) * 128], in_=selbf[:, j, :], identity=ident), r=[selbfb, cidb], w=[bankb[tb]])
                for hl in range(4):
                    h = 4 * g + hl
                    copy_op(ACT, RB[0:32, h * 2 + tc, :], tbv[0:32, 0:512], r=[bankb[tb]], w=[RBb[h * 2 + tc]])
                jobs = []
                for hl in range(4):
                    h = 4 * g + hl
                    Ob = rot_O.next()
                    ks = list(range(0, 12 + 4 * tc))
                    for n_, i in enumerate(ks):
                        m = i - (8 + 4 * tc)
                        mask = CSm(m) if m >= 0 else None

                        def sc(h=h, i=i, mask=mask):
                            return attn_tile(KnT[:, g, i * 128:(i + 1) * 128], [KnTb[g][i // 4]], Q(h), [qnTb[h][tc]], LBS(i), h * 2 + tc, mask)

                        def pvf(pj, h=h, i=i, n_=n_, Ob=Ob, last=(n_ == len(ks) - 1)):
                            pv(pj, Ob, Vaug[:, i, g, :], [Vaugb[i]], NV, n_ == 0, last)
                            if last:
                                finish(Ob, NV, h, tc, 1, oc, "last")
                        jobs.append((sc, pvf))
                run_stream(jobs)
                for hl in range(4):
                    h = 4 * g + hl
                    tb2 = rot_tr.next()
                    tv = bank[tb2][:, :].bitcast(BF16)
                    for j in range(4):
                        op(PE, lambda: nc.tensor.transpose(out=tv[:, j * 128:(j + 1) * 128], in_=ocb[oc][:, j, hl, :], identity=ident), r=[ocbb[oc], cidb], w=[bankb[tb2]])
                    copy_op(evac_engine(), catA[:, h, tc * 512:(tc + 1) * 512], tv[:, 0:512], r=[bankb[tb2]], w=[catTb[h][tc]])
        self.tap("catT_nsa", catA[:, :, :], [catTb[c][t] for c in range(8) for t in range(2)], [128, 8, TQ], BF16)
        if self.stop == "p2":
            return self.fin()
        AT.close()
        self.free(RBb + [bonusb, impb, sc2b, m8b, selbfb] + Ptb + ocombb + ocbb + rdb)
        A.close()
        self.free([b for l in KnTb for b in l] + Vaugb + [b for l in qnTb for b in l] + [gatesb, kcmpb, vcaugb, cbb])
        catB = self.tile(CT, "catB", [128, 8, TQ], BF16)

        self.mark("conv")
        CV = self.scope()
        dg = [self.tile(CV, f"dg{j}", [128, 31, 128], BF16) for j in range(2)]
        dgb = [self.buf(f"dg{j}") for j in range(2)]
        cv = self.tile(CV, "cv", [128, 8, TQ], F32)
        cvb_ = [[self.buf(f"cv{c}_{t}") for t in range(2)] for c in range(8)]
        cvs = [self.tile(CV, f"cvs{j}", [128, 2, 512], BF16) for j in range(2)]
        cvsb = [self.buf(f"cvs{j}") for j in range(2)]
        rot_cvs = Rot([0, 1])
        stat_bank = {(0, 0): 4, (0, 1): 5, (1, 0): 6, (1, 1): 7}
        mean = self.tile(CV, "mean", [128, 512], F32)
        meanb = self.buf("mean")
        rstd = self.tile(CV, "rstd", [128, 512], F32)
        rstdb = self.buf("rstd")
        t1 = [self.tile(CV, f"t1_{j}", [128, 512], F32) for j in range(2)]
        t1b = [self.buf(f"t1_{j}") for j in range(2)]
        ident_b = ident.unsqueeze(1).broadcast_to([128, 31, 128])

        def ln_prep(tc):
            sb0, sb1 = stat_bank[(tc, 0)], stat_bank[(tc, 1)]
            op(DVE, lambda: nc.vector.tensor_scalar(out=mean[:, :], in0=bank[sb0][:, :], scalar1=1.0 / 1024, scalar2=None, op0=ALU.mult), r=[bankb[sb0]], w=[meanb])
            op(DVE, lambda: nc.vector.tensor_tensor(out=rstd[:, :], in0=mean[:, :], in1=mean[:, :], op=ALU.mult), r=[meanb], w=[rstdb])
            op(DVE, lambda: nc.vector.scalar_tensor_tensor(out=rstd[:, :], in0=bank[sb1][:, :], scalar=1.0 / 1024, in1=rstd[:, :], op0=ALU.mult, op1=ALU.subtract), r=[bankb[sb1]], w=[rstdb])
            op(ACT, lambda: nc.scalar.activation(out=rstd[:, :], in_=rstd[:, :], func=AF.Sqrt, bias=eps[:, 1:2]), r=[epsb], w=[rstdb])
            op(DVE, lambda: nc.vector.reciprocal(out=rstd[:, :], in_=rstd[:, :]), w=[rstdb])

        def ln_chunk(tc, ch):
            j = ch % 2
            op(DVE, lambda: nc.vector.tensor_tensor(out=t1[j][:, :], in0=cv[:, ch, tc * 512:(tc + 1) * 512], in1=mean[:, :], op=ALU.subtract), r=[cvb_[ch][tc], meanb], w=[t1b[j]])
            op(DVE, lambda: nc.vector.tensor_tensor(out=t1[j][:, :], in0=t1[j][:, :], in1=rstd[:, :], op=ALU.mult), r=[rstdb], w=[t1b[j]])
            op(ACT, lambda: nc.scalar.activation(out=catB[:, ch, tc * 512:(tc + 1) * 512], in_=t1[j][:, :], func=AF.Silu, scale=pvec[:, PV_LNG + ch:PV_LNG + ch + 1], bias=pvec[:, PV_LNB + ch:PV_LNB + ch + 1]), r=[t1b[j], pvb], w=[catTb[8 + ch][tc]])

        pend = []

        def build_dg(n):
            ch_ = n % 8
            wv = pvec[:, PV_CW + ch_ * 31:PV_CW + ch_ * 31 + 31].unsqueeze(2).broadcast_to([128, 31, 128])
            op(DVE, lambda: nc.vector.tensor_tensor(out=dg[n % 2][:, :, :], in0=ident_b, in1=wv, op=ALU.mult), r=[cidb, pvb], w=[dgb[n % 2]])

        it_ = 0
        build_dg(0)
        for tc in range(2):
            for ch in range(8):
                dj = it_ % 2
                if it_ + 1 < 16:
                    build_dg(it_ + 1)
                it_ += 1
                psi = rot_proj.next()
                for j in range(31):
                    o = 2 + j + tc * 512
                    op(PE, lambda: nc.tensor.matmul(bank[psi][:, :], lhsT=dg[dj][:, j, :], rhs=hglu[:, ch, o:o + 512], start=(j == 0), stop=(j == 30)), r=[dgb[dj], hglub[ch]], w=[bankb[psi]])
                while pend:
                    pend.pop(0)()
                op(ACT, lambda: nc.scalar.activation(out=cv[:, ch, tc * 512:(tc + 1) * 512], in_=bank[psi][:, :], func=AF.Identity, bias=pvec[:, PV_CB + ch:PV_CB + ch + 1]), r=[bankb[psi], pvb], w=[cvb_[ch][tc]])
                sj = rot_cvs.next()
                op(DVE, lambda: nc.vector.tensor_copy(out=cvs[sj][:, 0, :], in_=cv[:, ch, tc * 512:(tc + 1) * 512]), r=[cvb_[ch][tc]], w=[cvsb[sj]])
                op(ACT, lambda: nc.scalar.activation(out=cvs[sj][:, 1, :], in_=cv[:, ch, tc * 512:(tc + 1) * 512], func=AF.Square), r=[cvb_[ch][tc]], pw=[cvsb[sj]])

                def stats(tc=tc, ch=ch, sj=sj):
                    for kind in range(2):
                        sb_ = stat_bank[(tc, kind)]
                        op(PE, lambda: nc.tensor.matmul(bank[sb_][:, :], lhsT=ones, rhs=cvs[sj][:, kind, :], start=(ch == 0), stop=(ch == 7)), r=[cvsb[sj], cidb], w=[bankb[sb_]])
                pend.append(stats)
                if tc == 1:
                    ln_chunk(0, ch)
            while pend:
                pend.pop(0)()
            if tc == 0:
                ln_prep(0)
        ln_prep(1)
        for ch in range(8):
            ln_chunk(1, ch)
        self.tap("cv", cv[:, :, :], [b for l in cvb_ for b in l], [128, 8, TQ], F32)
        self.tap("catB", catB[:, :, :], [catTb[c][t] for c in range(8, 16) for t in range(2)], [128, 8, TQ], BF16)
        if self.stop == "p3":
            return self.fin()
        CV.close()
        self.free(dgb + [b for l in cvb_ for b in l] + cvsb + [meanb, rstdb] + t1b)
        HG.close()
        self.free(hglub)

        self.mark("memkv")
        MK = self.scope()
        hmT = self.tile(MK, "hmT", [128, 16, 256], BF16)
        hmTb = self.buf("hmT")
        kmnT = self.tile(MK, "kmnT", [128, 4, 256], BF16)
        kmnTb = [self.buf(f"kmnT{h}") for h in range(4)]
        vmaug = self.tile(MK, "vmaug", [128, 2, 4, NV], BF16)
        vmaugb = self.buf("vmaug")
        S1m = self.scope()
        mt_ = [self.tile(S1m, f"memt{j}", [128, D], F32) for j in range(2)]
        mtb = [self.buf(f"memt{j}") for j in range(2)]
        for j in range(2):
            dma(SP, mt_[j][:, :], self.memb[j * 128:(j + 1) * 128, :], w=[mtb[j]])
        rmsnorm_T(S1m, lambda i: (mt_[i][:, :], [mtb[i]]), 2, PV_NMKV, hmT, lambda i: hmTb, lambda i: i * 128)
        S1m.close()
        self.free(mtb)
        op(POOL, lambda: nc.gpsimd.memset(vmaug[:, :, :, :], 1.0), w=[vmaugb])
        for blk in range(2):
            wt, wb = wget()
            for cc in range(2):
                h = blk * 2 + cc
                psi = proj_fm(wt, wb, cc, hmT, [hmTb], 0, 256)
                pnorm(psi, 256, pvec[:, PV_MKN:PV_MKN + 1], [pvb], kmnT[:, h, :], kmnTb[h])
            flush(0)
            wrel()
        for blk in range(2):
            wt, wb = wget()
            for mt in range(2):
                psi = proj_tm(wt, wb, hmT, [hmTb], mt * 128, 256)
                src = bank[psi][:, 0:256].rearrange("p (g d) -> p g d", g=2)
                copy_op(evac_engine(), vmaug[:, mt, blk * 2:blk * 2 + 2, 0:128], src, r=[bankb[psi]], pw=[vmaugb])
            wrel()

        self.mark("w_out")
        xres = self.tile(G, "xres", [128, 8, D], F32)
        xrb = [[self.buf(f"xres{t}_{c}") for c in range(8)] for t in range(8)]
        for tt in range(8):
            dma(SP, xres[:, tt, :], self.xc[TQ + tt * 128:TQ + (tt + 1) * 128, :], w=xrb[tt], owner=xrb[tt][0])
        for cb in range(8):
            wt, wb = wget()
            for tt in range(8):
                psi = rot_proj.next()
                for kc in range(16):
                    cat_ = catA if kc < 8 else catB
                    op(PE, lambda: nc.tensor.matmul(bank[psi][:, 0:256], lhsT=cat_[:, kc % 8, tt * 128:(tt + 1) * 128], rhs=wt[:, kc, 0:256], start=(kc == 0), stop=(kc == 15)), r=[wb, catTb[kc][tt // 4]], w=[bankb[psi]])
                dst = xres[:, tt, cb * 256:(cb + 1) * 256]
                op(DVE, lambda: nc.vector.tensor_tensor(out=dst, in0=bank[psi][:, 0:256], in1=dst, op=ALU.add), r=[bankb[psi]], w=[xrb[tt][cb]])
            wrel()
        self.tap("x1", xres[:, :, :], [b for l in xrb for b in l], [128, 8, D], F32)
        if self.stop == "p4":
            return self.fin()
        CT.close()
        self.free([b for l in catTb for b in l])

        self.mark("mem")
        M = self.scope()
        h2T = self.tile(M, "h2T", [128, 16, TQ], BF16)
        h2Tb = [self.buf(f"h2T{c}") for c in range(2)]
        S1 = self.scope()
        rmsnorm_T(S1, lambda i: (xres[:, i, :], xrb[i]), 8, PV_NMQ, h2T, lambda i: h2Tb[i // 4], lambda i: i * 128)
        S1.close()
        qmnT = self.tile(M, "qmnT", [128, 4, TQ], BF16)
        qmnTb = [[self.buf(f"qmnT{h}_{t}") for t in range(2)] for h in range(4)]
        omT = self.tile(M, "omT", [128, 4, TQ], BF16)
        omTb = [[self.buf(f"omT{h}_{t}") for t in range(2)] for h in range(4)]
        for blk in range(2):
            wt, wb = wget()
            for cc in range(2):
                h = blk * 2 + cc
                for tc in range(2):
                    psi = proj_fm(wt, wb, cc, h2T, [h2Tb[tc]], tc * 512, 512)
                    pnorm(psi, 512, eps[:, 3:4], [epsb], qmnT[:, h, tc * 512:(tc + 1) * 512], qmnTb[h][tc])
            flush(0)
            wrel()
        Pm = [self.tile(M, f"Pm{j}", [128, 512], BF16) for j in range(3)]
        Pmb = [self.buf(f"Pm{j}") for j in range(3)]
        omem = [self.tile(M, f"omem{j}", [128, 4, 128], BF16) for j in range(2)]
        omemb = [self.buf(f"omem{j}") for j in range(2)]
        rdm = [self.tile(M, f"rdm{j}", [128, 4], F32) for j in range(2)]
        rdmb = [self.buf(f"rdm{j}") for j in range(2)]
        jobs = []
        it = 0
        for h in range(4):
            for tc in range(2):
                Ob = rot_O.next()
                oj = it % 2
                it += 1
                for mt in range(2):
                    def sc(h=h, tc=tc, mt=mt):
                        si = rot_S.next()
                        op(PE, lambda: nc.tensor.matmul(bank[si][:, :], lhsT=kmnT[:, h, mt * 128:(mt + 1) * 128], rhs=qmnT[:, h, tc * 512:(tc + 1) * 512], start=True, stop=True), r=[kmnTb[h], qmnTb[h][tc]], w=[bankb[si]])
                        pj = rot_P.next()
                        op(ACT, lambda: nc.scalar.activation(out=Pm[pj][:, :], in_=bank[si][:, :], func=AF.Exp), r=[bankb[si]], w=[Pmb[pj]])
                        return pj

                    def pvf(pj, h=h, tc=tc, mt=mt, Ob=Ob, oj=oj):
                        for j in range(4):
                            ob = Ob[j // 2]
                            o0 = (j % 2) * NV
                            op(PE, lambda: nc.tensor.matmul(bank[ob][:, o0:o0 + NV], lhsT=Pm[pj][:, j * 128:(j + 1) * 128], rhs=vmaug[:, mt, h, :], start=(mt == 0 and j % 2 == 0), stop=(mt == 1), skip_group_check=True), r=[Pmb[pj], vmaugb], w=[bankb[ob]])
                        if mt == 0:
                            return
                        for b2 in range(2):
                            ob = Ob[b2]
                            op(DVE, lambda: nc.vector.tensor_scalar(out=rdm[oj][:, 2 * b2:2 * b2 + 2], in0=bank[ob][:, 128:128 + NV + 1:NV], scalar1=1e-30, scalar2=None, op0=ALU.max), r=[bankb[ob]], pw=[rdmb[oj]] if b2 else [], w=[rdmb[oj]] if not b2 else [])
                        op(DVE, lambda: nc.vector.reciprocal(out=rdm[oj][:, :], in_=rdm[oj][:, :]), w=[rdmb[oj]])
                        for j in range(4):
                            ob = Ob[j // 2]
                            o0 = (j % 2) * NV
                            op(DVE, lambda: nc.vector.tensor_scalar(out=omem[oj][:, j, :], in0=bank[ob][:, o0:o0 + 128], scalar1=rdm[oj][:, j:j + 1], scalar2=None, op0=ALU.mult), r=[bankb[ob], rdmb[oj]], w=[omemb[oj]] if j == 0 else [], pw=[omemb[oj]] if j else [])
                        tb2 = rot_tr.next()
                        tv = bank[tb2][:, :].bitcast(BF16)
                        for j in range(4):
                            op(PE, lambda: nc.tensor.transpose(out=tv[:, j * 128:(j + 1) * 128], in_=omem[oj][:, j, :], identity=ident), r=[omemb[oj], cidb], w=[bankb[tb2]])
                        copy_op(evac_engine(), omT[:, h, tc * 512:(tc + 1) * 512], tv[:, 0:512], r=[bankb[tb2]], w=[omTb[h][tc]])
                    jobs.append((sc, pvf))
        run_stream(jobs)
        wmo = [wget(), wget()]
        for tt in range(8):
            for cb4 in range(4):
                psi = rot_proj.next()
                for kc in range(4):
                    wt, wb = wmo[kc // 2]
                    op(PE, lambda: nc.tensor.matmul(bank[psi][:, :], lhsT=omT[:, kc, tt * 128:(tt + 1) * 128], rhs=wt[:, kc % 2, cb4 * 512:(cb4 + 1) * 512], start=(kc == 0), stop=(kc == 3)), r=[wb, omTb[kc][tt // 4]], w=[bankb[psi]])
                dst = xres[:, tt, cb4 * 512:(cb4 + 1) * 512]
                op(DVE, lambda: nc.vector.tensor_tensor(out=dst, in0=bank[psi][:, :], in1=dst, op=ALU.add), r=[bankb[psi]], w=[xrb[tt][2 * cb4], xrb[tt][2 * cb4 + 1]])
        wrel()
        self.tap("x2", xres[:, :, :], [b for l in xrb for b in l], [128, 8, D], F32)
        if self.stop == "p5":
            return self.fin()
        M.close()
        self.free(h2Tb + [b for l in qmnTb for b in l] + [b for l in omTb for b in l] + Pmb + omemb + rdmb)
        MK.close()
        self.free([hmTb, vmaugb] + kmnTb)

        self.mark("ffn")
        Fz = self.scope()
        h3T = self.tile(Fz, "h3T", [128, 16, TQ], BF16)
        h3Tb = [self.buf(f"h3T{c}") for c in range(2)]
        S3 = self.scope()
        rmsnorm_T(S3, lambda i: (xres[:, i, :], xrb[i]), 8, PV_NFFN, h3T, lambda i: h3Tb[i // 4], lambda i: i * 128)
        S3.close()
        actT = [self.tile(Fz, f"actT{j}", [128, 2, TQ], BF16) for j in range(2)]
        actTb = [[[self.buf(f"actT{j}_{c}_{t}") for t in range(2)] for c in range(2)] for j in range(2)]
        sgt = [self.tile(Fz, f"sgt{j}", [128, 512], BF16) for j in range(2)]
        sgtb = [self.buf(f"sgt{j}") for j in range(2)]
        rot_g = Rot([4, 5])
        rot_u = Rot([6, 7])
        rot_sgt = Rot([0, 1])
        dheld = []
        for jb in range(22):
            wgt, wgb_ = wget()
            wut, wub = wget()
            aj = jb % 2
            for cc in range(2):
                for tc in range(2):
                    pg = rot_g.next()
                    pu = rot_u.next()
                    for kc in range(16):
                        op(PE, lambda: nc.tensor.matmul(bank[pg][:, :], lhsT=wgt[:, kc, cc * 128:(cc + 1) * 128], rhs=h3T[:, kc, tc * 512:(tc + 1) * 512], start=(kc == 0), stop=(kc == 15)), r=[wgb_, h3Tb[tc]], w=[bankb[pg]])
                    for kc in range(16):
                        op(PE, lambda: nc.tensor.matmul(bank[pu][:, :], lhsT=wut[:, kc, cc * 128:(cc + 1) * 128], rhs=h3T[:, kc, tc * 512:(tc + 1) * 512], start=(kc == 0), stop=(kc == 15)), r=[wub, h3Tb[tc]], w=[bankb[pu]])
                    sj = rot_sgt.next()
                    op(ACT, lambda: nc.scalar.activation(out=sgt[sj][:, :], in_=bank[pg][:, :], func=AF.Silu), r=[bankb[pg]], w=[sgtb[sj]])
                    op(DVE, lambda: nc.vector.tensor_tensor(out=actT[aj][:, cc, tc * 512:(tc + 1) * 512], in0=bank[pu][:, :], in1=sgt[sj][:, :], op=ALU.mult), r=[bankb[pu], sgtb[sj]], w=[actTb[aj][cc][tc]])
            wrel(2, newest=True)
            wdt, wdb = wget()
            dheld.append((aj, wdt, wdb))
            if jb % 2 == 0:
                continue
            for tt in range(8):
                for cb4 in range(4):
                    psi = rot_proj.next()
                    n_ = 0
                    for (a_, wd_, wdb_) in dheld:
                        for cc in range(2):
                            op(PE, lambda: nc.tensor.matmul(bank[psi][:, :], lhsT=actT[a_][:, cc, tt * 128:(tt + 1) * 128], rhs=wd_[:, cc, cb4 * 512:(cb4 + 1) * 512], start=(n_ == 0), stop=(n_ == 3)), r=[wdb_, actTb[a_][cc][tt // 4]], w=[bankb[psi]])
                            n_ += 1
                    dst = xres[:, tt, cb4 * 512:(cb4 + 1) * 512]
                    op(DVE, lambda: nc.vector.tensor_tensor(out=dst, in0=bank[psi][:, :], in1=dst, op=ALU.add), r=[bankb[psi]], w=[xrb[tt][2 * cb4], xrb[tt][2 * cb4 + 1]])
            dheld = []
            wrel()
        self.mark("out")
        for tt in range(8):
            ob_ = self.buf(f"out{tt}")
            dma(SP, self.y[tt * 128:(tt + 1) * 128, :], xres[:, tt, :], r=xrb[tt], owner=ob_)
            self.outtoks.append((ob_.dsem, ob_.dcnt))
        for sem, val in self.outtoks:
            SP.eng.wait_ge(sem.h, val)
        return nc


def _bf(a):
    return np.ascontiguousarray(a.astype(np.float32)).astype(NPBF)


def _consts(s):
    first_real = 0 if s == 1 else 1024
    k = np.arange(TC)
    lbs = np.zeros((128, TC), np.float32)
    lbs[k // 64, k] = BIG
    lbs[32, :] = 16 * (k // 16)
    lbs[33, :] = k % 16
    lbs[34, :] = 1.0
    lbs[35, :] = 1.0
    lbs[36, :] = -BIG
    lbs[37, :] = np.where(k < first_real, -BIG, 0.0)
    lbw = lbs.copy()
    lbw[0:32, :] = 0.0
    lbw[36, :] = 0.0
    c = np.arange(128)
    lbc = np.zeros((128, 128), np.float32)
    lbc[32, :] = 16 * c
    lbc[33, :] = 15.5
    lbc[34, :] = 1.0
    lbc[35, :] = 1.0
    kk = np.arange(128)[:, None]
    x = np.arange(896)[None, :]
    cs = np.where((x - 384) < kk, -BIG, 0.0)
    ws = np.where((x - 384) >= kk, -BIG, 0.0)
    t_ctx = 1024 + np.arange(TQ)[None, :]
    cc = np.arange(128)[:, None]
    cm = np.where((16 * cc + 31 > t_ctx) | (cc >= 127) | (16 * cc < first_real), -BIG, 0.0)
    cs_ = np.arange(128)[:, None] * 16
    bs_ = np.arange(32)[None, :] * 64
    ov = np.clip(np.minimum(cs_ + 32, bs_ + 64) - np.maximum(cs_, bs_), 0, None).astype(np.float32) / 32.0
    ovaug = np.concatenate([np.ones((128, 1), np.float32), ov], axis=1)
    ovaug[127, :] = 0.0
    ident = np.eye(128, dtype=np.float32)
    ones = np.ones((128, 128), np.float32)
    cbf = np.concatenate([lbs, lbw, lbc, cs, ws, cm, ovaug, ident, ones], axis=1)
    assert cbf.shape[1] == NCB
    rb = np.zeros((128, 16, 512), np.float32)
    slopes = 2.0 ** (-(np.arange(NH) + 1.0))
    for h in range(NH):
        for tc in range(2):
            t = 1024 + tc * 512 + np.arange(512)
            i = h * 2 + tc
            rb[0:32, i, :] = 1.0
            rb[32, i, :] = slopes[h]
            rb[33, i, :] = slopes[h]
            rb[34, i, :] = -slopes[h] * (16 * (t // 16))
            rb[35, i, :] = -slopes[h] * (t % 16)
            rb[36, i, :] = 1.0
            rb[37, i, :] = 1.0
    tq = 1024 + np.arange(TQ)[:, None]
    blk = np.arange(32)[None, :]
    cur = tq // 64
    valid = (blk * 64 <= tq) & (blk * 64 >= first_real)
    forced = (blk == first_real // 64) | (blk == cur) | (blk == cur - 1)
    bon = np.where(valid, 1000.0 * forced, -1e30).astype(np.float32)
    bon = bon.reshape(8, 128, 32).transpose(1, 0, 2).reshape(128, 256)
    return _bf(cbf), _bf(rb.reshape(128, 16 * 512)), np.ascontiguousarray(bon)


def _pvec(I):
    pv = np.zeros((128, NPV), np.float32)
    f = lambda a: np.asarray(a, np.float32)
    pv[:, PV_NMIX:PV_NMIX + 16] = f(I["norm_mix"])[0].reshape(16, 128).T
    pv[:, PV_NMQ:PV_NMQ + 16] = f(I["norm_mem_q"])[0].reshape(16, 128).T
    pv[:, PV_NMKV:PV_NMKV + 16] = f(I["norm_mem_kv"])[0].reshape(16, 128).T
    pv[:, PV_NFFN:PV_NFFN + 16] = f(I["norm_ffn"])[0].reshape(16, 128).T
    pv[:, PV_QN] = f(I["q_norm"])[0]
    pv[:, PV_KNC] = f(I["k_norm_cmp"])[0]
    pv[:, PV_KNS] = f(I["k_norm_slc"])[0]
    pv[:, PV_KNW] = f(I["k_norm_win"])[0]
    pv[:, PV_MQN] = f(I["mq_norm"])[0]
    pv[:, PV_MKN] = f(I["mk_norm"])[0]
    pv[:, PV_CB:PV_CB + 8] = f(I["conv_b"])[0].reshape(8, 128).T
    pv[:, PV_LNG:PV_LNG + 8] = f(I["conv_ln_g"])[0].reshape(8, 128).T
    pv[:, PV_LNB:PV_LNB + 8] = f(I["conv_ln_b"])[0].reshape(8, 128).T
    cw = f(I["conv_w"])[0]
    pv[:, PV_CW:PV_CW + 248] = cw.reshape(31, 8, 128).transpose(2, 1, 0).reshape(128, 248)
    pv[:, PV_GB:PV_GB + 24] = f(I["gate_b"])[0][None, :]
    pv[:, PV_POSK:PV_POSK + 32] = f(I["cmp_pos_k"])[0].T
    pv[:, PV_POSV:PV_POSV + 32] = f(I["cmp_pos_v"])[0].T
    return pv


def make_in_maps(I):
    f = lambda a: np.ascontiguousarray(np.asarray(a, np.float32))
    x = f(I["x"])
    mem = f(I["mem"])
    w_in = f(I["w_in"])[0]

    def blk(W, cols):
        out = np.empty((len(cols), 128, 16, 256), np.float32)
        for i, c0 in enumerate(cols):
            out[i] = W[:, c0:c0 + 256].reshape(16, 128, 256).transpose(1, 0, 2)
        return out.reshape(len(cols) * 128, 4096)

    in_cols = [C_KC, C_VC, C_KSL, C_KW, C_VSL, C_VW] + [C_Q + 256 * i for i in range(4)]
    for j in range(4):
        in_cols += [C_U + 256 * j, C_U + 1024 + 256 * j]
    w1b = lambda W: np.ascontiguousarray(f(W)[0].reshape(32, 128, 128).transpose(1, 0, 2).reshape(128, 4096))
    shared = {
        "w_in_b": blk(w_in, in_cols), "w_g": np.ascontiguousarray(w_in[:, C_G:C_G + 24]),
        "w_out_b": blk(f(I["w_out"])[0], [256 * i for i in range(8)]),
        "w_mq_b": blk(f(I["w_mq"])[0], [0, 256]), "w_mk_b": blk(f(I["w_mk"])[0], [0, 256]), "w_mv_b": blk(f(I["w_mv"])[0], [0, 256]),
        "w_mo": f(I["w_mo"])[0],
        "w_gate_b": blk(f(I["w_gate"])[0], [256 * i for i in range(22)]), "w_up_b": blk(f(I["w_up"])[0], [256 * i for i in range(22)]),
        "w_down": f(I["w_down"])[0],
        "w1k_b": w1b(I["cmp_k_w1"]), "w1v_b": w1b(I["cmp_v_w1"]), "w2k": f(I["cmp_k_w2"])[0], "w2v": f(I["cmp_v_w2"])[0],
        "pvec": _pvec(I),
    }
    cs = [_consts(0), _consts(1)]
    maps = []
    for core in range(8):
        b, s = core // 2, core % 2
        if s == 1:
            xc = x[b]
        else:
            xc = np.concatenate([np.zeros((TQ, D), np.float32), x[b, :TQ]], axis=0)
        m = dict(shared)
        m["xc"] = np.ascontiguousarray(xc)
        m["memb"] = mem[b]
        m["cbf"], m["rb"], m["bonus"] = cs[s]
        maps.append(m)
    return maps


_CACHE = {}


def kernel(**inputs):
    if "k" not in _CACHE:
        K = Kern()
        K.build()
        _CACHE["k"] = K
    K = _CACHE["k"]
    maps = make_in_maps(inputs)
    res = run_bass_kernel_spmd(K.nc, maps, core_ids=list(range(8)))
    out = np.zeros((4, 2048, D), np.float32)
    for core in range(8):
        b, s = core // 2, core % 2
        out[b, s * TQ:(s + 1) * TQ] = np.asarray(res.results[core]["y"], np.float32).reshape(TQ, D)
    return out
```

```python
import numpy as np
import ml_dtypes
from contextlib import ExitStack
import concourse.bass as bass
import concourse.mybir as mybir
from concourse.bass_utils import run_bass_kernel_spmd

F32 = mybir.dt.float32
BF16 = mybir.dt.bfloat16
AF = mybir.ActivationFunctionType
ALU = mybir.AluOpType
NPBF = ml_dtypes.bfloat16

D = 2048
TQ = 1024
TC = 2048
NH = 8
DH = 128
IN_W = 4632
FFN = 5632
BIG = 32768.0
NV = 129
NVC = 161

C_Q, C_KC, C_VC, C_KSL, C_VSL, C_KW, C_VW, C_G, C_U = 0, 1024, 1280, 1536, 1792, 2048, 2304, 2560, 2584

PV_NMIX, PV_NMQ, PV_NMKV, PV_NFFN = 0, 16, 32, 48
PV_QN, PV_KNC, PV_KNS, PV_KNW, PV_MQN, PV_MKN = 64, 65, 66, 67, 68, 69
PV_CB, PV_LNG, PV_LNB, PV_CW = 70, 78, 86, 94
PV_GB = 94 + 248
PV_POSK = PV_GB + 24
PV_POSV = PV_POSK + 32
NPV = PV_POSV + 32

CB_LBS, CB_LBW, CB_LBC, CB_CS, CB_WS, CB_CM, CB_OV, CB_ID, CB_ONE = 0, 2048, 4096, 4224, 5120, 6016, 7040, 7073, 7201
NCB = 7329


class Sem:
    n = 0

    def __init__(self, h):
        self.h = h
        Sem.n += 1
        self.id = Sem.n


class Tok:
    __slots__ = ("sem", "val", "eng")

    def __init__(self, sem, val, eng):
        self.sem, self.val, self.eng = sem, val, eng


def _add(d, tok):
    cur = d.get(tok.sem.id)
    if cur is None or cur.val < tok.val:
        d[tok.sem.id] = tok


class Buf:
    def __init__(self, name, fence):
        self.name = name
        self.w = {}
        self.wf = {}
        self.r = dict(fence)
        self.dsem = None
        self.dcnt = 0
        self.excl = False


class Eng:
    EPOCH = 12000

    def __init__(self, K, eng, name, is_pe=False):
        self.K, self.eng, self.name, self.is_pe = K, eng, name, is_pe
        self.seen = {}
        self.ep = 0
        self.nwait = 0
        self.nins = 0
        self._new()

    def _new(self):
        self.sem = self.K.new_sem(f"e{self.name}{self.ep}")
        self.ep += 1
        self.cnt = 0

    def wait(self, tok):
        if tok is None:
            return
        if self.is_pe and tok.eng is self:
            return
        if self.seen.get(tok.sem.id, 0) >= tok.val:
            return
        self.eng.wait_ge(tok.sem.h, tok.val)
        self.nwait += 1
        self.seen[tok.sem.id] = tok.val

    def issue(self, ins):
        if self.cnt >= self.EPOCH:
            self._new()
        ins.then_inc(self.sem.h, 1)
        self.cnt += 1
        self.nins += 1
        return Tok(self.sem, self.cnt, self)


class Scope:
    def __init__(self, K):
        self.K = K
        self.blocks = []

    def close(self):
        for off, n in self.blocks:
            self.K.arena_release(off, n)
        self.blocks = []


class Rot:
    def __init__(self, items):
        self.items = list(items)
        self.i = 0

    def next(self):
        it = self.items[self.i % len(self.items)]
        self.i += 1
        return it


class Kern:
    def __init__(self, taps=(), stop=None):
        self.stop = stop
        self.taps = list(taps)
        self.tap_tensors = {}
        self.nc = bass.Bass("TRN2", target_bir_lowering=False)
        self.nsem = 0
        self.fence = {}
        self.es = ExitStack()

    def new_sem(self, name):
        self.nsem += 1
        return Sem(self.nc.alloc_semaphore(name=f"s{self.nsem}_{name}"))

    def buf(self, name):
        return Buf(name, self.fence)

    def free(self, bufs):
        for b in bufs:
            for t in b.w.values():
                _add(self.fence, t)
            for t in b.r.values():
                _add(self.fence, t)

    ARENA_BYTES = 198 * 1024

    def arena_init(self):
        self.arena = self.es.enter_context(self.nc.sbuf_tensor("arena", [128, self.ARENA_BYTES // 2], BF16))
        self.afree = [(0, self.ARENA_BYTES)]
        self.apeak = 0

    def arena_release(self, off, n):
        self.afree.append((off, n))
        self.afree.sort()
        m = []
        for o, l in self.afree:
            if m and m[-1][0] + m[-1][1] == o:
                m[-1] = (m[-1][0], m[-1][1] + l)
            else:
                m.append((o, l))
        self.afree = m

    def scope(self):
        return Scope(self)

    def tile(self, S, name, shape, dt):
        esz = 4 if dt == F32 else 2
        nel = int(np.prod(shape[1:]))
        nb = (nel * esz + 63) // 64 * 64
        for i, (o, l) in enumerate(self.afree):
            if l >= nb:
                self.afree[i] = (o + nb, l - nb)
                if l == nb:
                    self.afree.pop(i)
                break
        else:
            raise RuntimeError(f"arena OOM allocating {name} {shape} ({nb}B); free={self.afree}")
        S.blocks.append((o, nb))
        self.apeak = max(self.apeak, o + nb)
        v = self.arena[:, o // 2:o // 2 + nel * esz // 2]
        if dt == F32:
            v = v.bitcast(F32)
        if len(shape) == 3:
            v = v.rearrange("p (a b) -> p a b", a=shape[1])
        elif len(shape) == 4:
            v = v.rearrange("p (a b c) -> p a b c", a=shape[1], b=shape[2])
        return v

    def _waits(self, E, r, w, pw):
        for b in r:
            for t in b.w.values():
                E.wait(t)
            if b.excl:
                for t in b.r.values():
                    if t.eng is not E:
                        E.wait(t)
        for b in w:
            for t in b.w.values():
                E.wait(t)
            for t in b.r.values():
                E.wait(t)
        for b in pw:
            for t in b.wf.values():
                E.wait(t)
            for t in b.r.values():
                E.wait(t)

    def _upd(self, tok, r, w, pw):
        for b in w:
            b.w = {tok.sem.id: tok}
            b.wf = {tok.sem.id: tok}
            b.r = {}
        for b in pw:
            _add(b.w, tok)
        for b in r:
            if b not in w and b not in pw:
                _add(b.r, tok)

    def op(self, E, fn, r=(), w=(), pw=()):
        self._waits(E, r, w, pw)
        ins = fn()
        tok = E.issue(ins)
        self._upd(tok, r, w, pw)
        return tok

    def dma(self, Q, out, in_, r=(), w=(), pw=(), owner=None, **kw):
        self._waits(Q, r, w, pw)
        if owner is None:
            owner = (list(w) + list(pw) + list(r))[0]
        if owner.dsem is None:
            owner.dsem = self.new_sem("d" + owner.name)
        ins = Q.eng.dma_start(out=out, in_=in_, **kw)
        owner.dcnt += 16
        ins.then_inc(owner.dsem.h, 16)
        tok = Tok(owner.dsem, owner.dcnt, None)
        self._upd(tok, r, w, pw)
        return tok

    def tap(self, name, ap, bufs, shape, dt=F32):
        if name not in self.taps:
            return
        t = self.nc.dram_tensor("tap_" + name, list(shape), dt, kind="ExternalOutput").ap()
        self.tap_tensors[name] = t
        tb = self.buf("tap" + name)
        self.dma(self.SP, t, ap, r=list(bufs), owner=tb)
        self.outtoks.append((tb.dsem, tb.dcnt))

    def mark(self, name):
        self.marks.append((name, self.PE.nins, self.ACT.nins, self.DVE.nins))

    def fin(self):
        for sem, val in self.outtoks:
            self.SP.eng.wait_ge(sem.h, val)
        return self.nc

    def build(self):
        nc = self.nc
        es = self.es
        dr = lambda n, s, dt=F32: nc.dram_tensor(n, list(s), dt, kind="ExternalInput").ap()
        self.xc = dr("xc", [TC, D])
        self.memb = dr("memb", [256, D])
        self.w_in = dr("w_in", [D, IN_W])
        self.w_g = dr("w_g", [D, 24])
        self.w_out = dr("w_out", [D, D])
        self.w_mq = dr("w_mq", [D, 512])
        self.w_mk = dr("w_mk", [D, 512])
        self.w_mv = dr("w_mv", [D, 512])
        self.w_mo = dr("w_mo", [512, D])
        self.w_gate = dr("w_gate", [D, FFN])
        self.w_up = dr("w_up", [D, FFN])
        self.w_down = dr("w_down", [FFN, D])
        self.w1k = dr("w1k", [4096, 128])
        self.w1v = dr("w1v", [4096, 128])
        self.w2k = dr("w2k", [128, 128])
        self.w2v = dr("w2v", [128, 128])
        self.pvec_d = dr("pvec", [128, NPV])
        self.cbf_d = dr("cbf", [128, NCB], BF16)
        self.rb_d = dr("rb", [128, 16 * 512], BF16)
        self.bonus_d = dr("bonus", [128, 256])
        self.y = nc.dram_tensor("y", [TQ, D], F32, kind="ExternalOutput").ap()
        self.outtoks = []
        self.marks = []

        self.PE = Eng(self, nc.tensor, "pe", is_pe=True)
        self.ACT = Eng(self, nc.scalar, "act")
        self.DVE = Eng(self, nc.vector, "dve")
        self.POOL = Eng(self, nc.gpsimd, "pool")
        self.SP = Eng(self, nc.sync, "sp")
        PE, ACT, DVE, POOL, SP = self.PE, self.ACT, self.DVE, self.POOL, self.SP
        op, dma = self.op, self.dma

        self.arena_init()
        self.bank = []
        self.bankb = []
        for i in range(8):
            self.bank.append(es.enter_context(nc.psum_tensor(f"bank{i}", [128, 512], F32)))
            self.bankb.append(self.buf(f"bank{i}"))
            self.bankb[-1].excl = True
        bank, bankb = self.bank, self.bankb

        G = self.scope()
        pvec = self.tile(G, "pvec", [128, NPV], F32)
        pvb = self.buf("pvec")
        cid = self.tile(G, "cid", [128, 256], BF16)
        cidb = self.buf("cid")
        eps = self.tile(G, "eps", [128, 4], F32)
        epsb = self.buf("eps")
        dma(SP, pvec[:, :], self.pvec_d, w=[pvb])
        dma(SP, cid[:, :], self.cbf_d[:, CB_ID:CB_ID + 256], w=[cidb])
        op(DVE, lambda: nc.vector.memset(eps[:, 0:1], 1e-6), w=[epsb])
        op(DVE, lambda: nc.vector.memset(eps[:, 1:2], 1e-5), pw=[epsb])
        op(DVE, lambda: nc.vector.tensor_scalar(out=eps[:, 2:3], in0=pvec[:, PV_QN:PV_QN + 1], scalar1=DH ** -0.5, scalar2=None, op0=ALU.mult), r=[pvb], pw=[epsb])
        op(DVE, lambda: nc.vector.tensor_scalar(out=eps[:, 3:4], in0=pvec[:, PV_MQN:PV_MQN + 1], scalar1=DH ** -0.5, scalar2=None, op0=ALU.mult), r=[pvb], pw=[epsb])
        ident = cid[:, 0:128]
        ones = cid[:, 128:256]

        NSLOT = 6
        wslot = [self.tile(G, f"wslot{i}", [128, 4096], BF16) for i in range(NSLOT)]
        wslotb = [self.buf(f"wslot{i}") for i in range(NSLOT)]
        self.wq = []
        self.wnext_issue = 0

        def wcol(w, c0, n=256):
            return w[:, c0:c0 + n].rearrange("(k p) n -> p k n", p=128)

        def wrow(w, r0, n=256):
            return w[r0:r0 + n, :].rearrange("(k p) n -> p k n", p=128)

        def w1v_(w):
            return w.rearrange("(l d) o -> d l o", d=128)

        self.wreleased = set()

        def wissue():
            while self.wnext_issue < len(self.wq):
                i = self.wnext_issue
                if i >= NSLOT and (i - NSLOT) not in self.wreleased:
                    break
                src, shp = self.wq[i]
                s_ = i % NSLOT
                dst = wslot[s_][:, :].rearrange("p (k n) -> p k n", k=shp[0])
                dma(POOL, dst, src, w=[wslotb[s_]])
                self.wnext_issue += 1

        self.wcur = 0
        self.wheld = []

        def wget():
            i = self.wcur
            self.wcur += 1
            wissue()
            assert i < self.wnext_issue, "weight block not issuable: too many blocks held"
            self.wheld.append(i)
            s_ = i % NSLOT
            shp = self.wq[i][1]
            return wslot[s_][:, :].rearrange("p (k n) -> p k n", k=shp[0]), wslotb[s_]

        def wrel(n=None, newest=False):
            n = len(self.wheld) if n is None else n
            for _ in range(n):
                self.wreleased.add(self.wheld.pop(-1 if newest else 0))
            wissue()

        kvblocks = [(C_KC, "kc"), (C_VC, "vc"), (C_KSL, "ksl"), (C_KW, "kw"), (C_VSL, "vsl"), (C_VW, "vw")]
        for c0, _ in kvblocks:
            self.wq.append((wcol(self.w_in, c0), (16, 256)))
        for i in range(4):
            self.wq.append((wcol(self.w_in, C_Q + 256 * i), (16, 256)))
        for j in range(4):
            self.wq.append((wcol(self.w_in, C_U + 256 * j), (16, 256)))
            self.wq.append((wcol(self.w_in, C_U + 1024 + 256 * j), (16, 256)))
        self.wq.append((w1v_(self.w1k), (32, 128)))
        self.wq.append((w1v_(self.w1v), (32, 128)))
        for w in (self.w_mk, self.w_mv):
            for i in range(2):
                self.wq.append((wcol(w, 256 * i), (16, 256)))
        for i in range(8):
            self.wq.append((wcol(self.w_out, 256 * i), (16, 256)))
        for i in range(2):
            self.wq.append((wcol(self.w_mq, 256 * i), (16, 256)))
        for i in range(2):
            self.wq.append((wrow(self.w_mo, 256 * i), (2, 2048)))
        for j in range(22):
            self.wq.append((wcol(self.w_gate, 256 * j), (16, 256)))
            self.wq.append((wcol(self.w_up, 256 * j), (16, 256)))
            self.wq.append((wrow(self.w_down, 256 * j), (2, 2048)))

        wg = self.tile(G, "wg", [128, 16, 24], BF16)
        wgb = self.buf("wg")
        dma(POOL, wg[:, :, :], self.w_g.rearrange("(k p) n -> p k n", p=128), w=[wgb])
        w2 = self.tile(G, "w2", [128, 2, 128], BF16)
        w2b = self.buf("w2")
        dma(POOL, w2[:, 0, :], self.w2k, w=[w2b])
        dma(POOL, w2[:, 1, :], self.w2v, pw=[w2b])
        wissue()
        if self.stop == "init":
            self.tap("w0", wslot[0][:, :], [wslotb[0]], [128, 4096], BF16)
            self.tap("w4", wslot[4][:, :], [wslotb[4]], [128, 4096], BF16)
            self.tap("wg", wg[:, :, :], [wgb], [128, 16, 24], BF16)
            self.tap("pvec", pvec[:, :], [pvb], [128, NPV], F32)
            self.tap("cid", cid[:, :], [cidb], [128, 256], BF16)
            self.tap("eps", eps[:, :], [epsb], [128, 4], F32)
            return self.fin()

        rot_proj = Rot([0, 1, 2, 3])
        rot_ssq = Rot([4, 5])
        rot_tr = Rot([6, 7])
        self.flip = 0

        def evac_engine():
            self.flip ^= 1
            return ACT if self.flip else DVE

        def copy_op(E, out, in_, r, w=(), pw=()):
            if E is ACT:
                return op(ACT, lambda: nc.scalar.copy(out=out, in_=in_), r=r, w=w, pw=pw)
            return op(E, lambda: E.eng.tensor_copy(out=out, in_=in_), r=r, w=w, pw=pw)

        def norm_tile_a(xap, xbufs, xs_, xsb_, ss_, ssb_):
            op(ACT, lambda: nc.scalar.activation(out=xs_[:, :], in_=xap, func=AF.Square, accum_out=ss_[:, 0:1]), r=xbufs, w=[xsb_, ssb_])
            op(ACT, lambda: nc.scalar.activation(out=ss_[:, 1:2], in_=ss_[:, 0:1], func=AF.Sqrt, scale=1.0 / D, bias=eps[:, 0:1]), r=[epsb], w=[ssb_])
            op(DVE, lambda: nc.vector.reciprocal(out=ss_[:, 2:3], in_=ss_[:, 1:2]), w=[ssb_])
            op(DVE, lambda: nc.vector.tensor_scalar(out=xs_[:, :], in0=xap, scalar1=ss_[:, 2:3], scalar2=None, op0=ALU.mult), r=list(xbufs) + [ssb_], w=[xsb_])

        def norm_tile_b(gcol, dstT, t0, dstbuf, xs_, xsb_, tail=None):
            for half in range(2):
                bi = rot_tr.next()
                bv = bank[bi][:, :].bitcast(BF16)
                for c8 in range(8):
                    c = half * 8 + c8
                    op(PE, lambda: nc.tensor.transpose(out=bv[:, c8 * 128:(c8 + 1) * 128], in_=xs_[:, c * 128:(c + 1) * 128], identity=ident), r=[xsb_, cidb], w=[bankb[bi]])
                src = bv[:, 0:1024].rearrange("p (c t) -> p c t", c=8)
                dst = dstT[:, half * 8:(half + 1) * 8, t0:t0 + 128]
                g = pvec[:, gcol + half * 8:gcol + half * 8 + 8].unsqueeze(2).broadcast_to([128, 8, 128])
                op(DVE, lambda: nc.vector.tensor_tensor(out=dst, in0=src, in1=g, op=ALU.mult), r=[bankb[bi], pvb], pw=[dstbuf])
                if tail is not None:
                    tl, tlb = tail
                    op(POOL, lambda: nc.gpsimd.tensor_copy(out=tl[:, half * 8:(half + 1) * 8, :], in_=dstT[:, half * 8:(half + 1) * 8, t0 + 96:t0 + 128]), r=[dstbuf], pw=[tlb])

        def norm_tile(xap, xbufs, gcol, dstT, t0, dstbuf, xs_, xsb_, ss_, ssb_, tail=None):
            norm_tile_a(xap, xbufs, xs_, xsb_, ss_, ssb_)
            norm_tile_b(gcol, dstT, t0, dstbuf, xs_, xsb_, tail=tail)

        def rmsnorm_T(S, get_x, ntiles, gcol, dstT, dstbuf_of, tcol_of, tail=None):
            xs = [self.tile(S, f"xs{j}", [128, D], BF16) for j in range(2)]
            xsb = [self.buf(f"xs{j}") for j in range(2)]
            ss = [self.tile(S, f"ss{j}", [128, 4], F32) for j in range(2)]
            ssb = [self.buf(f"ss{j}") for j in range(2)]
            for i in range(ntiles):
                xap, xbufs = get_x(i)
                j = i % 2
                norm_tile(xap, xbufs, gcol, dstT, tcol_of(i), dstbuf_of(i), xs[j], xsb[j], ss[j], ssb[j], tail=tail if i == ntiles - 1 else None)
            self.free(xsb + ssb)

        P1 = self.scope()
        sqt = [self.tile(G, f"sqt{j}", [128, 512], BF16) for j in range(2)]
        sqtb = [self.buf(f"sqt{j}") for j in range(2)]
        rtt = [self.tile(G, f"rtt{j}", [128, 512], F32) for j in range(2)]
        rttb = [self.buf(f"rtt{j}") for j in range(2)]
        rot_sq = Rot([0, 1])

        def pnorm(psi, n, gain_ap, gain_bufs, dst, dstb):
            j = rot_sq.next()
            ps = bank[psi]
            op(ACT, lambda: nc.scalar.activation(out=sqt[j][:, 0:n], in_=ps[:, 0:n], func=AF.Square), r=[bankb[psi]], w=[sqtb[j]])

            def tail():
                si = rot_ssq.next()
                op(PE, lambda: nc.tensor.matmul(bank[si][:, 0:n], lhsT=ones, rhs=sqt[j][:, 0:n], start=True, stop=True), r=[sqtb[j], cidb], w=[bankb[si]])
                op(ACT, lambda: nc.scalar.activation(out=rtt[j][:, 0:n], in_=bank[si][:, 0:n], func=AF.Sqrt, scale=1.0 / DH, bias=eps[:, 0:1]), r=[bankb[si], epsb], w=[rttb[j]])
                op(DVE, lambda: nc.vector.reciprocal(out=rtt[j][:, 0:n], in_=rtt[j][:, 0:n]), w=[rttb[j]])
                op(DVE, lambda: nc.vector.scalar_tensor_tensor(out=dst, in0=ps[:, 0:n], scalar=gain_ap, in1=rtt[j][:, 0:n], op0=ALU.mult, op1=ALU.mult), r=[bankb[psi], rttb[j]] + list(gain_bufs), pw=[dstb])
            self.deferred.append(tail)

        self.deferred = []

        def flush(keep=0):
            while len(self.deferred) > keep:
                self.deferred.pop(0)()

        def proj_fm(wt, wb, cc, actT, actbufs, t0, n):
            psi = rot_proj.next()
            for kc in range(16):
                op(PE, lambda: nc.tensor.matmul(bank[psi][:, 0:n], lhsT=wt[:, kc, cc * 128:(cc + 1) * 128], rhs=actT[:, kc, t0:t0 + n], start=(kc == 0), stop=(kc == 15)), r=[wb] + list(actbufs), w=[bankb[psi]])
            flush(0)
            return psi

        def proj_tm(wt, wb, actT, actbufs, t0, ncols, kcs=16):
            psi = rot_proj.next()
            for kc in range(kcs):
                op(PE, lambda: nc.tensor.matmul(bank[psi][:, 0:ncols], lhsT=actT[:, kc, t0:t0 + 128], rhs=wt[:, kc, 0:ncols], start=(kc == 0), stop=(kc == kcs - 1)), r=[wb] + list(actbufs), w=[bankb[psi]])
            return psi

        A = self.scope()
        cbf = self.tile(A, "cbf", [128, CB_ID], BF16)
        cbb = self.buf("cbf")
        dma(SP, cbf[:, :], self.cbf_d[:, 0:CB_ID], w=[cbb])
        KnT = self.tile(A, "KnT", [128, 4, TC], BF16)
        KnTb = [[self.buf(f"KnT{i}_{c}") for c in range(4)] for i in range(4)]
        Vaug = self.tile(A, "Vaug", [128, 16, 4, NV], BF16)
        Vaugb = [self.buf(f"Vaug{i}") for i in range(16)]
        CK = self.scope()
        kcvcT = self.tile(CK, "kcvcT", [128, 4, TC], BF16)
        kcvcb = [self.buf(f"kcvc{i}") for i in range(4)]
        vab = self.buf("vaug_init")
        op(POOL, lambda: nc.gpsimd.memset(Vaug[:, :, :, :], 1.0), w=Vaugb)

        if self.stop == "p0a":
            self.tap("Vaug", Vaug[:, :, :, :], Vaugb, [128, 16, 4, NV], BF16)
            self.tap("cbf", cbf[:, :], [cbb], [128, CB_ID], BF16)
            return self.fin()
        hTs = [self.tile(P1, f"hT{c}", [128, 16, 512], BF16) for c in range(2)]
        hTb = [self.buf(f"hT{c}") for c in range(2)]
        htail = self.tile(P1, "htail", [128, 16, 32], BF16)
        htailb = self.buf("htail")

        S0 = self.scope()
        xt = [self.tile(S0, f"xt{j}", [128, D], F32) for j in range(2)]
        xtb = [self.buf(f"xt{j}") for j in range(2)]
        xs0 = [self.tile(S0, f"xs{j}", [128, D], BF16) for j in range(2)]
        xs0b = [self.buf(f"xs{j}") for j in range(2)]
        ss0 = [self.tile(S0, f"ss{j}", [128, 4], F32) for j in range(2)]
        ss0b = [self.buf(f"ss{j}") for j in range(2)]

        def xload(gt):
            dma(SP, xt[gt % 2][:, :], self.xc[gt * 128:(gt + 1) * 128, :], w=[xtb[gt % 2]])

        def norm_a(gt):
            if gt + 1 < 16:
                xload(gt + 1)
            j = gt % 2
            norm_tile_a(xt[j][:, :], [xtb[j]], xs0[j], xs0b[j], ss0[j], ss0b[j])

        def norm_b(gt):
            q_, i_ = gt // 4, gt % 4
            j = gt % 2
            norm_tile_b(PV_NMIX, hTs[q_ % 2], i_ * 128, hTb[q_ % 2], xs0[j], xs0b[j], tail=(htail, htailb) if gt == 7 else None)

        kvw = []

        def kv_block(q_, c0, kind, k_):
            hT_, hb_ = hTs[q_ % 2], hTb[q_ % 2]
            if q_ == 0:
                kvw.append(wget())
            wt, wb = kvw[k_]
            tg = q_ * 512
            if kind in ("kc", "vc", "ksl", "kw"):
                for cc in range(2):
                    psi = proj_fm(wt, wb, cc, hT_, [hb_], 0, 512)
                    if kind in ("kc", "vc"):
                        idx = (0 if kind == "kc" else 2) + cc
                        copy_op(evac_engine(), kcvcT[:, idx, tg:tg + 512], bank[psi][:, :], r=[bankb[psi]], pw=[kcvcb[idx]])
                    else:
                        idx = (0 if kind == "ksl" else 2) + cc
                        gcol = PV_KNS if kind == "ksl" else PV_KNW
                        pnorm(psi, 512, pvec[:, gcol:gcol + 1], [pvb], KnT[:, idx, tg:tg + 512], KnTb[idx][q_])
                flush(0)
            else:
                vk = 0 if kind == "vsl" else 2
                for tt in range(4):
                    psi = proj_tm(wt, wb, hT_, [hb_], tt * 128, 256)
                    kt = q_ * 4 + tt
                    E = evac_engine()
                    src = bank[psi][:, 0:256].rearrange("p (g d) -> p g d", g=2)
                    copy_op(E, Vaug[:, kt, vk:vk + 2, 0:128], src, r=[bankb[psi]], pw=[Vaugb[kt]])
            if q_ == 3:
                wrel(1)

        self.mark("p1_start")
        xload(0)
        norm_a(0)
        for gt in range(4):
            norm_b(gt)
            norm_a(gt + 1)
        for q_ in range(4):
            for k_, (c0, kind) in enumerate(kvblocks):
                kv_block(q_, c0, kind, k_)
                if q_ < 3 and k_ < 4:
                    gt = (q_ + 1) * 4 + k_
                    norm_b(gt)
                    if gt + 1 < 16 and k_ < 3:
                        norm_a(gt + 1)
            if q_ < 2:
                norm_a((q_ + 2) * 4)
            flush(0)
            if self.stop == "p0" and q_ == 0:
                return self.fin()
        S0.close()
        self.free(xtb + xs0b + ss0b)
        qnT = self.tile(A, "qnT", [128, NH, TQ], BF16)
        qnTb = [[self.buf(f"qnT{h}_{c}") for c in range(2)] for h in range(NH)]
        gates = self.tile(A, "gates", [128, 8, 24], F32)
        gatesb = self.buf("gates")
        HG = self.scope()
        hglu = self.tile(HG, "hglu", [128, 8, 1056], BF16)
        hglub = [self.buf(f"hglu{c}") for c in range(8)]

        self.mark("q_proj")
        for blk in range(4):
            wt, wb = wget()
            for cc in range(2):
                h = blk * 2 + cc
                for tc in range(2):
                    psi = proj_fm(wt, wb, cc, hTs[tc], [hTb[tc]], 0, 512)
                    pnorm(psi, 512, eps[:, 2:3], [epsb], qnT[:, h, tc * 512:(tc + 1) * 512], qnTb[h][tc])
            flush(0)
            wrel()
        gtmp = self.tile(P1, "gtmp", [128, 24], F32)
        gtmpb = self.buf("gtmp")
        for tt in range(8):
            psi = rot_proj.next()
            for kc in range(16):
                op(PE, lambda: nc.tensor.matmul(bank[psi][:, 0:24], lhsT=hTs[tt // 4][:, kc, (tt % 4) * 128:(tt % 4 + 1) * 128], rhs=wg[:, kc, :], start=(kc == 0), stop=(kc == 15)), r=[wgb, hTb[tt // 4]], w=[bankb[psi]])
            op(DVE, lambda: nc.vector.tensor_tensor(out=gtmp[:, :], in0=bank[psi][:, 0:24], in1=pvec[:, PV_GB:PV_GB + 24], op=ALU.add), r=[bankb[psi], pvb], w=[gtmpb])
            op(ACT, lambda: nc.scalar.activation(out=gates[:, tt, :], in_=gtmp[:, :], func=AF.Sigmoid), r=[gtmpb], pw=[gatesb])
        self.mark("u_proj")
        sg = [self.tile(P1, f"sg{j}", [128, 512], F32) for j in range(2)]
        sgb = [self.buf(f"sg{j}") for j in range(2)]
        rot_sg = Rot([0, 1])
        for j4 in range(4):
            wa, wab = wget()
            wbt, wbb = wget()
            for cc in range(2):
                ch = j4 * 2 + cc
                for seg in range(3):
                    if seg == 0:
                        actT, actb, t0, n, off = htail, [htailb], 0, 32, 0
                    else:
                        actT, actb, t0, n, off = hTs[seg - 1], [hTb[seg - 1]], 0, 512, 32 + (seg - 1) * 512
                    pa = proj_fm(wa, wab, cc, actT, actb, t0, n)
                    pb = proj_fm(wbt, wbb, cc, actT, actb, t0, n)
                    j = rot_sg.next()
                    op(ACT, lambda: nc.scalar.activation(out=sg[j][:, 0:n], in_=bank[pb][:, 0:n], func=AF.Sigmoid), r=[bankb[pb]], w=[sgb[j]])
                    op(DVE, lambda: nc.vector.tensor_tensor(out=hglu[:, ch, off:off + n], in0=bank[pa][:, 0:n], in1=sg[j][:, 0:n], op=ALU.mult), r=[bankb[pa], sgb[j]], pw=[hglub[ch]])
            wrel()
        self.tap("KnT", KnT[:, :, :], [b for l in KnTb for b in l], [128, 4, TC], BF16)
        self.tap("qnT", qnT[:, :, :], [b for l in qnTb for b in l], [128, NH, TQ], BF16)
        self.tap("kcvcT", kcvcT[:, :, :], kcvcb, [128, 4, TC], BF16)
        self.tap("Vaug", Vaug[:, :, :, :], Vaugb, [128, 16, 4, NV], BF16)
        self.tap("gates", gates[:, :, :], [gatesb], [128, 8, 24], F32)
        self.tap("hglu", hglu[:, :, :], hglub, [128, 8, 1056], BF16)
        P1.close()
        self.free(hTb + [htailb, gtmpb] + sgb)
        if self.stop == "p1":
            return self.fin()

        self.mark("compress")
        kcmpT = self.tile(A, "kcmpT", [128, 2, 128], BF16)
        kcmpb = self.buf("kcmpT")
        vcaug = self.tile(A, "vcaug", [128, 2, NVC], BF16)
        vcaugb = self.buf("vcaug")
        C1 = self.scope()
        posbf = self.tile(C1, "posbf", [128, 64], BF16)
        posbfb = self.buf("posbf")
        posb = self.tile(C1, "posb", [128, 2], F32)
        posbb = self.buf("posb")
        h1 = [self.tile(C1, f"h1_{j}", [128, 128], BF16) for j in range(2)]
        h1b = [self.buf(f"h1_{j}") for j in range(2)]
        op(DVE, lambda: nc.vector.memset(kcmpT[:, :, :], 0.0), w=[kcmpb])
        op(DVE, lambda: nc.vector.memset(vcaug[:, :, :], 0.0), w=[vcaugb])
        for g in range(2):
            op(DVE, lambda: nc.vector.tensor_copy(out=vcaug[:, g, 128:NVC], in_=cbf[:, CB_OV:CB_OV + 33]), r=[cbb], pw=[vcaugb])
        op(DVE, lambda: nc.vector.tensor_copy(out=posbf[:, :], in_=pvec[:, PV_POSK:PV_POSK + 64]), r=[pvb], w=[posbfb])
        rot_h1 = Rot([0, 1])
        for kind in range(2):
            w1, w1b = wget()
            pbi = rot_proj.next()
            for l in range(32):
                op(PE, lambda: nc.tensor.matmul(bank[pbi][:, 0:1], lhsT=w1[:, l, :], rhs=posbf[:, kind * 32 + l:kind * 32 + l + 1], start=(l == 0), stop=(l == 31)), r=[w1b, posbfb], w=[bankb[pbi]])
            op(DVE, lambda: nc.vector.tensor_copy(out=posb[:, kind:kind + 1], in_=bank[pbi][:, 0:1]), r=[bankb[pbi]], pw=[posbb])
            for g in range(2):
                psi = rot_proj.next()
                src = kcvcT[:, kind * 2 + g, :]
                for l in range(32):
                    op(PE, lambda: nc.tensor.matmul(bank[psi][:, 0:127], lhsT=w1[:, l, :], rhs=src[:, l:l + 16 * 126 + 1:16], start=(l == 0), stop=(l == 31)), r=[w1b, kcvcb[kind * 2 + g]], w=[bankb[psi]])
                j = rot_h1.next()
                op(ACT, lambda: nc.scalar.activation(out=h1[j][:, 0:127], in_=bank[psi][:, 0:127], func=AF.Silu, bias=posb[:, kind:kind + 1]), r=[bankb[psi], posbb], w=[h1b[j]])
                ps2 = rot_proj.next()
                if kind == 0:
                    op(PE, lambda: nc.tensor.matmul(bank[ps2][:, 0:127], lhsT=w2[:, 0, :], rhs=h1[j][:, 0:127], start=True, stop=True), r=[w2b, h1b[j]], w=[bankb[ps2]])
                    pnorm(ps2, 127, pvec[:, PV_KNC:PV_KNC + 1], [pvb], kcmpT[:, g, 0:127], kcmpb)
                    flush(0)
                else:
                    op(PE, lambda: nc.tensor.matmul(bank[ps2][0:127, 0:128], lhsT=h1[j][:, 0:127], rhs=w2[:, 1, :], start=True, stop=True), r=[w2b, h1b[j]], w=[bankb[ps2]])
                    copy_op(evac_engine(), vcaug[0:127, g, 0:128], bank[ps2][0:127, 0:128], r=[bankb[ps2]], pw=[vcaugb])
            wrel()
        self.tap("kcmpT", kcmpT[:, :, :], [kcmpb], [128, 2, 128], BF16)
        self.tap("vcaug", vcaug[:, :, :], [vcaugb], [128, 2, NVC], BF16)
        C1.close()
        self.free([posbfb, posbb] + h1b)
        if self.stop == "p1c":
            return self.fin()
        CK.close()
        self.free(kcvcb)

        self.mark("attn")
        CT = self.scope()
        catA = self.tile(CT, "catA", [128, 8, TQ], BF16)
        catTb = [[self.buf(f"catT{c}_{t}") for t in range(2)] for c in range(16)]
        AT = self.scope()
        RB = self.tile(AT, "RB", [128, 16, 512], BF16)
        RBb = [self.buf(f"RB{i}") for i in range(16)]
        dma(SP, RB[:, :, :], self.rb_d.rearrange("p (i n) -> p i n", i=16), w=RBb, owner=RBb[0])
        bonus = self.tile(AT, "bonus", [128, 8, 32], F32)
        bonusb = self.buf("bonus")
        dma(SP, bonus[:, :, :], self.bonus_d.rearrange("p (i n) -> p i n", i=8), w=[bonusb])
        Pt = [self.tile(AT, f"Pt{j}", [128, 512], BF16) for j in range(3)]
        Ptb = [self.buf(f"Pt{j}") for j in range(3)]
        rot_P = Rot([0, 1, 2])
        rot_S = Rot([0, 1, 6])
        rot_O = Rot([(2, 3), (4, 5)])
        ocomb = [self.tile(AT, f"ocomb{j}", [128, 4, 4, 128], F32) for j in range(1)]
        ocombb = [self.buf(f"ocomb{j}") for j in range(1)]
        ocb = [self.tile(AT, f"ocb{j}", [128, 4, 4, 128], BF16) for j in range(1)]
        ocbb = [self.buf(f"ocb{j}") for j in range(1)]
        imp = self.tile(AT, "imp", [128, 4, 32], F32)
        impb = self.buf("imp")
        sc2 = self.tile(AT, "sc2", [128, 4, 32], F32)
        sc2b = self.buf("sc2")
        m8 = self.tile(AT, "m8", [128, 16], F32)
        m8b = self.buf("m8")
        selbf = self.tile(AT, "selbf", [128, 4, 32], BF16)
        selbfb = self.buf("selbf")
        rdt = [self.tile(AT, f"rd{j}", [128, 8], F32) for j in range(4)]
        rdb = [self.buf(f"rd{j}") for j in range(4)]
        rot_rd = Rot([0, 1, 2, 3])

        def attn_tile(Kt, Kb, Qt, Qb, LBap, RBi, mask_ap):
            si = rot_S.next()
            n_extra = 1 if mask_ap is not None else 0
            op(PE, lambda: nc.tensor.matmul(bank[si][:, :], lhsT=Kt, rhs=Qt, start=True, stop=False), r=list(Kb) + list(Qb), w=[bankb[si]])
            op(PE, lambda: nc.tensor.matmul(bank[si][:, :], lhsT=LBap, rhs=RB[:, RBi, :], start=False, stop=(n_extra == 0)), r=[cbb, RBb[RBi]], w=[bankb[si]])
            if mask_ap is not None:
                op(PE, lambda: nc.tensor.matmul(bank[si][:, :], lhsT=ident, rhs=mask_ap, start=False, stop=True), r=[cbb, cidb], w=[bankb[si]])
            pj = rot_P.next()
            op(ACT, lambda: nc.scalar.activation(out=Pt[pj][:, :], in_=bank[si][:, :], func=AF.Exp), r=[bankb[si]], w=[Ptb[pj]])
            return pj

        def pv(pj, Obanks, Vap, Vb, nv, first, last):
            for j in range(4):
                ob = Obanks[j // 2]
                o0 = (j % 2) * nv
                op(PE, lambda: nc.tensor.matmul(bank[ob][:, o0:o0 + nv], lhsT=Pt[pj][:, j * 128:(j + 1) * 128], rhs=Vap, start=(first and j % 2 == 0), stop=last, skip_group_check=True), r=[Ptb[pj]] + list(Vb), w=[bankb[ob]])

        def finish(Obanks, nv, h, tc, br, oc, mode):
            hl = h % 4
            rj = rot_rd.next()
            rd = rdt[rj]
            for b2 in range(2):
                ob = Obanks[b2]
                op(DVE, lambda: nc.vector.tensor_scalar(out=rd[:, 2 * b2:2 * b2 + 2], in0=bank[ob][:, 128:128 + nv + 1:nv], scalar1=1e-30, scalar2=None, op0=ALU.max), r=[bankb[ob]], pw=[rdb[rj]])
            op(DVE, lambda: nc.vector.reciprocal(out=rd[:, 0:4], in_=rd[:, 0:4]), w=[rdb[rj]])
            gcol = h * 3 + br
            op(DVE, lambda: nc.vector.tensor_tensor(out=rd[:, 4:8], in0=rd[:, 0:4], in1=gates[:, tc * 4:tc * 4 + 4, gcol], op=ALU.mult), r=[gatesb], w=[rdb[rj]])
            for j in range(4):
                ob = Obanks[j // 2]
                o0 = (j % 2) * nv
                src = bank[ob][:, o0:o0 + 128]
                f = rd[:, 4 + j:5 + j]
                if mode == "first":
                    op(DVE, lambda: nc.vector.tensor_scalar(out=ocomb[oc][:, j, hl, :], in0=src, scalar1=f, scalar2=None, op0=ALU.mult), r=[bankb[ob], rdb[rj]], pw=[ocombb[oc]])
                elif mode == "add":
                    op(DVE, lambda: nc.vector.scalar_tensor_tensor(out=ocomb[oc][:, j, hl, :], in0=src, scalar=f, in1=ocomb[oc][:, j, hl, :], op0=ALU.mult, op1=ALU.add), r=[bankb[ob], rdb[rj]], w=[ocombb[oc]])
                else:
                    op(DVE, lambda: nc.vector.scalar_tensor_tensor(out=ocb[oc][:, j, hl, :], in0=src, scalar=f, in1=ocomb[oc][:, j, hl, :], op0=ALU.mult, op1=ALU.add), r=[bankb[ob], rdb[rj], ocombb[oc]], pw=[ocbb[oc]])
            return rj

        LBS = lambda i: cbf[:, CB_LBS + i * 128:CB_LBS + (i + 1) * 128]
        LBW = lambda i: cbf[:, CB_LBW + i * 128:CB_LBW + (i + 1) * 128]
        LBC = cbf[:, CB_LBC:CB_LBC + 128]
        CSm = lambda m: cbf[:, CB_CS + 384 - 128 * m:CB_CS + 384 - 128 * m + 512]
        WSm = lambda m: cbf[:, CB_WS + 384 - 128 * m:CB_WS + 384 - 128 * m + 512]

        def run_stream(jobs, L=2):
            pend = []
            for n in range(len(jobs) + L):
                if n < len(jobs):
                    pend.append((jobs[n], jobs[n][0]()))
                if n >= L:
                    job, pj = pend.pop(0)
                    job[1](pj)

        for g in range(2):
            for tc in range(2):
                oc = 0
                Q = lambda h: qnT[:, h, tc * 512:(tc + 1) * 512]
                jobs = []
                for hl in range(4):
                    h = 4 * g + hl
                    Ob = rot_O.next()

                    def sc(h=h):
                        return attn_tile(kcmpT[:, g, :], [kcmpb], Q(h), [qnTb[h][tc]], LBC, h * 2 + tc, cbf[:, CB_CM + tc * 512:CB_CM + (tc + 1) * 512])

                    def pvf(pj, h=h, hl=hl, Ob=Ob):
                        pv(pj, Ob, vcaug[:, g, :], [vcaugb], NVC, True, True)
                        rj = finish(Ob, NVC, h, tc, 0, oc, "first")
                        rd = rdt[rj]
                        for j in range(4):
                            ob = Ob[j // 2]
                            o0 = (j % 2) * NVC + 129
                            if hl == 0:
                                op(DVE, lambda: nc.vector.tensor_scalar(out=imp[:, j, :], in0=bank[ob][:, o0:o0 + 32], scalar1=rd[:, j:j + 1], scalar2=None, op0=ALU.mult), r=[bankb[ob], rdb[rj]], w=[impb] if j == 0 else [], pw=[impb] if j else [])
                            else:
                                op(DVE, lambda: nc.vector.scalar_tensor_tensor(out=imp[:, j, :], in0=bank[ob][:, o0:o0 + 32], scalar=rd[:, j:j + 1], in1=imp[:, j, :], op0=ALU.mult, op1=ALU.add), r=[bankb[ob], rdb[rj]], w=[impb])
                    jobs.append((sc, pvf))
                for hl in range(4):
                    h = 4 * g + hl
                    Ob = rot_O.next()
                    ks = list(range(4 + 4 * tc, 12 + 4 * tc))
                    for n_, i in enumerate(ks):
                        m = i - (4 + 4 * tc)
                        mask = WSm(m) if m < 4 else CSm(m - 4)

                        def sc(h=h, i=i, mask=mask):
                            return attn_tile(KnT[:, 2 + g, i * 128:(i + 1) * 128], [KnTb[2 + g][i // 4]], Q(h), [qnTb[h][tc]], LBW(i), h * 2 + tc, mask)

                        def pvf(pj, h=h, i=i, n_=n_, Ob=Ob, last=(n_ == len(ks) - 1)):
                            pv(pj, Ob, Vaug[:, i, 2 + g, :], [Vaugb[i]], NV, n_ == 0, last)
                            if last:
                                finish(Ob, NV, h, tc, 2, oc, "add")
                        jobs.append((sc, pvf))
                run_stream(jobs[:4])
                op(DVE, lambda: nc.vector.tensor_tensor(out=imp[:, :, :], in0=imp[:, :, :], in1=bonus[:, tc * 4:tc * 4 + 4, :], op=ALU.add), r=[bonusb], w=[impb])
                tb = rot_tr.next()
                tbv = bank[tb][:, :].bitcast(BF16)
                for j in range(4):
                    op(DVE, lambda: nc.vector.max(out=m8[:, 0:8], in_=imp[:, j, :]), r=[impb], w=[m8b])
                    op(DVE, lambda: nc.vector.match_replace(out=sc2[:, j, :], in_to_replace=m8[:, 0:8], in_values=imp[:, j, :], imm_value=-3.0e38), r=[impb, m8b], w=[sc2b])
                    op(DVE, lambda: nc.vector.max(out=m8[:, 8:16], in_=sc2[:, j, :]), r=[sc2b], w=[m8b])
                    op(DVE, lambda: nc.vector.tensor_scalar(out=selbf[:, j, :], in0=imp[:, j, :], scalar1=m8[:, 15:16], scalar2=None, op0=ALU.is_ge), r=[impb, m8b], w=[selbfb])
                if g == 0 and tc == 1:
                    self.tap("imp", imp[:, :, :], [impb], [128, 4, 32], F32)
                    self.tap("selbf", selbf[:, :, :], [selbfb], [128, 4, 32], BF16)
                run_stream(jobs[4:])
                for j in range(4):
                    op(PE, lambda: nc.tensor.transpose(out=tbv[0:32, j * 128:(j + 1) * 128], in_=selbf[:, j, :], identity=ident), r=[selbfb, cidb], w=[bankb[tb]])
                for hl in range(4):
                    h = 4 * g + hl
                    copy_op(ACT, RB[0:32, h * 2 + tc, :], tbv[0:32, 0:512], r=[bankb[tb]], w=[RBb[h * 2 + tc]])
                jobs = []
                for hl in range(4):
                    h = 4 * g + hl
                    Ob = rot_O.next()
                    ks = list(range(0, 12 + 4 * tc))
                    for n_, i in enumerate(ks):
                        m = i - (8 + 4 * tc)
                        mask = CSm(m) if m >= 0 else None

                        def sc(h=h, i=i, mask=mask):
                            return attn_tile(KnT[:, g, i * 128:(i + 1) * 128], [KnTb[g][i // 4]], Q(h), [qnTb[h][tc]], LBS(i), h * 2 + tc, mask)

                        def pvf(pj, h=h, i=i, n_=n_, Ob=Ob, last=(n_ == len(ks) - 1)):
                            pv(pj, Ob, Vaug[:, i, g, :], [Vaugb[i]], NV, n_ == 0, last)
                            if last:
                                finish(Ob, NV, h, tc, 1, oc, "last")
                        jobs.append((sc, pvf))
                run_stream(jobs)
                for hl in range(4):
                    h = 4 * g + hl
                    tb2 = rot_tr.next()
                    tv = bank[tb2][:, :].bitcast(BF16)
                    for j in range(4):
                        op(PE, lambda: nc.tensor.transpose(out=tv[:, j * 128:(j + 1) * 128], in_=ocb[oc][:, j, hl, :], identity=ident), r=[ocbb[oc], cidb], w=[bankb[tb2]])
                    copy_op(evac_engine(), catA[:, h, tc * 512:(tc + 1) * 512], tv[:, 0:512], r=[bankb[tb2]], w=[catTb[h][tc]])
        self.tap("catT_nsa", catA[:, :, :], [catTb[c][t] for c in range(8) for t in range(2)], [128, 8, TQ], BF16)
        if self.stop == "p2":
            return self.fin()
        AT.close()
        self.free(RBb + [bonusb, impb, sc2b, m8b, selbfb] + Ptb + ocombb + ocbb + rdb)
        A.close()
        self.free([b for l in KnTb for b in l] + Vaugb + [b for l in qnTb for b in l] + [gatesb, kcmpb, vcaugb, cbb])
        catB = self.tile(CT, "catB", [128, 8, TQ], BF16)

        self.mark("conv")
        CV = self.scope()
        dg = [self.tile(CV, f"dg{j}", [128, 31, 128], BF16) for j in range(2)]
        dgb = [self.buf(f"dg{j}") for j in range(2)]
        cv = self.tile(CV, "cv", [128, 8, TQ], F32)
        cvb_ = [[self.buf(f"cv{c}_{t}") for t in range(2)] for c in range(8)]
        cvs = [self.tile(CV, f"cvs{j}", [128, 2, 512], BF16) for j in range(2)]
        cvsb = [self.buf(f"cvs{j}") for j in range(2)]
        rot_cvs = Rot([0, 1])
        stat_bank = {(0, 0): 4, (0, 1): 5, (1, 0): 6, (1, 1): 7}
        mean = self.tile(CV, "mean", [128, 512], F32)
        meanb = self.buf("mean")
        rstd = self.tile(CV, "rstd", [128, 512], F32)
        rstdb = self.buf("rstd")
        t1 = [self.tile(CV, f"t1_{j}", [128, 512], F32) for j in range(2)]
        t1b = [self.buf(f"t1_{j}") for j in range(2)]
        ident_b = ident.unsqueeze(1).broadcast_to([128, 31, 128])

        def ln_prep(tc):
            sb0, sb1 = stat_bank[(tc, 0)], stat_bank[(tc, 1)]
            op(DVE, lambda: nc.vector.tensor_scalar(out=mean[:, :], in0=bank[sb0][:, :], scalar1=1.0 / 1024, scalar2=None, op0=ALU.mult), r=[bankb[sb0]], w=[meanb])
            op(DVE, lambda: nc.vector.tensor_tensor(out=rstd[:, :], in0=mean[:, :], in1=mean[:, :], op=ALU.mult), r=[meanb], w=[rstdb])
            op(DVE, lambda: nc.vector.scalar_tensor_tensor(out=rstd[:, :], in0=bank[sb1][:, :], scalar=1.0 / 1024, in1=rstd[:, :], op0=ALU.mult, op1=ALU.subtract), r=[bankb[sb1]], w=[rstdb])
            op(ACT, lambda: nc.scalar.activation(out=rstd[:, :], in_=rstd[:, :], func=AF.Sqrt, bias=eps[:, 1:2]), r=[epsb], w=[rstdb])
            op(DVE, lambda: nc.vector.reciprocal(out=rstd[:, :], in_=rstd[:, :]), w=[rstdb])

        def ln_chunk(tc, ch):
            j = ch % 2
            op(DVE, lambda: nc.vector.tensor_tensor(out=t1[j][:, :], in0=cv[:, ch, tc * 512:(tc + 1) * 512], in1=mean[:, :], op=ALU.subtract), r=[cvb_[ch][tc], meanb], w=[t1b[j]])
            op(DVE, lambda: nc.vector.tensor_tensor(out=t1[j][:, :], in0=t1[j][:, :], in1=rstd[:, :], op=ALU.mult), r=[rstdb], w=[t1b[j]])
            op(ACT, lambda: nc.scalar.activation(out=catB[:, ch, tc * 512:(tc + 1) * 512], in_=t1[j][:, :], func=AF.Silu, scale=pvec[:, PV_LNG + ch:PV_LNG + ch + 1], bias=pvec[:, PV_LNB + ch:PV_LNB + ch + 1]), r=[t1b[j], pvb], w=[catTb[8 + ch][tc]])

        pend = []

        def build_dg(n):
            ch_ = n % 8
            wv = pvec[:, PV_CW + ch_ * 31:PV_CW + ch_ * 31 + 31].unsqueeze(2).broadcast_to([128, 31, 128])
            op(DVE, lambda: nc.vector.tensor_tensor(out=dg[n % 2][:, :, :], in0=ident_b, in1=wv, op=ALU.mult), r=[cidb, pvb], w=[dgb[n % 2]])

        it_ = 0
        build_dg(0)
        for tc in range(2):
            for ch in range(8):
                dj = it_ % 2
                if it_ + 1 < 16:
                    build_dg(it_ + 1)
                it_ += 1
                psi = rot_proj.next()
                for j in range(31):
                    o = 2 + j + tc * 512
                    op(PE, lambda: nc.tensor.matmul(bank[psi][:, :], lhsT=dg[dj][:, j, :], rhs=hglu[:, ch, o:o + 512], start=(j == 0), stop=(j == 30)), r=[dgb[dj], hglub[ch]], w=[bankb[psi]])
                while pend:
                    pend.pop(0)()
                op(ACT, lambda: nc.scalar.activation(out=cv[:, ch, tc * 512:(tc + 1) * 512], in_=bank[psi][:, :], func=AF.Identity, bias=pvec[:, PV_CB + ch:PV_CB + ch + 1]), r=[bankb[psi], pvb], w=[cvb_[ch][tc]])
                sj = rot_cvs.next()
                op(DVE, lambda: nc.vector.tensor_copy(out=cvs[sj][:, 0, :], in_=cv[:, ch, tc * 512:(tc + 1) * 512]), r=[cvb_[ch][tc]], w=[cvsb[sj]])
                op(ACT, lambda: nc.scalar.activation(out=cvs[sj][:, 1, :], in_=cv[:, ch, tc * 512:(tc + 1) * 512], func=AF.Square), r=[cvb_[ch][tc]], pw=[cvsb[sj]])

                def stats(tc=tc, ch=ch, sj=sj):
                    for kind in range(2):
                        sb_ = stat_bank[(tc, kind)]
                        op(PE, lambda: nc.tensor.matmul(bank[sb_][:, :], lhsT=ones, rhs=cvs[sj][:, kind, :], start=(ch == 0), stop=(ch == 7)), r=[cvsb[sj], cidb], w=[bankb[sb_]])
                pend.append(stats)
                if tc == 1:
                    ln_chunk(0, ch)
            while pend:
                pend.pop(0)()
            if tc == 0:
                ln_prep(0)
        ln_prep(1)
        for ch in range(8):
            ln_chunk(1, ch)
        self.tap("cv", cv[:, :, :], [b for l in cvb_ for b in l], [128, 8, TQ], F32)
        self.tap("catB", catB[:, :, :], [catTb[c][t] for c in range(8, 16) for t in range(2)], [128, 8, TQ], BF16)
        if self.stop == "p3":
            return self.fin()
        CV.close()
        self.free(dgb + [b for l in cvb_ for b in l] + cvsb + [meanb, rstdb] + t1b)
        HG.close()
        self.free(hglub)

        self.mark("memkv")
        MK = self.scope()
        hmT = self.tile(MK, "hmT", [128, 16, 256], BF16)
        hmTb = self.buf("hmT")
        kmnT = self.tile(MK, "kmnT", [128, 4, 256], BF16)
        kmnTb = [self.buf(f"kmnT{h}") for h in range(4)]
        vmaug = self.tile(MK, "vmaug", [128, 2, 4, NV], BF16)
        vmaugb = self.buf("vmaug")
        S1m = self.scope()
        mt_ = [self.tile(S1m, f"memt{j}", [128, D], F32) for j in range(2)]
        mtb = [self.buf(f"memt{j}") for j in range(2)]
        for j in range(2):
            dma(SP, mt_[j][:, :], self.memb[j * 128:(j + 1) * 128, :], w=[mtb[j]])
        rmsnorm_T(S1m, lambda i: (mt_[i][:, :], [mtb[i]]), 2, PV_NMKV, hmT, lambda i: hmTb, lambda i: i * 128)
        S1m.close()
        self.free(mtb)
        op(POOL, lambda: nc.gpsimd.memset(vmaug[:, :, :, :], 1.0), w=[vmaugb])
        for blk in range(2):
            wt, wb = wget()
            for cc in range(2):
                h = blk * 2 + cc
                psi = proj_fm(wt, wb, cc, hmT, [hmTb], 0, 256)
                pnorm(psi, 256, pvec[:, PV_MKN:PV_MKN + 1], [pvb], kmnT[:, h, :], kmnTb[h])
            flush(0)
            wrel()
        for blk in range(2):
            wt, wb = wget()
            for mt in range(2):
                psi = proj_tm(wt, wb, hmT, [hmTb], mt * 128, 256)
                src = bank[psi][:, 0:256].rearrange("p (g d) -> p g d", g=2)
                copy_op(evac_engine(), vmaug[:, mt, blk * 2:blk * 2 + 2, 0:128], src, r=[bankb[psi]], pw=[vmaugb])
            wrel()

        self.mark("w_out")
        xres = self.tile(G, "xres", [128, 8, D], F32)
        xrb = [[self.buf(f"xres{t}_{c}") for c in range(8)] for t in range(8)]
        for tt in range(8):
            dma(SP, xres[:, tt, :], self.xc[TQ + tt * 128:TQ + (tt + 1) * 128, :], w=xrb[tt], owner=xrb[tt][0])
        for cb in range(8):
            wt, wb = wget()
            for tt in range(8):
                psi = rot_proj.next()
                for kc in range(16):
                    cat_ = catA if kc < 8 else catB
                    op(PE, lambda: nc.tensor.matmul(bank[psi][:, 0:256], lhsT=cat_[:, kc % 8, tt * 128:(tt + 1) * 128], rhs=wt[:, kc, 0:256], start=(kc == 0), stop=(kc == 15)), r=[wb, catTb[kc][tt // 4]], w=[bankb[psi]])
                dst = xres[:, tt, cb * 256:(cb + 1) * 256]
                op(DVE, lambda: nc.vector.tensor_tensor(out=dst, in0=bank[psi][:, 0:256], in1=dst, op=ALU.add), r=[bankb[psi]], w=[xrb[tt][cb]])
            wrel()
        self.tap("x1", xres[:, :, :], [b for l in xrb for b in l], [128, 8, D], F32)
        if self.stop == "p4":
            return self.fin()
        CT.close()
        self.free([b for l in catTb for b in l])

        self.mark("mem")
        M = self.scope()
        h2T = self.tile(M, "h2T", [128, 16, TQ], BF16)
        h2Tb = [self.buf(f"h2T{c}") for c in range(2)]
        S1 = self.scope()
        rmsnorm_T(S1, lambda i: (xres[:, i, :], xrb[i]), 8, PV_NMQ, h2T, lambda i: h2Tb[i // 4], lambda i: i * 128)
        S1.close()
        qmnT = self.tile(M, "qmnT", [128, 4, TQ], BF16)
        qmnTb = [[self.buf(f"qmnT{h}_{t}") for t in range(2)] for h in range(4)]
        omT = self.tile(M, "omT", [128, 4, TQ], BF16)
        omTb = [[self.buf(f"omT{h}_{t}") for t in range(2)] for h in range(4)]
        for blk in range(2):
            wt, wb = wget()
            for cc in range(2):
                h = blk * 2 + cc
                for tc in range(2):
                    psi = proj_fm(wt, wb, cc, h2T, [h2Tb[tc]], tc * 512, 512)
                    pnorm(psi, 512, eps[:, 3:4], [epsb], qmnT[:, h, tc * 512:(tc + 1) * 512], qmnTb[h][tc])
            flush(0)
            wrel()
        Pm = [self.tile(M, f"Pm{j}", [128, 512], BF16) for j in range(3)]
        Pmb = [self.buf(f"Pm{j}") for j in range(3)]
        omem = [self.tile(M, f"omem{j}", [128, 4, 128], BF16) for j in range(2)]
        omemb = [self.buf(f"omem{j}") for j in range(2)]
        rdm = [self.tile(M, f"rdm{j}", [128, 4], F32) for j in range(2)]
        rdmb = [self.buf(f"rdm{j}") for j in range(2)]
        jobs = []
        it = 0
        for h in range(4):
            for tc in range(2):
                Ob = rot_O.next()
                oj = it % 2
                it += 1
                for mt in range(2):
                    def sc(h=h, tc=tc, mt=mt):
                        si = rot_S.next()
                        op(PE, lambda: nc.tensor.matmul(bank[si][:, :], lhsT=kmnT[:, h, mt * 128:(mt + 1) * 128], rhs=qmnT[:, h, tc * 512:(tc + 1) * 512], start=True, stop=True), r=[kmnTb[h], qmnTb[h][tc]], w=[bankb[si]])
                        pj = rot_P.next()
                        op(ACT, lambda: nc.scalar.activation(out=Pm[pj][:, :], in_=bank[si][:, :], func=AF.Exp), r=[bankb[si]], w=[Pmb[pj]])
                        return pj

                    def pvf(pj, h=h, tc=tc, mt=mt, Ob=Ob, oj=oj):
                        for j in range(4):
                            ob = Ob[j // 2]
                            o0 = (j % 2) * NV
                            op(PE, lambda: nc.tensor.matmul(bank[ob][:, o0:o0 + NV], lhsT=Pm[pj][:, j * 128:(j + 1) * 128], rhs=vmaug[:, mt, h, :], start=(mt == 0 and j % 2 == 0), stop=(mt == 1), skip_group_check=True), r=[Pmb[pj], vmaugb], w=[bankb[ob]])
                        if mt == 0:
                            return
                        for b2 in range(2):
                            ob = Ob[b2]
                            op(DVE, lambda: nc.vector.tensor_scalar(out=rdm[oj][:, 2 * b2:2 * b2 + 2], in0=bank[ob][:, 128:128 + NV + 1:NV], scalar1=1e-30, scalar2=None, op0=ALU.max), r=[bankb[ob]], pw=[rdmb[oj]] if b2 else [], w=[rdmb[oj]] if not b2 else [])
                        op(DVE, lambda: nc.vector.reciprocal(out=rdm[oj][:, :], in_=rdm[oj][:, :]), w=[rdmb[oj]])
                        for j in range(4):
                            ob = Ob[j // 2]
                            o0 = (j % 2) * NV
                            op(DVE, lambda: nc.vector.tensor_scalar(out=omem[oj][:, j, :], in0=bank[ob][:, o0:o0 + 128], scalar1=rdm[oj][:, j:j + 1], scalar2=None, op0=ALU.mult), r=[bankb[ob], rdmb[oj]], w=[omemb[oj]] if j == 0 else [], pw=[omemb[oj]] if j else [])
                        while pend_tr:
                            pend_tr.pop(0)()

                        def trf(h=h, tc=tc, oj=oj):
                            tb2 = rot_tr.next()
                            tv = bank[tb2][:, :].bitcast(BF16)
                            for j in range(4):
                                op(PE, lambda: nc.tensor.transpose(out=tv[:, j * 128:(j + 1) * 128], in_=omem[oj][:, j, :], identity=ident), r=[omemb[oj], cidb], w=[bankb[tb2]])
                            copy_op(evac_engine(), omT[:, h, tc * 512:(tc + 1) * 512], tv[:, 0:512], r=[bankb[tb2]], w=[omTb[h][tc]])
                        pend_tr.append(trf)
                    jobs.append((sc, pvf))
        pend_tr = []
        run_stream(jobs)
        while pend_tr:
            pend_tr.pop(0)()
        wmo = [wget(), wget()]
        for tt in range(8):
            for cb4 in range(4):
                psi = rot_proj.next()
                for kc in range(4):
                    wt, wb = wmo[kc // 2]
                    op(PE, lambda: nc.tensor.matmul(bank[psi][:, :], lhsT=omT[:, kc, tt * 128:(tt + 1) * 128], rhs=wt[:, kc % 2, cb4 * 512:(cb4 + 1) * 512], start=(kc == 0), stop=(kc == 3)), r=[wb, omTb[kc][tt // 4]], w=[bankb[psi]])
                dst = xres[:, tt, cb4 * 512:(cb4 + 1) * 512]
                op(DVE, lambda: nc.vector.tensor_tensor(out=dst, in0=bank[psi][:, :], in1=dst, op=ALU.add), r=[bankb[psi]], w=[xrb[tt][2 * cb4], xrb[tt][2 * cb4 + 1]])
        wrel()
        self.tap("x2", xres[:, :, :], [b for l in xrb for b in l], [128, 8, D], F32)
        if self.stop == "p5":
            return self.fin()
        M.close()
        self.free(h2Tb + [b for l in qmnTb for b in l] + [b for l in omTb for b in l] + Pmb + omemb + rdmb)
        MK.close()
        self.free([hmTb, vmaugb] + kmnTb)

        self.mark("ffn")
        Fz = self.scope()
        h3T = self.tile(Fz, "h3T", [128, 16, TQ], BF16)
        h3Tb = [self.buf(f"h3T{c}") for c in range(2)]
        S3 = self.scope()
        rmsnorm_T(S3, lambda i: (xres[:, i, :], xrb[i]), 8, PV_NFFN, h3T, lambda i: h3Tb[i // 4], lambda i: i * 128)
        S3.close()
        actT = [self.tile(Fz, f"actT{j}", [128, 2, TQ], BF16) for j in range(2)]
        actTb = [[[self.buf(f"actT{j}_{c}_{t}") for t in range(2)] for c in range(2)] for j in range(2)]
        sgt = [self.tile(Fz, f"sgt{j}", [128, 512], BF16) for j in range(2)]
        sgtb = [self.buf(f"sgt{j}") for j in range(2)]
        rot_g = Rot([4, 5])
        rot_u = Rot([6, 7])
        rot_sgt = Rot([0, 1])
        dheld = []
        for jb in range(22):
            wgt, wgb_ = wget()
            wut, wub = wget()
            aj = jb % 2
            for cc in range(2):
                for tc in range(2):
                    pg = rot_g.next()
                    pu = rot_u.next()
                    for kc in range(16):
                        op(PE, lambda: nc.tensor.matmul(bank[pg][:, :], lhsT=wgt[:, kc, cc * 128:(cc + 1) * 128], rhs=h3T[:, kc, tc * 512:(tc + 1) * 512], start=(kc == 0), stop=(kc == 15)), r=[wgb_, h3Tb[tc]], w=[bankb[pg]])
                    for kc in range(16):
                        op(PE, lambda: nc.tensor.matmul(bank[pu][:, :], lhsT=wut[:, kc, cc * 128:(cc + 1) * 128], rhs=h3T[:, kc, tc * 512:(tc + 1) * 512], start=(kc == 0), stop=(kc == 15)), r=[wub, h3Tb[tc]], w=[bankb[pu]])
                    sj = rot_sgt.next()
                    op(ACT, lambda: nc.scalar.activation(out=sgt[sj][:, :], in_=bank[pg][:, :], func=AF.Silu), r=[bankb[pg]], w=[sgtb[sj]])
                    op(DVE, lambda: nc.vector.tensor_tensor(out=actT[aj][:, cc, tc * 512:(tc + 1) * 512], in0=bank[pu][:, :], in1=sgt[sj][:, :], op=ALU.mult), r=[bankb[pu], sgtb[sj]], w=[actTb[aj][cc][tc]])
            wrel(2, newest=True)
            wdt, wdb = wget()
            dheld.append((aj, wdt, wdb))
            if jb % 2 == 0:
                continue
            for tt in range(8):
                for cb4 in range(4):
                    psi = rot_proj.next()
                    n_ = 0
                    for (a_, wd_, wdb_) in dheld:
                        for cc in range(2):
                            op(PE, lambda: nc.tensor.matmul(bank[psi][:, :], lhsT=actT[a_][:, cc, tt * 128:(tt + 1) * 128], rhs=wd_[:, cc, cb4 * 512:(cb4 + 1) * 512], start=(n_ == 0), stop=(n_ == 3)), r=[wdb_, actTb[a_][cc][tt // 4]], w=[bankb[psi]])
                            n_ += 1
                    dst = xres[:, tt, cb4 * 512:(cb4 + 1) * 512]
                    op(DVE, lambda: nc.vector.tensor_tensor(out=dst, in0=bank[psi][:, :], in1=dst, op=ALU.add), r=[bankb[psi]], w=[xrb[tt][2 * cb4], xrb[tt][2 * cb4 + 1]])
            dheld = []
            wrel()
        self.mark("out")
        for tt in range(8):
            ob_ = self.buf(f"out{tt}")
            dma(SP, self.y[tt * 128:(tt + 1) * 128, :], xres[:, tt, :], r=xrb[tt], owner=ob_)
            self.outtoks.append((ob_.dsem, ob_.dcnt))
        for sem, val in self.outtoks:
            SP.eng.wait_ge(sem.h, val)
        return nc


def _bf(a):
    return np.ascontiguousarray(a.astype(np.float32)).astype(NPBF)


def _consts(s):
    first_real = 0 if s == 1 else 1024
    k = np.arange(TC)
    lbs = np.zeros((128, TC), np.float32)
    lbs[k // 64, k] = BIG
    lbs[32, :] = 16 * (k // 16)
    lbs[33, :] = k % 16
    lbs[34, :] = 1.0
    lbs[35, :] = 1.0
    lbs[36, :] = -BIG
    lbs[37, :] = np.where(k < first_real, -BIG, 0.0)
    lbw = lbs.copy()
    lbw[0:32, :] = 0.0
    lbw[36, :] = 0.0
    c = np.arange(128)
    lbc = np.zeros((128, 128), np.float32)
    lbc[32, :] = 16 * c
    lbc[33, :] = 15.5
    lbc[34, :] = 1.0
    lbc[35, :] = 1.0
    kk = np.arange(128)[:, None]
    x = np.arange(896)[None, :]
    cs = np.where((x - 384) < kk, -BIG, 0.0)
    ws = np.where((x - 384) >= kk, -BIG, 0.0)
    t_ctx = 1024 + np.arange(TQ)[None, :]
    cc = np.arange(128)[:, None]
    cm = np.where((16 * cc + 31 > t_ctx) | (cc >= 127) | (16 * cc < first_real), -BIG, 0.0)
    cs_ = np.arange(128)[:, None] * 16
    bs_ = np.arange(32)[None, :] * 64
    ov = np.clip(np.minimum(cs_ + 32, bs_ + 64) - np.maximum(cs_, bs_), 0, None).astype(np.float32) / 32.0
    ovaug = np.concatenate([np.ones((128, 1), np.float32), ov], axis=1)
    ovaug[127, :] = 0.0
    ident = np.eye(128, dtype=np.float32)
    ones = np.ones((128, 128), np.float32)
    cbf = np.concatenate([lbs, lbw, lbc, cs, ws, cm, ovaug, ident, ones], axis=1)
    assert cbf.shape[1] == NCB
    rb = np.zeros((128, 16, 512), np.float32)
    slopes = 2.0 ** (-(np.arange(NH) + 1.0))
    for h in range(NH):
        for tc in range(2):
            t = 1024 + tc * 512 + np.arange(512)
            i = h * 2 + tc
            rb[0:32, i, :] = 1.0
            rb[32, i, :] = slopes[h]
            rb[33, i, :] = slopes[h]
            rb[34, i, :] = -slopes[h] * (16 * (t // 16))
            rb[35, i, :] = -slopes[h] * (t % 16)
            rb[36, i, :] = 1.0
            rb[37, i, :] = 1.0
    tq = 1024 + np.arange(TQ)[:, None]
    blk = np.arange(32)[None, :]
    cur = tq // 64
    valid = (blk * 64 <= tq) & (blk * 64 >= first_real)
    forced = (blk == first_real // 64) | (blk == cur) | (blk == cur - 1)
    bon = np.where(valid, 1000.0 * forced, -1e30).astype(np.float32)
    bon = bon.reshape(8, 128, 32).transpose(1, 0, 2).reshape(128, 256)
    return _bf(cbf), _bf(rb.reshape(128, 16 * 512)), np.ascontiguousarray(bon)


def _pvec(I):
    pv = np.zeros((128, NPV), np.float32)
    f = lambda a: np.asarray(a, np.float32)
    pv[:, PV_NMIX:PV_NMIX + 16] = f(I["norm_mix"])[0].reshape(16, 128).T
    pv[:, PV_NMQ:PV_NMQ + 16] = f(I["norm_mem_q"])[0].reshape(16, 128).T
    pv[:, PV_NMKV:PV_NMKV + 16] = f(I["norm_mem_kv"])[0].reshape(16, 128).T
    pv[:, PV_NFFN:PV_NFFN + 16] = f(I["norm_ffn"])[0].reshape(16, 128).T
    pv[:, PV_QN] = f(I["q_norm"])[0]
    pv[:, PV_KNC] = f(I["k_norm_cmp"])[0]
    pv[:, PV_KNS] = f(I["k_norm_slc"])[0]
    pv[:, PV_KNW] = f(I["k_norm_win"])[0]
    pv[:, PV_MQN] = f(I["mq_norm"])[0]
    pv[:, PV_MKN] = f(I["mk_norm"])[0]
    pv[:, PV_CB:PV_CB + 8] = f(I["conv_b"])[0].reshape(8, 128).T
    pv[:, PV_LNG:PV_LNG + 8] = f(I["conv_ln_g"])[0].reshape(8, 128).T
    pv[:, PV_LNB:PV_LNB + 8] = f(I["conv_ln_b"])[0].reshape(8, 128).T
    cw = f(I["conv_w"])[0]
    pv[:, PV_CW:PV_CW + 248] = cw.reshape(31, 8, 128).transpose(2, 1, 0).reshape(128, 248)
    pv[:, PV_GB:PV_GB + 24] = f(I["gate_b"])[0][None, :]
    pv[:, PV_POSK:PV_POSK + 32] = f(I["cmp_pos_k"])[0].T
    pv[:, PV_POSV:PV_POSV + 32] = f(I["cmp_pos_v"])[0].T
    return pv


def make_in_maps(I):
    f = lambda a: np.ascontiguousarray(np.asarray(a, np.float32))
    x = f(I["x"])
    mem = f(I["mem"])
    w_in = f(I["w_in"])[0]
    shared = {
        "w_in": w_in, "w_g": np.ascontiguousarray(w_in[:, C_G:C_G + 24]),
        "w_out": f(I["w_out"])[0], "w_mq": f(I["w_mq"])[0], "w_mk": f(I["w_mk"])[0], "w_mv": f(I["w_mv"])[0],
        "w_mo": f(I["w_mo"])[0], "w_gate": f(I["w_gate"])[0], "w_up": f(I["w_up"])[0], "w_down": f(I["w_down"])[0],
        "w1k": f(I["cmp_k_w1"])[0], "w1v": f(I["cmp_v_w1"])[0], "w2k": f(I["cmp_k_w2"])[0], "w2v": f(I["cmp_v_w2"])[0],
        "pvec": _pvec(I),
    }
    cs = [_consts(0), _consts(1)]
    maps = []
    for core in range(8):
        b, s = core // 2, core % 2
        if s == 1:
            xc = x[b]
        else:
            xc = np.concatenate([np.zeros((TQ, D), np.float32), x[b, :TQ]], axis=0)
        m = dict(shared)
        m["xc"] = np.ascontiguousarray(xc)
        m["memb"] = mem[b]
        m["cbf"], m["rb"], m["bonus"] = cs[s]
        maps.append(m)
    return maps


_CACHE = {}


def kernel(**inputs):
    if "k" not in _CACHE:
        K = Kern()
        K.build()
        _CACHE["k"] = K
    K = _CACHE["k"]
    maps = make_in_maps(inputs)
    res = run_bass_kernel_spmd(K.nc, maps, core_ids=list(range(8)))
    out = np.zeros((4, 2048, D), np.float32)
    for core in range(8):
        b, s = core // 2, core % 2
        out[b, s * TQ:(s + 1) * TQ] = np.asarray(res.results[core]["y"], np.float32).reshape(TQ, D)
    return out
```

```python
import numpy as np
import ml_dtypes
from contextlib import ExitStack
import concourse.bass as bass
import concourse.mybir as mybir
from concourse.bass_utils import run_bass_kernel_spmd

F32 = mybir.dt.float32
BF16 = mybir.dt.bfloat16
AF = mybir.ActivationFunctionType
ALU = mybir.AluOpType
NPBF = ml_dtypes.bfloat16

D = 2048
TQ = 1024
TC = 2048
NH = 8
DH = 128
IN_W = 4632
FFN = 5632
BIG = 32768.0
NV = 129
NVC = 161

C_Q, C_KC, C_VC, C_KSL, C_VSL, C_KW, C_VW, C_G, C_U = 0, 1024, 1280, 1536, 1792, 2048, 2304, 2560, 2584

PV_NMIX, PV_NMQ, PV_NMKV, PV_NFFN = 0, 16, 32, 48
PV_QN, PV_KNC, PV_KNS, PV_KNW, PV_MQN, PV_MKN = 64, 65, 66, 67, 68, 69
PV_CB, PV_LNG, PV_LNB, PV_CW = 70, 78, 86, 94
PV_GB = 94 + 248
PV_POSK = PV_GB + 24
PV_POSV = PV_POSK + 32
NPV = PV_POSV + 32

CB_LBS, CB_LBW, CB_LBC, CB_CS, CB_WS, CB_CM, CB_OV, CB_ID, CB_ONE = 0, 2048, 4096, 4224, 5120, 6016, 7040, 7073, 7201
NCB = 7329


class Sem:
    n = 0

    def __init__(self, h):
        self.h = h
        Sem.n += 1
        self.id = Sem.n


class Tok:
    __slots__ = ("sem", "val", "eng")

    def __init__(self, sem, val, eng):
        self.sem, self.val, self.eng = sem, val, eng


def _add(d, tok):
    cur = d.get(tok.sem.id)
    if cur is None or cur.val < tok.val:
        d[tok.sem.id] = tok


class Buf:
    def __init__(self, name, fence):
        self.name = name
        self.w = {}
        self.wf = {}
        self.r = dict(fence)
        self.dsem = None
        self.dcnt = 0
        self.excl = False


class Eng:
    EPOCH = 12000

    def __init__(self, K, eng, name, is_pe=False):
        self.K, self.eng, self.name, self.is_pe = K, eng, name, is_pe
        self.seen = {}
        self.ep = 0
        self.nwait = 0
        self.nins = 0
        self._new()

    def _new(self):
        self.sem = self.K.new_sem(f"e{self.name}{self.ep}")
        self.ep += 1
        self.cnt = 0

    def wait(self, tok):
        if tok is None:
            return
        if self.is_pe and tok.eng is self:
            return
        if self.seen.get(tok.sem.id, 0) >= tok.val:
            return
        self.eng.wait_ge(tok.sem.h, tok.val)
        self.nwait += 1
        self.seen[tok.sem.id] = tok.val

    def issue(self, ins):
        if self.cnt >= self.EPOCH:
            self._new()
        ins.then_inc(self.sem.h, 1)
        self.cnt += 1
        self.nins += 1
        return Tok(self.sem, self.cnt, self)


class Scope:
    def __init__(self, K):
        self.K = K
        self.blocks = []

    def close(self):
        for off, n in self.blocks:
            self.K.arena_release(off, n)
        self.blocks = []


class Rot:
    def __init__(self, items):
        self.items = list(items)
        self.i = 0

    def next(self):
        it = self.items[self.i % len(self.items)]
        self.i += 1
        return it


class Kern:
    def __init__(self, taps=(), stop=None):
        self.stop = stop
        self.taps = list(taps)
        self.tap_tensors = {}
        self.nc = bass.Bass("TRN2", target_bir_lowering=False)
        self.nsem = 0
        self.fence = {}
        self.es = ExitStack()

    def new_sem(self, name):
        self.nsem += 1
        return Sem(self.nc.alloc_semaphore(name=f"s{self.nsem}_{name}"))

    def buf(self, name):
        return Buf(name, self.fence)

    def free(self, bufs):
        for b in bufs:
            for t in b.w.values():
                _add(self.fence, t)
            for t in b.r.values():
                _add(self.fence, t)

    ARENA_BYTES = 198 * 1024

    def arena_init(self):
        self.arena = self.es.enter_context(self.nc.sbuf_tensor("arena", [128, self.ARENA_BYTES // 2], BF16))
        self.afree = [(0, self.ARENA_BYTES)]
        self.apeak = 0

    def arena_release(self, off, n):
        self.afree.append((off, n))
        self.afree.sort()
        m = []
        for o, l in self.afree:
            if m and m[-1][0] + m[-1][1] == o:
                m[-1] = (m[-1][0], m[-1][1] + l)
            else:
                m.append((o, l))
        self.afree = m

    def scope(self):
        return Scope(self)

    def tile(self, S, name, shape, dt):
        esz = 4 if dt == F32 else 2
        nel = int(np.prod(shape[1:]))
        nb = (nel * esz + 63) // 64 * 64
        for i, (o, l) in enumerate(self.afree):
            if l >= nb:
                self.afree[i] = (o + nb, l - nb)
                if l == nb:
                    self.afree.pop(i)
                break
        else:
            raise RuntimeError(f"arena OOM allocating {name} {shape} ({nb}B); free={self.afree}")
        S.blocks.append((o, nb))
        self.apeak = max(self.apeak, o + nb)
        v = self.arena[:, o // 2:o // 2 + nel * esz // 2]
        if dt == F32:
            v = v.bitcast(F32)
        if len(shape) == 3:
            v = v.rearrange("p (a b) -> p a b", a=shape[1])
        elif len(shape) == 4:
            v = v.rearrange("p (a b c) -> p a b c", a=shape[1], b=shape[2])
        return v

    def _waits(self, E, r, w, pw):
        for b in r:
            for t in b.w.values():
                E.wait(t)
            if b.excl:
                for t in b.r.values():
                    if t.eng is not E:
                        E.wait(t)
        for b in w:
            for t in b.w.values():
                E.wait(t)
            for t in b.r.values():
                E.wait(t)
        for b in pw:
            for t in b.wf.values():
                E.wait(t)
            for t in b.r.values():
                E.wait(t)

    def _upd(self, tok, r, w, pw):
        for b in w:
            b.w = {tok.sem.id: tok}
            b.wf = {tok.sem.id: tok}
            b.r = {}
        for b in pw:
            _add(b.w, tok)
        for b in r:
            if b not in w and b not in pw:
                _add(b.r, tok)

    def op(self, E, fn, r=(), w=(), pw=()):
        self._waits(E, r, w, pw)
        ins = fn()
        tok = E.issue(ins)
        self._upd(tok, r, w, pw)
        return tok

    def dma(self, Q, out, in_, r=(), w=(), pw=(), owner=None, **kw):
        self._waits(Q, r, w, pw)
        if owner is None:
            owner = (list(w) + list(pw) + list(r))[0]
        if owner.dsem is None:
            owner.dsem = self.new_sem("d" + owner.name)
        ins = Q.eng.dma_start(out=out, in_=in_, **kw)
        owner.dcnt += 16
        ins.then_inc(owner.dsem.h, 16)
        tok = Tok(owner.dsem, owner.dcnt, None)
        self._upd(tok, r, w, pw)
        return tok

    def tap(self, name, ap, bufs, shape, dt=F32):
        if name not in self.taps:
            return
        t = self.nc.dram_tensor("tap_" + name, list(shape), dt, kind="ExternalOutput").ap()
        self.tap_tensors[name] = t
        tb = self.buf("tap" + name)
        self.dma(self.SP, t, ap, r=list(bufs), owner=tb)
        self.outtoks.append((tb.dsem, tb.dcnt))

    def mark(self, name):
        self.marks.append((name, self.PE.nins, self.ACT.nins, self.DVE.nins))

    def fin(self):
        for sem, val in self.outtoks:
            self.SP.eng.wait_ge(sem.h, val)
        return self.nc

    def build(self):
        nc = self.nc
        es = self.es
        dr = lambda n, s, dt=F32: nc.dram_tensor(n, list(s), dt, kind="ExternalInput").ap()
        self.xc = dr("xc", [TC, D])
        self.memb = dr("memb", [256, D])
        self.w_in = dr("w_in", [D, IN_W])
        self.w_g = dr("w_g", [D, 24])
        self.w_out = dr("w_out", [D, D])
        self.w_mq = dr("w_mq", [D, 512])
        self.w_mk = dr("w_mk", [D, 512])
        self.w_mv = dr("w_mv", [D, 512])
        self.w_mo = dr("w_mo", [512, D])
        self.w_gate = dr("w_gate", [D, FFN])
        self.w_up = dr("w_up", [D, FFN])
        self.w_down = dr("w_down", [FFN, D])
        self.w1k = dr("w1k", [4096, 128])
        self.w1v = dr("w1v", [4096, 128])
        self.w2k = dr("w2k", [128, 128])
        self.w2v = dr("w2v", [128, 128])
        self.pvec_d = dr("pvec", [128, NPV])
        self.cbf_d = dr("cbf", [128, NCB], BF16)
        self.rb_d = dr("rb", [128, 16 * 512], BF16)
        self.bonus_d = dr("bonus", [128, 256])
        self.y = nc.dram_tensor("y", [TQ, D], F32, kind="ExternalOutput").ap()
        self.outtoks = []
        self.marks = []

        self.PE = Eng(self, nc.tensor, "pe", is_pe=True)
        self.ACT = Eng(self, nc.scalar, "act")
        self.DVE = Eng(self, nc.vector, "dve")
        self.POOL = Eng(self, nc.gpsimd, "pool")
        self.SP = Eng(self, nc.sync, "sp")
        PE, ACT, DVE, POOL, SP = self.PE, self.ACT, self.DVE, self.POOL, self.SP
        op, dma = self.op, self.dma

        self.arena_init()
        self.bank = []
        self.bankb = []
        for i in range(8):
            self.bank.append(es.enter_context(nc.psum_tensor(f"bank{i}", [128, 512], F32)))
            self.bankb.append(self.buf(f"bank{i}"))
            self.bankb[-1].excl = True
        bank, bankb = self.bank, self.bankb

        G = self.scope()
        pvec = self.tile(G, "pvec", [128, NPV], F32)
        pvb = self.buf("pvec")
        cid = self.tile(G, "cid", [128, 256], BF16)
        cidb = self.buf("cid")
        eps = self.tile(G, "eps", [128, 4], F32)
        epsb = self.buf("eps")
        dma(SP, pvec[:, :], self.pvec_d, w=[pvb])
        dma(SP, cid[:, :], self.cbf_d[:, CB_ID:CB_ID + 256], w=[cidb])
        op(DVE, lambda: nc.vector.memset(eps[:, 0:1], 1e-6), w=[epsb])
        op(DVE, lambda: nc.vector.memset(eps[:, 1:2], 1e-5), pw=[epsb])
        op(DVE, lambda: nc.vector.tensor_scalar(out=eps[:, 2:3], in0=pvec[:, PV_QN:PV_QN + 1], scalar1=DH ** -0.5, scalar2=None, op0=ALU.mult), r=[pvb], pw=[epsb])
        op(DVE, lambda: nc.vector.tensor_scalar(out=eps[:, 3:4], in0=pvec[:, PV_MQN:PV_MQN + 1], scalar1=DH ** -0.5, scalar2=None, op0=ALU.mult), r=[pvb], pw=[epsb])
        ident = cid[:, 0:128]
        ones = cid[:, 128:256]

        NSLOT = 6
        wslot = [self.tile(G, f"wslot{i}", [128, 4096], BF16) for i in range(NSLOT)]
        wslotb = [self.buf(f"wslot{i}") for i in range(NSLOT)]
        self.wq = []
        self.wnext_issue = 0

        def wcol(w, c0, n=256):
            return w[:, c0:c0 + n].rearrange("(k p) n -> p k n", p=128)

        def wrow(w, r0, n=256):
            return w[r0:r0 + n, :].rearrange("(k p) n -> p k n", p=128)

        def w1v_(w):
            return w.rearrange("(l d) o -> d l o", d=128)

        self.wreleased = set()

        def wissue():
            while self.wnext_issue < len(self.wq):
                i = self.wnext_issue
                if i >= NSLOT and (i - NSLOT) not in self.wreleased:
                    break
                src, shp = self.wq[i]
                s_ = i % NSLOT
                dst = wslot[s_][:, :].rearrange("p (k n) -> p k n", k=shp[0])
                dma(POOL, dst, src, w=[wslotb[s_]])
                self.wnext_issue += 1

        self.wcur = 0
        self.wheld = []

        def wget():
            i = self.wcur
            self.wcur += 1
            wissue()
            assert i < self.wnext_issue, "weight block not issuable: too many blocks held"
            self.wheld.append(i)
            s_ = i % NSLOT
            shp = self.wq[i][1]
            return wslot[s_][:, :].rearrange("p (k n) -> p k n", k=shp[0]), wslotb[s_]

        def wrel(n=None, newest=False):
            n = len(self.wheld) if n is None else n
            for _ in range(n):
                self.wreleased.add(self.wheld.pop(-1 if newest else 0))
            wissue()

        kvblocks = [(C_KC, "kc"), (C_VC, "vc"), (C_KSL, "ksl"), (C_KW, "kw"), (C_VSL, "vsl"), (C_VW, "vw")]
        for c0, _ in kvblocks:
            self.wq.append((wcol(self.w_in, c0), (16, 256)))
        for i in range(4):
            self.wq.append((wcol(self.w_in, C_Q + 256 * i), (16, 256)))
        for j in range(4):
            self.wq.append((wcol(self.w_in, C_U + 256 * j), (16, 256)))
            self.wq.append((wcol(self.w_in, C_U + 1024 + 256 * j), (16, 256)))
        self.wq.append((w1v_(self.w1k), (32, 128)))
        self.wq.append((w1v_(self.w1v), (32, 128)))
        for w in (self.w_mk, self.w_mv):
            for i in range(2):
                self.wq.append((wcol(w, 256 * i), (16, 256)))
        for i in range(8):
            self.wq.append((wcol(self.w_out, 256 * i), (16, 256)))
        for i in range(2):
            self.wq.append((wcol(self.w_mq, 256 * i), (16, 256)))
        for i in range(2):
            self.wq.append((wrow(self.w_mo, 256 * i), (2, 2048)))
        for j in range(22):
            self.wq.append((wcol(self.w_gate, 256 * j), (16, 256)))
            self.wq.append((wcol(self.w_up, 256 * j), (16, 256)))
            self.wq.append((wrow(self.w_down, 256 * j), (2, 2048)))

        wg = self.tile(G, "wg", [128, 16, 24], BF16)
        wgb = self.buf("wg")
        dma(POOL, wg[:, :, :], self.w_g.rearrange("(k p) n -> p k n", p=128), w=[wgb])
        w2 = self.tile(G, "w2", [128, 2, 128], BF16)
        w2b = self.buf("w2")
        dma(POOL, w2[:, 0, :], self.w2k, w=[w2b])
        dma(POOL, w2[:, 1, :], self.w2v, pw=[w2b])
        wissue()
        if self.stop == "init":
            self.tap("w0", wslot[0][:, :], [wslotb[0]], [128, 4096], BF16)
            self.tap("w4", wslot[4][:, :], [wslotb[4]], [128, 4096], BF16)
            self.tap("wg", wg[:, :, :], [wgb], [128, 16, 24], BF16)
            self.tap("pvec", pvec[:, :], [pvb], [128, NPV], F32)
            self.tap("cid", cid[:, :], [cidb], [128, 256], BF16)
            self.tap("eps", eps[:, :], [epsb], [128, 4], F32)
            return self.fin()

        rot_proj = Rot([0, 1, 2, 3])
        rot_ssq = Rot([4, 5])
        rot_tr = Rot([6, 7])
        self.flip = 0

        def evac_engine():
            self.flip ^= 1
            return ACT if self.flip else DVE

        def copy_op(E, out, in_, r, w=(), pw=()):
            if E is ACT:
                return op(ACT, lambda: nc.scalar.copy(out=out, in_=in_), r=r, w=w, pw=pw)
            return op(E, lambda: E.eng.tensor_copy(out=out, in_=in_), r=r, w=w, pw=pw)

        def norm_tile_a(xap, xbufs, xs_, xsb_, ss_, ssb_):
            op(ACT, lambda: nc.scalar.activation(out=xs_[:, :], in_=xap, func=AF.Square, accum_out=ss_[:, 0:1]), r=xbufs, w=[xsb_, ssb_])
            op(ACT, lambda: nc.scalar.activation(out=ss_[:, 1:2], in_=ss_[:, 0:1], func=AF.Sqrt, scale=1.0 / D, bias=eps[:, 0:1]), r=[epsb], w=[ssb_])
            op(DVE, lambda: nc.vector.reciprocal(out=ss_[:, 2:3], in_=ss_[:, 1:2]), w=[ssb_])
            op(DVE, lambda: nc.vector.tensor_scalar(out=xs_[:, :], in0=xap, scalar1=ss_[:, 2:3], scalar2=None, op0=ALU.mult), r=list(xbufs) + [ssb_], w=[xsb_])

        def norm_tile_b(gcol, dstT, t0, dstbuf, xs_, xsb_, tail=None):
            for half in range(2):
                bi = rot_tr.next()
                bv = bank[bi][:, :].bitcast(BF16)
                for c8 in range(8):
                    c = half * 8 + c8
                    op(PE, lambda: nc.tensor.transpose(out=bv[:, c8 * 128:(c8 + 1) * 128], in_=xs_[:, c * 128:(c + 1) * 128], identity=ident), r=[xsb_, cidb], w=[bankb[bi]])
                src = bv[:, 0:1024].rearrange("p (c t) -> p c t", c=8)
                dst = dstT[:, half * 8:(half + 1) * 8, t0:t0 + 128]
                g = pvec[:, gcol + half * 8:gcol + half * 8 + 8].unsqueeze(2).broadcast_to([128, 8, 128])
                op(DVE, lambda: nc.vector.tensor_tensor(out=dst, in0=src, in1=g, op=ALU.mult), r=[bankb[bi], pvb], pw=[dstbuf])
                if tail is not None:
                    tl, tlb = tail
                    op(POOL, lambda: nc.gpsimd.tensor_copy(out=tl[:, half * 8:(half + 1) * 8, :], in_=dstT[:, half * 8:(half + 1) * 8, t0 + 96:t0 + 128]), r=[dstbuf], pw=[tlb])

        def norm_tile(xap, xbufs, gcol, dstT, t0, dstbuf, xs_, xsb_, ss_, ssb_, tail=None):
            norm_tile_a(xap, xbufs, xs_, xsb_, ss_, ssb_)
            norm_tile_b(gcol, dstT, t0, dstbuf, xs_, xsb_, tail=tail)

        def rmsnorm_T(S, get_x, ntiles, gcol, dstT, dstbuf_of, tcol_of, tail=None):
            xs = [self.tile(S, f"xs{j}", [128, D], BF16) for j in range(2)]
            xsb = [self.buf(f"xs{j}") for j in range(2)]
            ss = [self.tile(S, f"ss{j}", [128, 4], F32) for j in range(2)]
            ssb = [self.buf(f"ss{j}") for j in range(2)]
            for i in range(ntiles):
                xap, xbufs = get_x(i)
                j = i % 2
                norm_tile(xap, xbufs, gcol, dstT, tcol_of(i), dstbuf_of(i), xs[j], xsb[j], ss[j], ssb[j], tail=tail if i == ntiles - 1 else None)
            self.free(xsb + ssb)

        P1 = self.scope()
        sqt = [self.tile(G, f"sqt{j}", [128, 512], BF16) for j in range(2)]
        sqtb = [self.buf(f"sqt{j}") for j in range(2)]
        rtt = [self.tile(G, f"rtt{j}", [128, 512], F32) for j in range(2)]
        rttb = [self.buf(f"rtt{j}") for j in range(2)]
        rot_sq = Rot([0, 1])

        def pnorm(psi, n, gain_ap, gain_bufs, dst, dstb):
            j = rot_sq.next()
            ps = bank[psi]
            op(ACT, lambda: nc.scalar.activation(out=sqt[j][:, 0:n], in_=ps[:, 0:n], func=AF.Square), r=[bankb[psi]], w=[sqtb[j]])

            def tail():
                si = rot_ssq.next()
                op(PE, lambda: nc.tensor.matmul(bank[si][:, 0:n], lhsT=ones, rhs=sqt[j][:, 0:n], start=True, stop=True), r=[sqtb[j], cidb], w=[bankb[si]])
                op(ACT, lambda: nc.scalar.activation(out=rtt[j][:, 0:n], in_=bank[si][:, 0:n], func=AF.Sqrt, scale=1.0 / DH, bias=eps[:, 0:1]), r=[bankb[si], epsb], w=[rttb[j]])
                op(DVE, lambda: nc.vector.reciprocal(out=rtt[j][:, 0:n], in_=rtt[j][:, 0:n]), w=[rttb[j]])
                op(DVE, lambda: nc.vector.scalar_tensor_tensor(out=dst, in0=ps[:, 0:n], scalar=gain_ap, in1=rtt[j][:, 0:n], op0=ALU.mult, op1=ALU.mult), r=[bankb[psi], rttb[j]] + list(gain_bufs), pw=[dstb])
            self.deferred.append(tail)

        self.deferred = []

        def flush(keep=0):
            while len(self.deferred) > keep:
                self.deferred.pop(0)()

        def proj_fm(wt, wb, cc, actT, actbufs, t0, n):
            psi = rot_proj.next()
            for kc in range(16):
                op(PE, lambda: nc.tensor.matmul(bank[psi][:, 0:n], lhsT=wt[:, kc, cc * 128:(cc + 1) * 128], rhs=actT[:, kc, t0:t0 + n], start=(kc == 0), stop=(kc == 15)), r=[wb] + list(actbufs), w=[bankb[psi]])
            flush(0)
            return psi

        def proj_tm(wt, wb, actT, actbufs, t0, ncols, kcs=16):
            psi = rot_proj.next()
            for kc in range(kcs):
                op(PE, lambda: nc.tensor.matmul(bank[psi][:, 0:ncols], lhsT=actT[:, kc, t0:t0 + 128], rhs=wt[:, kc, 0:ncols], start=(kc == 0), stop=(kc == kcs - 1)), r=[wb] + list(actbufs), w=[bankb[psi]])
            return psi

        A = self.scope()
        cbf = self.tile(A, "cbf", [128, CB_ID], BF16)
        cbb = self.buf("cbf")
        dma(SP, cbf[:, :], self.cbf_d[:, 0:CB_ID], w=[cbb])
        KnT = self.tile(A, "KnT", [128, 4, TC], BF16)
        KnTb = [[self.buf(f"KnT{i}_{c}") for c in range(4)] for i in range(4)]
        Vaug = self.tile(A, "Vaug", [128, 16, 4, NV], BF16)
        Vaugb = [self.buf(f"Vaug{i}") for i in range(16)]
        CK = self.scope()
        kcvcT = self.tile(CK, "kcvcT", [128, 4, TC], BF16)
        kcvcb = [self.buf(f"kcvc{i}") for i in range(4)]
        vab = self.buf("vaug_init")
        op(POOL, lambda: nc.gpsimd.memset(Vaug[:, :, :, :], 1.0), w=Vaugb)

        if self.stop == "p0a":
            self.tap("Vaug", Vaug[:, :, :, :], Vaugb, [128, 16, 4, NV], BF16)
            self.tap("cbf", cbf[:, :], [cbb], [128, CB_ID], BF16)
            return self.fin()
        hTs = [self.tile(P1, f"hT{c}", [128, 16, 512], BF16) for c in range(2)]
        hTb = [self.buf(f"hT{c}") for c in range(2)]
        htail = self.tile(P1, "htail", [128, 16, 32], BF16)
        htailb = self.buf("htail")

        S0 = self.scope()
        xt = [self.tile(S0, f"xt{j}", [128, D], F32) for j in range(2)]
        xtb = [self.buf(f"xt{j}") for j in range(2)]
        xs0 = [self.tile(S0, f"xs{j}", [128, D], BF16) for j in range(2)]
        xs0b = [self.buf(f"xs{j}") for j in range(2)]
        ss0 = [self.tile(S0, f"ss{j}", [128, 4], F32) for j in range(2)]
        ss0b = [self.buf(f"ss{j}") for j in range(2)]

        def xload(gt):
            dma(SP, xt[gt % 2][:, :], self.xc[gt * 128:(gt + 1) * 128, :], w=[xtb[gt % 2]])

        def norm_a(gt):
            if gt + 1 < 16:
                xload(gt + 1)
            j = gt % 2
            norm_tile_a(xt[j][:, :], [xtb[j]], xs0[j], xs0b[j], ss0[j], ss0b[j])

        def norm_b(gt):
            q_, i_ = gt // 4, gt % 4
            j = gt % 2
            norm_tile_b(PV_NMIX, hTs[q_ % 2], i_ * 128, hTb[q_ % 2], xs0[j], xs0b[j], tail=(htail, htailb) if gt == 7 else None)

        kvw = []

        def kv_block(q_, c0, kind, k_):
            hT_, hb_ = hTs[q_ % 2], hTb[q_ % 2]
            if q_ == 0:
                kvw.append(wget())
            wt, wb = kvw[k_]
            tg = q_ * 512
            if kind in ("kc", "vc", "ksl", "kw"):
                for cc in range(2):
                    psi = proj_fm(wt, wb, cc, hT_, [hb_], 0, 512)
                    if kind in ("kc", "vc"):
                        idx = (0 if kind == "kc" else 2) + cc
                        copy_op(evac_engine(), kcvcT[:, idx, tg:tg + 512], bank[psi][:, :], r=[bankb[psi]], pw=[kcvcb[idx]])
                    else:
                        idx = (0 if kind == "ksl" else 2) + cc
                        gcol = PV_KNS if kind == "ksl" else PV_KNW
                        pnorm(psi, 512, pvec[:, gcol:gcol + 1], [pvb], KnT[:, idx, tg:tg + 512], KnTb[idx][q_])
                flush(0)
            else:
                vk = 0 if kind == "vsl" else 2
                for tt in range(4):
                    psi = proj_tm(wt, wb, hT_, [hb_], tt * 128, 256)
                    kt = q_ * 4 + tt
                    E = evac_engine()
                    src = bank[psi][:, 0:256].rearrange("p (g d) -> p g d", g=2)
                    copy_op(E, Vaug[:, kt, vk:vk + 2, 0:128], src, r=[bankb[psi]], pw=[Vaugb[kt]])
            if q_ == 3:
                wrel(1)

        self.mark("p1_start")
        xload(0)
        norm_a(0)
        for gt in range(4):
            norm_b(gt)
            norm_a(gt + 1)
        for q_ in range(4):
            for k_, (c0, kind) in enumerate(kvblocks):
                kv_block(q_, c0, kind, k_)
                if q_ < 3 and k_ < 4:
                    gt = (q_ + 1) * 4 + k_
                    norm_b(gt)
                    if gt + 1 < 16 and k_ < 3:
                        norm_a(gt + 1)
            if q_ < 2:
                norm_a((q_ + 2) * 4)
            flush(0)
            if self.stop == "p0" and q_ == 0:
                return self.fin()
        S0.close()
        self.free(xtb + xs0b + ss0b)
        qnT = self.tile(A, "qnT", [128, NH, TQ], BF16)
        qnTb = [[self.buf(f"qnT{h}_{c}") for c in range(2)] for h in range(NH)]
        gates = self.tile(A, "gates", [128, 8, 24], F32)
        gatesb = self.buf("gates")
        HG = self.scope()
        hglu = self.tile(HG, "hglu", [128, 8, 1056], BF16)
        hglub = [self.buf(f"hglu{c}") for c in range(8)]

        self.mark("q_proj")
        for blk in range(4):
            wt, wb = wget()
            for cc in range(2):
                h = blk * 2 + cc
                for tc in range(2):
                    psi = proj_fm(wt, wb, cc, hTs[tc], [hTb[tc]], 0, 512)
                    pnorm(psi, 512, eps[:, 2:3], [epsb], qnT[:, h, tc * 512:(tc + 1) * 512], qnTb[h][tc])
            flush(0)
            wrel()
        gtmp = self.tile(P1, "gtmp", [128, 24], F32)
        gtmpb = self.buf("gtmp")
        for tt in range(8):
            psi = rot_proj.next()
            for kc in range(16):
                op(PE, lambda: nc.tensor.matmul(bank[psi][:, 0:24], lhsT=hTs[tt // 4][:, kc, (tt % 4) * 128:(tt % 4 + 1) * 128], rhs=wg[:, kc, :], start=(kc == 0), stop=(kc == 15)), r=[wgb, hTb[tt // 4]], w=[bankb[psi]])
            op(DVE, lambda: nc.vector.tensor_tensor(out=gtmp[:, :], in0=bank[psi][:, 0:24], in1=pvec[:, PV_GB:PV_GB + 24], op=ALU.add), r=[bankb[psi], pvb], w=[gtmpb])
            op(ACT, lambda: nc.scalar.activation(out=gates[:, tt, :], in_=gtmp[:, :], func=AF.Sigmoid), r=[gtmpb], pw=[gatesb])
        self.mark("u_proj")
        sg = [self.tile(P1, f"sg{j}", [128, 512], F32) for j in range(2)]
        sgb = [self.buf(f"sg{j}") for j in range(2)]
        rot_sg = Rot([0, 1])
        for j4 in range(4):
            wa, wab = wget()
            wbt, wbb = wget()
            for cc in range(2):
                ch = j4 * 2 + cc
                for seg in range(3):
                    if seg == 0:
                        actT, actb, t0, n, off = htail, [htailb], 0, 32, 0
                    else:
                        actT, actb, t0, n, off = hTs[seg - 1], [hTb[seg - 1]], 0, 512, 32 + (seg - 1) * 512
                    pa = proj_fm(wa, wab, cc, actT, actb, t0, n)
                    pb = proj_fm(wbt, wbb, cc, actT, actb, t0, n)
                    j = rot_sg.next()
                    op(ACT, lambda: nc.scalar.activation(out=sg[j][:, 0:n], in_=bank[pb][:, 0:n], func=AF.Sigmoid), r=[bankb[pb]], w=[sgb[j]])
                    op(DVE, lambda: nc.vector.tensor_tensor(out=hglu[:, ch, off:off + n], in0=bank[pa][:, 0:n], in1=sg[j][:, 0:n], op=ALU.mult), r=[bankb[pa], sgb[j]], pw=[hglub[ch]])
            wrel()
        self.tap("KnT", KnT[:, :, :], [b for l in KnTb for b in l], [128, 4, TC], BF16)
        self.tap("qnT", qnT[:, :, :], [b for l in qnTb for b in l], [128, NH, TQ], BF16)
        self.tap("kcvcT", kcvcT[:, :, :], kcvcb, [128, 4, TC], BF16)
        self.tap("Vaug", Vaug[:, :, :, :], Vaugb, [128, 16, 4, NV], BF16)
        self.tap("gates", gates[:, :, :], [gatesb], [128, 8, 24], F32)
        self.tap("hglu", hglu[:, :, :], hglub, [128, 8, 1056], BF16)
        P1.close()
        self.free(hTb + [htailb, gtmpb] + sgb)
        if self.stop == "p1":
            return self.fin()

        self.mark("compress")
        kcmpT = self.tile(A, "kcmpT", [128, 2, 128], BF16)
        kcmpb = self.buf("kcmpT")
        vcaug = self.tile(A, "vcaug", [128, 2, NVC], BF16)
        vcaugb = self.buf("vcaug")
        C1 = self.scope()
        posbf = self.tile(C1, "posbf", [128, 64], BF16)
        posbfb = self.buf("posbf")
        posb = self.tile(C1, "posb", [128, 2], F32)
        posbb = self.buf("posb")
        h1 = [self.tile(C1, f"h1_{j}", [128, 128], BF16) for j in range(2)]
        h1b = [self.buf(f"h1_{j}") for j in range(2)]
        op(DVE, lambda: nc.vector.memset(kcmpT[:, :, :], 0.0), w=[kcmpb])
        op(DVE, lambda: nc.vector.memset(vcaug[:, :, :], 0.0), w=[vcaugb])
        for g in range(2):
            op(DVE, lambda: nc.vector.tensor_copy(out=vcaug[:, g, 128:NVC], in_=cbf[:, CB_OV:CB_OV + 33]), r=[cbb], pw=[vcaugb])
        op(DVE, lambda: nc.vector.tensor_copy(out=posbf[:, :], in_=pvec[:, PV_POSK:PV_POSK + 64]), r=[pvb], w=[posbfb])
        rot_h1 = Rot([0, 1])
        for kind in range(2):
            w1, w1b = wget()
            pbi = rot_proj.next()
            for l in range(32):
                op(PE, lambda: nc.tensor.matmul(bank[pbi][:, 0:1], lhsT=w1[:, l, :], rhs=posbf[:, kind * 32 + l:kind * 32 + l + 1], start=(l == 0), stop=(l == 31)), r=[w1b, posbfb], w=[bankb[pbi]])
            op(DVE, lambda: nc.vector.tensor_copy(out=posb[:, kind:kind + 1], in_=bank[pbi][:, 0:1]), r=[bankb[pbi]], pw=[posbb])
            for g in range(2):
                psi = rot_proj.next()
                src = kcvcT[:, kind * 2 + g, :]
                for l in range(32):
                    op(PE, lambda: nc.tensor.matmul(bank[psi][:, 0:127], lhsT=w1[:, l, :], rhs=src[:, l:l + 16 * 126 + 1:16], start=(l == 0), stop=(l == 31)), r=[w1b, kcvcb[kind * 2 + g]], w=[bankb[psi]])
                j = rot_h1.next()
                op(ACT, lambda: nc.scalar.activation(out=h1[j][:, 0:127], in_=bank[psi][:, 0:127], func=AF.Silu, bias=posb[:, kind:kind + 1]), r=[bankb[psi], posbb], w=[h1b[j]])
                ps2 = rot_proj.next()
                if kind == 0:
                    op(PE, lambda: nc.tensor.matmul(bank[ps2][:, 0:127], lhsT=w2[:, 0, :], rhs=h1[j][:, 0:127], start=True, stop=True), r=[w2b, h1b[j]], w=[bankb[ps2]])
                    pnorm(ps2, 127, pvec[:, PV_KNC:PV_KNC + 1], [pvb], kcmpT[:, g, 0:127], kcmpb)
                    flush(0)
                else:
                    op(PE, lambda: nc.tensor.matmul(bank[ps2][0:127, 0:128], lhsT=h1[j][:, 0:127], rhs=w2[:, 1, :], start=True, stop=True), r=[w2b, h1b[j]], w=[bankb[ps2]])
                    copy_op(evac_engine(), vcaug[0:127, g, 0:128], bank[ps2][0:127, 0:128], r=[bankb[ps2]], pw=[vcaugb])
            wrel()
        self.tap("kcmpT", kcmpT[:, :, :], [kcmpb], [128, 2, 128], BF16)
        self.tap("vcaug", vcaug[:, :, :], [vcaugb], [128, 2, NVC], BF16)
        C1.close()
        self.free([posbfb, posbb] + h1b)
        if self.stop == "p1c":
            return self.fin()
        CK.close()
        self.free(kcvcb)

        self.mark("attn")
        CT = self.scope()
        catA = self.tile(CT, "catA", [128, 8, TQ], BF16)
        catTb = [[self.buf(f"catT{c}_{t}") for t in range(2)] for c in range(16)]
        AT = self.scope()
        RB = self.tile(AT, "RB", [128, 16, 512], BF16)
        RBb = [self.buf(f"RB{i}") for i in range(16)]
        dma(SP, RB[:, :, :], self.rb_d.rearrange("p (i n) -> p i n", i=16), w=RBb, owner=RBb[0])
        bonus = self.tile(AT, "bonus", [128, 8, 32], F32)
        bonusb = self.buf("bonus")
        dma(SP, bonus[:, :, :], self.bonus_d.rearrange("p (i n) -> p i n", i=8), w=[bonusb])
        Pt = [self.tile(AT, f"Pt{j}", [128, 512], BF16) for j in range(3)]
        Ptb = [self.buf(f"Pt{j}") for j in range(3)]
        rot_P = Rot([0, 1, 2])
        rot_S = Rot([0, 1, 6])
        rot_O = Rot([(2, 3), (4, 5)])
        ocomb = [self.tile(AT, f"ocomb{j}", [128, 4, 4, 128], F32) for j in range(1)]
        ocombb = [self.buf(f"ocomb{j}") for j in range(1)]
        ocb = [self.tile(AT, f"ocb{j}", [128, 4, 4, 128], BF16) for j in range(1)]
        ocbb = [self.buf(f"ocb{j}") for j in range(1)]
        imp = self.tile(AT, "imp", [128, 4, 32], F32)
        impb = self.buf("imp")
        sc2 = self.tile(AT, "sc2", [128, 4, 32], F32)
        sc2b = self.buf("sc2")
        m8 = self.tile(AT, "m8", [128, 16], F32)
        m8b = self.buf("m8")
        selbf = self.tile(AT, "selbf", [128, 4, 32], BF16)
        selbfb = self.buf("selbf")
        rdt = [self.tile(AT, f"rd{j}", [128, 8], F32) for j in range(4)]
        rdb = [self.buf(f"rd{j}") for j in range(4)]
        rot_rd = Rot([0, 1, 2, 3])

        def attn_tile(Kt, Kb, Qt, Qb, LBap, RBi, mask_ap):
            si = rot_S.next()
            n_extra = 1 if mask_ap is not None else 0
            op(PE, lambda: nc.tensor.matmul(bank[si][:, :], lhsT=Kt, rhs=Qt, start=True, stop=False), r=list(Kb) + list(Qb), w=[bankb[si]])
            op(PE, lambda: nc.tensor.matmul(bank[si][:, :], lhsT=LBap, rhs=RB[:, RBi, :], start=False, stop=(n_extra == 0)), r=[cbb, RBb[RBi]], w=[bankb[si]])
            if mask_ap is not None:
                op(PE, lambda: nc.tensor.matmul(bank[si][:, :], lhsT=ident, rhs=mask_ap, start=False, stop=True), r=[cbb, cidb], w=[bankb[si]])
            pj = rot_P.next()
            op(ACT, lambda: nc.scalar.activation(out=Pt[pj][:, :], in_=bank[si][:, :], func=AF.Exp), r=[bankb[si]], w=[Ptb[pj]])
            return pj

        def pv(pj, Obanks, Vap, Vb, nv, first, last):
            for j in range(4):
                ob = Obanks[j // 2]
                o0 = (j % 2) * nv
                op(PE, lambda: nc.tensor.matmul(bank[ob][:, o0:o0 + nv], lhsT=Pt[pj][:, j * 128:(j + 1) * 128], rhs=Vap, start=(first and j % 2 == 0), stop=last, skip_group_check=True), r=[Ptb[pj]] + list(Vb), w=[bankb[ob]])

        def finish(Obanks, nv, h, tc, br, oc, mode):
            hl = h % 4
            rj = rot_rd.next()
            rd = rdt[rj]
            for b2 in range(2):
                ob = Obanks[b2]
                op(DVE, lambda: nc.vector.tensor_scalar(out=rd[:, 2 * b2:2 * b2 + 2], in0=bank[ob][:, 128:128 + nv + 1:nv], scalar1=1e-30, scalar2=None, op0=ALU.max), r=[bankb[ob]], pw=[rdb[rj]])
            op(DVE, lambda: nc.vector.reciprocal(out=rd[:, 0:4], in_=rd[:, 0:4]), w=[rdb[rj]])
            gcol = h * 3 + br
            op(DVE, lambda: nc.vector.tensor_tensor(out=rd[:, 4:8], in0=rd[:, 0:4], in1=gates[:, tc * 4:tc * 4 + 4, gcol], op=ALU.mult), r=[gatesb], w=[rdb[rj]])
            for j in range(4):
                ob = Obanks[j // 2]
                o0 = (j % 2) * nv
                src = bank[ob][:, o0:o0 + 128]
                f = rd[:, 4 + j:5 + j]
                if mode == "first":
                    op(DVE, lambda: nc.vector.tensor_scalar(out=ocomb[oc][:, j, hl, :], in0=src, scalar1=f, scalar2=None, op0=ALU.mult), r=[bankb[ob], rdb[rj]], pw=[ocombb[oc]])
                elif mode == "add":
                    op(DVE, lambda: nc.vector.scalar_tensor_tensor(out=ocomb[oc][:, j, hl, :], in0=src, scalar=f, in1=ocomb[oc][:, j, hl, :], op0=ALU.mult, op1=ALU.add), r=[bankb[ob], rdb[rj]], w=[ocombb[oc]])
                else:
                    op(DVE, lambda: nc.vector.scalar_tensor_tensor(out=ocb[oc][:, j, hl, :], in0=src, scalar=f, in1=ocomb[oc][:, j, hl, :], op0=ALU.mult, op1=ALU.add), r=[bankb[ob], rdb[rj], ocombb[oc]], pw=[ocbb[oc]])
            return rj

        LBS = lambda i: cbf[:, CB_LBS + i * 128:CB_LBS + (i + 1) * 128]
        LBW = lambda i: cbf[:, CB_LBW + i * 128:CB_LBW + (i + 1) * 128]
        LBC = cbf[:, CB_LBC:CB_LBC + 128]
        CSm = lambda m: cbf[:, CB_CS + 384 - 128 * m:CB_CS + 384 - 128 * m + 512]
        WSm = lambda m: cbf[:, CB_WS + 384 - 128 * m:CB_WS + 384 - 128 * m + 512]

        def run_stream(jobs, L=2):
            pend = []
            for n in range(len(jobs) + L):
                if n < len(jobs):
                    pend.append((jobs[n], jobs[n][0]()))
                if n >= L:
                    job, pj = pend.pop(0)
                    job[1](pj)

        for g in range(2):
            for tc in range(2):
                oc = 0
                Q = lambda h: qnT[:, h, tc * 512:(tc + 1) * 512]
                jobs = []
                for hl in range(4):
                    h = 4 * g + hl
                    Ob = rot_O.next()

                    def sc(h=h):
                        return attn_tile(kcmpT[:, g, :], [kcmpb], Q(h), [qnTb[h][tc]], LBC, h * 2 + tc, cbf[:, CB_CM + tc * 512:CB_CM + (tc + 1) * 512])

                    def pvf(pj, h=h, hl=hl, Ob=Ob):
                        pv(pj, Ob, vcaug[:, g, :], [vcaugb], NVC, True, True)
                        rj = finish(Ob, NVC, h, tc, 0, oc, "first")
                        rd = rdt[rj]
                        for j in range(4):
                            ob = Ob[j // 2]
                            o0 = (j % 2) * NVC + 129
                            if hl == 0:
                                op(DVE, lambda: nc.vector.tensor_scalar(out=imp[:, j, :], in0=bank[ob][:, o0:o0 + 32], scalar1=rd[:, j:j + 1], scalar2=None, op0=ALU.mult), r=[bankb[ob], rdb[rj]], w=[impb] if j == 0 else [], pw=[impb] if j else [])
                            else:
                                op(DVE, lambda: nc.vector.scalar_tensor_tensor(out=imp[:, j, :], in0=bank[ob][:, o0:o0 + 32], scalar=rd[:, j:j + 1], in1=imp[:, j, :], op0=ALU.mult, op1=ALU.add), r=[bankb[ob], rdb[rj]], w=[impb])
                    jobs.append((sc, pvf))
                for hl in range(4):
                    h = 4 * g + hl
                    Ob = rot_O.next()
                    ks = list(range(4 + 4 * tc, 12 + 4 * tc))
                    for n_, i in enumerate(ks):
                        m = i - (4 + 4 * tc)
                        mask = WSm(m) if m < 4 else CSm(m - 4)

                        def sc(h=h, i=i, mask=mask):
                            return attn_tile(KnT[:, 2 + g, i * 128:(i + 1) * 128], [KnTb[2 + g][i // 4]], Q(h), [qnTb[h][tc]], LBW(i), h * 2 + tc, mask)

                        def pvf(pj, h=h, i=i, n_=n_, Ob=Ob, last=(n_ == len(ks) - 1)):
                            pv(pj, Ob, Vaug[:, i, 2 + g, :], [Vaugb[i]], NV, n_ == 0, last)
                            if last:
                                finish(Ob, NV, h, tc, 2, oc, "add")
                        jobs.append((sc, pvf))
                run_stream(jobs[:4])
                op(DVE, lambda: nc.vector.tensor_tensor(out=imp[:, :, :], in0=imp[:, :, :], in1=bonus[:, tc * 4:tc * 4 + 4, :], op=ALU.add), r=[bonusb], w=[impb])
                tb = rot_tr.next()
                tbv = bank[tb][:, :].bitcast(BF16)
                for j in range(4):
                    op(DVE, lambda: nc.vector.max(out=m8[:, 0:8], in_=imp[:, j, :]), r=[impb], w=[m8b])
                    op(DVE, lambda: nc.vector.match_replace(out=sc2[:, j, :], in_to_replace=m8[:, 0:8], in_values=imp[:, j, :], imm_value=-3.0e38), r=[impb, m8b], w=[sc2b])
                    op(DVE, lambda: nc.vector.max(out=m8[:, 8:16], in_=sc2[:, j, :]), r=[sc2b], w=[m8b])
                    op(DVE, lambda: nc.vector.tensor_scalar(out=selbf[:, j, :], in0=imp[:, j, :], scalar1=m8[:, 15:16], scalar2=None, op0=ALU.is_ge), r=[impb, m8b], w=[selbfb])
                if g == 0 and tc == 1:
                    self.tap("imp", imp[:, :, :], [impb], [128, 4, 32], F32)
                    self.tap("selbf", selbf[:, :, :], [selbfb], [128, 4, 32], BF16)
                run_stream(jobs[4:])
                for j in range(4):
                    op(PE, lambda: nc.tensor.transpose(out=tbv[0:32, j * 128:(j + 1) * 128], in_=selbf[:, j, :], identity=ident), r=[selbfb, cidb], w=[bankb[tb]])
                for hl in range(4):
                    h = 4 * g + hl
                    copy_op(ACT, RB[0:32, h * 2 + tc, :], tbv[0:32, 0:512], r=[bankb[tb]], w=[RBb[h * 2 + tc]])
                jobs = []
                for hl in range(4):
                    h = 4 * g + hl
                    Ob = rot_O.next()
                    ks = list(range(0, 12 + 4 * tc))
                    for n_, i in enumerate(ks):
                        m = i - (8 + 4 * tc)
                        mask = CSm(m) if m >= 0 else None

                        def sc(h=h, i=i, mask=mask):
                            return attn_tile(KnT[:, g, i * 128:(i + 1) * 128], [KnTb[g][i // 4]], Q(h), [qnTb[h][tc]], LBS(i), h * 2 + tc, mask)

                        def pvf(pj, h=h, i=i, n_=n_, Ob=Ob, last=(n_ == len(ks) - 1)):
                            pv(pj, Ob, Vaug[:, i, g, :], [Vaugb[i]], NV, n_ == 0, last)
                            if last:
                                finish(Ob, NV, h, tc, 1, oc, "last")
                        jobs.append((sc, pvf))
                run_stream(jobs)
                for hl in range(4):
                    h = 4 * g + hl
                    tb2 = rot_tr.next()
                    tv = bank[tb2][:, :].bitcast(BF16)
                    for j in range(4):
                        op(PE, lambda: nc.tensor.transpose(out=tv[:, j * 128:(j + 1) * 128], in_=ocb[oc][:, j, hl, :], identity=ident), r=[ocbb[oc], cidb], w=[bankb[tb2]])
                    copy_op(evac_engine(), catA[:, h, tc * 512:(tc + 1) * 512], tv[:, 0:512], r=[bankb[tb2]], w=[catTb[h][tc]])
        self.tap("catT_nsa", catA[:, :, :], [catTb[c][t] for c in range(8) for t in range(2)], [128, 8, TQ], BF16)
        if self.stop == "p2":
            return self.fin()
        AT.close()
        self.free(RBb + [bonusb, impb, sc2b, m8b, selbfb] + Ptb + ocombb + ocbb + rdb)
        A.close()
        self.free([b for l in KnTb for b in l] + Vaugb + [b for l in qnTb for b in l] + [gatesb, kcmpb, vcaugb, cbb])
        catB = self.tile(CT, "catB", [128, 8, TQ], BF16)

        self.mark("conv")
        CV = self.scope()
        dg = [self.tile(CV, f"dg{j}", [128, 31, 128], BF16) for j in range(2)]
        dgb = [self.buf(f"dg{j}") for j in range(2)]
        cv = self.tile(CV, "cv", [128, 8, TQ], F32)
        cvb_ = [[self.buf(f"cv{c}_{t}") for t in range(2)] for c in range(8)]
        cvs = [self.tile(CV, f"cvs{j}", [128, 2, 512], BF16) for j in range(2)]
        cvsb = [self.buf(f"cvs{j}") for j in range(2)]
        rot_cvs = Rot([0, 1])
        stat_bank = {(0, 0): 4, (0, 1): 5, (1, 0): 6, (1, 1): 7}
        mean = self.tile(CV, "mean", [128, 512], F32)
        meanb = self.buf("mean")
        rstd = self.tile(CV, "rstd", [128, 512], F32)
        rstdb = self.buf("rstd")
        t1 = [self.tile(CV, f"t1_{j}", [128, 512], F32) for j in range(2)]
        t1b = [self.buf(f"t1_{j}") for j in range(2)]
        ident_b = ident.unsqueeze(1).broadcast_to([128, 31, 128])

        def ln_prep(tc):
            sb0, sb1 = stat_bank[(tc, 0)], stat_bank[(tc, 1)]
            op(DVE, lambda: nc.vector.tensor_scalar(out=mean[:, :], in0=bank[sb0][:, :], scalar1=1.0 / 1024, scalar2=None, op0=ALU.mult), r=[bankb[sb0]], w=[meanb])
            op(DVE, lambda: nc.vector.tensor_tensor(out=rstd[:, :], in0=mean[:, :], in1=mean[:, :], op=ALU.mult), r=[meanb], w=[rstdb])
            op(DVE, lambda: nc.vector.scalar_tensor_tensor(out=rstd[:, :], in0=bank[sb1][:, :], scalar=1.0 / 1024, in1=rstd[:, :], op0=ALU.mult, op1=ALU.subtract), r=[bankb[sb1]], w=[rstdb])
            op(ACT, lambda: nc.scalar.activation(out=rstd[:, :], in_=rstd[:, :], func=AF.Sqrt, bias=eps[:, 1:2]), r=[epsb], w=[rstdb])
            op(DVE, lambda: nc.vector.reciprocal(out=rstd[:, :], in_=rstd[:, :]), w=[rstdb])

        def ln_chunk(tc, ch):
            j = ch % 2
            op(DVE, lambda: nc.vector.tensor_tensor(out=t1[j][:, :], in0=cv[:, ch, tc * 512:(tc + 1) * 512], in1=mean[:, :], op=ALU.subtract), r=[cvb_[ch][tc], meanb], w=[t1b[j]])
            op(DVE, lambda: nc.vector.tensor_tensor(out=t1[j][:, :], in0=t1[j][:, :], in1=rstd[:, :], op=ALU.mult), r=[rstdb], w=[t1b[j]])
            op(ACT, lambda: nc.scalar.activation(out=catB[:, ch, tc * 512:(tc + 1) * 512], in_=t1[j][:, :], func=AF.Silu, scale=pvec[:, PV_LNG + ch:PV_LNG + ch + 1], bias=pvec[:, PV_LNB + ch:PV_LNB + ch + 1]), r=[t1b[j], pvb], w=[catTb[8 + ch][tc]])

        pend = []

        def build_dg(n):
            ch_ = n % 8
            wv = pvec[:, PV_CW + ch_ * 31:PV_CW + ch_ * 31 + 31].unsqueeze(2).broadcast_to([128, 31, 128])
            op(DVE, lambda: nc.vector.tensor_tensor(out=dg[n % 2][:, :, :], in0=ident_b, in1=wv, op=ALU.mult), r=[cidb, pvb], w=[dgb[n % 2]])

        it_ = 0
        build_dg(0)
        for tc in range(2):
            for ch in range(8):
                dj = it_ % 2
                if it_ + 1 < 16:
                    build_dg(it_ + 1)
                it_ += 1
                psi = rot_proj.next()
                for j in range(31):
                    o = 2 + j + tc * 512
                    op(PE, lambda: nc.tensor.matmul(bank[psi][:, :], lhsT=dg[dj][:, j, :], rhs=hglu[:, ch, o:o + 512], start=(j == 0), stop=(j == 30)), r=[dgb[dj], hglub[ch]], w=[bankb[psi]])
                while pend:
                    pend.pop(0)()
                op(ACT, lambda: nc.scalar.activation(out=cv[:, ch, tc * 512:(tc + 1) * 512], in_=bank[psi][:, :], func=AF.Identity, bias=pvec[:, PV_CB + ch:PV_CB + ch + 1]), r=[bankb[psi], pvb], w=[cvb_[ch][tc]])
                sj = rot_cvs.next()
                op(DVE, lambda: nc.vector.tensor_copy(out=cvs[sj][:, 0, :], in_=cv[:, ch, tc * 512:(tc + 1) * 512]), r=[cvb_[ch][tc]], w=[cvsb[sj]])
                op(ACT, lambda: nc.scalar.activation(out=cvs[sj][:, 1, :], in_=cv[:, ch, tc * 512:(tc + 1) * 512], func=AF.Square), r=[cvb_[ch][tc]], pw=[cvsb[sj]])

                def stats(tc=tc, ch=ch, sj=sj):
                    for kind in range(2):
                        sb_ = stat_bank[(tc, kind)]
                        op(PE, lambda: nc.tensor.matmul(bank[sb_][:, :], lhsT=ones, rhs=cvs[sj][:, kind, :], start=(ch == 0), stop=(ch == 7)), r=[cvsb[sj], cidb], w=[bankb[sb_]])
                pend.append(stats)
                if tc == 1:
                    ln_chunk(0, ch)
            while pend:
                pend.pop(0)()
            if tc == 0:
                ln_prep(0)
        ln_prep(1)
        for ch in range(8):
            ln_chunk(1, ch)
        self.tap("cv", cv[:, :, :], [b for l in cvb_ for b in l], [128, 8, TQ], F32)
        self.tap("catB", catB[:, :, :], [catTb[c][t] for c in range(8, 16) for t in range(2)], [128, 8, TQ], BF16)
        if self.stop == "p3":
            return self.fin()
        CV.close()
        self.free(dgb + [b for l in cvb_ for b in l] + cvsb + [meanb, rstdb] + t1b)
        HG.close()
        self.free(hglub)

        self.mark("memkv")
        MK = self.scope()
        hmT = self.tile(MK, "hmT", [128, 16, 256], BF16)
        hmTb = self.buf("hmT")
        kmnT = self.tile(MK, "kmnT", [128, 4, 256], BF16)
        kmnTb = [self.buf(f"kmnT{h}") for h in range(4)]
        vmaug = self.tile(MK, "vmaug", [128, 2, 4, NV], BF16)
        vmaugb = self.buf("vmaug")
        S1m = self.scope()
        mt_ = [self.tile(S1m, f"memt{j}", [128, D], F32) for j in range(2)]
        mtb = [self.buf(f"memt{j}") for j in range(2)]
        for j in range(2):
            dma(SP, mt_[j][:, :], self.memb[j * 128:(j + 1) * 128, :], w=[mtb[j]])
        rmsnorm_T(S1m, lambda i: (mt_[i][:, :], [mtb[i]]), 2, PV_NMKV, hmT, lambda i: hmTb, lambda i: i * 128)
        S1m.close()
        self.free(mtb)
        op(POOL, lambda: nc.gpsimd.memset(vmaug[:, :, :, :], 1.0), w=[vmaugb])
        for blk in range(2):
            wt, wb = wget()
            for cc in range(2):
                h = blk * 2 + cc
                psi = proj_fm(wt, wb, cc, hmT, [hmTb], 0, 256)
                pnorm(psi, 256, pvec[:, PV_MKN:PV_MKN + 1], [pvb], kmnT[:, h, :], kmnTb[h])
            flush(0)
            wrel()
        for blk in range(2):
            wt, wb = wget()
            for mt in range(2):
                psi = proj_tm(wt, wb, hmT, [hmTb], mt * 128, 256)
                src = bank[psi][:, 0:256].rearrange("p (g d) -> p g d", g=2)
                copy_op(evac_engine(), vmaug[:, mt, blk * 2:blk * 2 + 2, 0:128], src, r=[bankb[psi]], pw=[vmaugb])
            wrel()

        self.mark("w_out")
        xres = self.tile(G, "xres", [128, 8, D], F32)
        xrb = [[self.buf(f"xres{t}_{c}") for c in range(8)] for t in range(8)]
        for tt in range(8):
            dma(SP, xres[:, tt, :], self.xc[TQ + tt * 128:TQ + (tt + 1) * 128, :], w=xrb[tt], owner=xrb[tt][0])
        for cb in range(8):
            wt, wb = wget()
            for tt in range(8):
                psi = rot_proj.next()
                for kc in range(16):
                    cat_ = catA if kc < 8 else catB
                    op(PE, lambda: nc.tensor.matmul(bank[psi][:, 0:256], lhsT=cat_[:, kc % 8, tt * 128:(tt + 1) * 128], rhs=wt[:, kc, 0:256], start=(kc == 0), stop=(kc == 15)), r=[wb, catTb[kc][tt // 4]], w=[bankb[psi]])
                dst = xres[:, tt, cb * 256:(cb + 1) * 256]
                op(DVE, lambda: nc.vector.tensor_tensor(out=dst, in0=bank[psi][:, 0:256], in1=dst, op=ALU.add), r=[bankb[psi]], w=[xrb[tt][cb]])
            wrel()
        self.tap("x1", xres[:, :, :], [b for l in xrb for b in l], [128, 8, D], F32)
        if self.stop == "p4":
            return self.fin()
        CT.close()
        self.free([b for l in catTb for b in l])

        self.mark("mem")
        M = self.scope()
        HH = self.scope()
        h2T = self.tile(HH, "h2T", [128, 16, TQ], BF16)
        h2Tb = [self.buf(f"h2T{c}") for c in range(2)]
        S1 = self.scope()
        rmsnorm_T(S1, lambda i: (xres[:, i, :], xrb[i]), 8, PV_NMQ, h2T, lambda i: h2Tb[i // 4], lambda i: i * 128)
        S1.close()
        qmnT = self.tile(M, "qmnT", [128, 4, TQ], BF16)
        qmnTb = [[self.buf(f"qmnT{h}_{t}") for t in range(2)] for h in range(4)]
        omT = self.tile(M, "omT", [128, 4, TQ], BF16)
        omTb = [[self.buf(f"omT{h}_{t}") for t in range(2)] for h in range(4)]
        for blk in range(2):
            wt, wb = wget()
            for cc in range(2):
                h = blk * 2 + cc
                for tc in range(2):
                    psi = proj_fm(wt, wb, cc, h2T, [h2Tb[tc]], tc * 512, 512)
                    pnorm(psi, 512, eps[:, 3:4], [epsb], qmnT[:, h, tc * 512:(tc + 1) * 512], qmnTb[h][tc])
            flush(0)
            wrel()
        Pm = [self.tile(M, f"Pm{j}", [128, 512], BF16) for j in range(3)]
        Pmb = [self.buf(f"Pm{j}") for j in range(3)]
        omem = [self.tile(M, f"omem{j}", [128, 4, 128], BF16) for j in range(2)]
        omemb = [self.buf(f"omem{j}") for j in range(2)]
        rdm = [self.tile(M, f"rdm{j}", [128, 4], F32) for j in range(2)]
        rdmb = [self.buf(f"rdm{j}") for j in range(2)]
        jobs = []
        it = 0
        for h in range(4):
            for tc in range(2):
                Ob = rot_O.next()
                oj = it % 2
                it += 1
                for mt in range(2):
                    def sc(h=h, tc=tc, mt=mt):
                        si = rot_S.next()
                        op(PE, lambda: nc.tensor.matmul(bank[si][:, :], lhsT=kmnT[:, h, mt * 128:(mt + 1) * 128], rhs=qmnT[:, h, tc * 512:(tc + 1) * 512], start=True, stop=True), r=[kmnTb[h], qmnTb[h][tc]], w=[bankb[si]])
                        pj = rot_P.next()
                        op(ACT, lambda: nc.scalar.activation(out=Pm[pj][:, :], in_=bank[si][:, :], func=AF.Exp), r=[bankb[si]], w=[Pmb[pj]])
                        return pj

                    def pvf(pj, h=h, tc=tc, mt=mt, Ob=Ob, oj=oj):
                        for j in range(4):
                            ob = Ob[j // 2]
                            o0 = (j % 2) * NV
                            op(PE, lambda: nc.tensor.matmul(bank[ob][:, o0:o0 + NV], lhsT=Pm[pj][:, j * 128:(j + 1) * 128], rhs=vmaug[:, mt, h, :], start=(mt == 0 and j % 2 == 0), stop=(mt == 1), skip_group_check=True), r=[Pmb[pj], vmaugb], w=[bankb[ob]])
                        if mt == 0:
                            return
                        for b2 in range(2):
                            ob = Ob[b2]
                            op(DVE, lambda: nc.vector.tensor_scalar(out=rdm[oj][:, 2 * b2:2 * b2 + 2], in0=bank[ob][:, 128:128 + NV + 1:NV], scalar1=1e-30, scalar2=None, op0=ALU.max), r=[bankb[ob]], pw=[rdmb[oj]] if b2 else [], w=[rdmb[oj]] if not b2 else [])
                        op(DVE, lambda: nc.vector.reciprocal(out=rdm[oj][:, :], in_=rdm[oj][:, :]), w=[rdmb[oj]])
                        for j in range(4):
                            ob = Ob[j // 2]
                            o0 = (j % 2) * NV
                            op(DVE, lambda: nc.vector.tensor_scalar(out=omem[oj][:, j, :], in0=bank[ob][:, o0:o0 + 128], scalar1=rdm[oj][:, j:j + 1], scalar2=None, op0=ALU.mult), r=[bankb[ob], rdmb[oj]], w=[omemb[oj]] if j == 0 else [], pw=[omemb[oj]] if j else [])
                        while pend_tr:
                            pend_tr.pop(0)()

                        def trf(h=h, tc=tc, oj=oj):
                            tb2 = rot_tr.next()
                            tv = bank[tb2][:, :].bitcast(BF16)
                            for j in range(4):
                                op(PE, lambda: nc.tensor.transpose(out=tv[:, j * 128:(j + 1) * 128], in_=omem[oj][:, j, :], identity=ident), r=[omemb[oj], cidb], w=[bankb[tb2]])
                            copy_op(evac_engine(), omT[:, h, tc * 512:(tc + 1) * 512], tv[:, 0:512], r=[bankb[tb2]], w=[omTb[h][tc]])
                        pend_tr.append(trf)
                    jobs.append((sc, pvf))
        pend_tr = []
        run_stream(jobs)
        while pend_tr:
            pend_tr.pop(0)()
        wmo = [wget(), wget()]
        h3T, h3Tb = h2T, h2Tb
        S3 = self.scope()
        xs3 = [self.tile(S3, f"xs3{j}", [128, D], BF16) for j in range(2)]
        xs3b = [self.buf(f"xs3{j}") for j in range(2)]
        ss3 = [self.tile(S3, f"ss3{j}", [128, 4], F32) for j in range(2)]
        ss3b = [self.buf(f"ss3{j}") for j in range(2)]

        def n3b(t_):
            norm_tile_b(PV_NFFN, h3T, t_ * 128, h3Tb[t_ // 4], xs3[t_ % 2], xs3b[t_ % 2])

        for tt in range(8):
            if tt >= 2:
                n3b(tt - 2)
            for cb4 in range(4):
                psi = rot_proj.next()
                for kc in range(4):
                    wt, wb = wmo[kc // 2]
                    op(PE, lambda: nc.tensor.matmul(bank[psi][:, :], lhsT=omT[:, kc, tt * 128:(tt + 1) * 128], rhs=wt[:, kc % 2, cb4 * 512:(cb4 + 1) * 512], start=(kc == 0), stop=(kc == 3)), r=[wb, omTb[kc][tt // 4]], w=[bankb[psi]])
                dst = xres[:, tt, cb4 * 512:(cb4 + 1) * 512]
                op(DVE, lambda: nc.vector.tensor_tensor(out=dst, in0=bank[psi][:, :], in1=dst, op=ALU.add), r=[bankb[psi]], w=[xrb[tt][2 * cb4], xrb[tt][2 * cb4 + 1]])
            norm_tile_a(xres[:, tt, :], xrb[tt], xs3[tt % 2], xs3b[tt % 2], ss3[tt % 2], ss3b[tt % 2])
        n3b(6)
        n3b(7)
        wrel()
        S3.close()
        self.free(xs3b + ss3b)
        self.tap("x2", xres[:, :, :], [b for l in xrb for b in l], [128, 8, D], F32)
        if self.stop == "p5":
            return self.fin()
        M.close()
        self.free([b for l in qmnTb for b in l] + [b for l in omTb for b in l] + Pmb + omemb + rdmb)
        MK.close()
        self.free([hmTb, vmaugb] + kmnTb)

        self.mark("ffn")
        Fz = self.scope()
        actT = [self.tile(Fz, f"actT{j}", [128, 2, TQ], BF16) for j in range(2)]
        actTb = [[[self.buf(f"actT{j}_{c}_{t}") for t in range(2)] for c in range(2)] for j in range(2)]
        sgt = [self.tile(Fz, f"sgt{j}", [128, 512], BF16) for j in range(2)]
        sgtb = [self.buf(f"sgt{j}") for j in range(2)]
        rot_g = Rot([4, 5])
        rot_u = Rot([6, 7])
        rot_sgt = Rot([0, 1])
        dheld = []
        for jb in range(22):
            wgt, wgb_ = wget()
            wut, wub = wget()
            aj = jb % 2
            for cc in range(2):
                for tc in range(2):
                    pg = rot_g.next()
                    pu = rot_u.next()
                    for kc in range(16):
                        op(PE, lambda: nc.tensor.matmul(bank[pg][:, :], lhsT=wgt[:, kc, cc * 128:(cc + 1) * 128], rhs=h3T[:, kc, tc * 512:(tc + 1) * 512], start=(kc == 0), stop=(kc == 15)), r=[wgb_, h3Tb[tc]], w=[bankb[pg]])
                    for kc in range(16):
                        op(PE, lambda: nc.tensor.matmul(bank[pu][:, :], lhsT=wut[:, kc, cc * 128:(cc + 1) * 128], rhs=h3T[:, kc, tc * 512:(tc + 1) * 512], start=(kc == 0), stop=(kc == 15)), r=[wub, h3Tb[tc]], w=[bankb[pu]])
                    sj = rot_sgt.next()
                    op(ACT, lambda: nc.scalar.activation(out=sgt[sj][:, :], in_=bank[pg][:, :], func=AF.Silu), r=[bankb[pg]], w=[sgtb[sj]])
                    op(DVE, lambda: nc.vector.tensor_tensor(out=actT[aj][:, cc, tc * 512:(tc + 1) * 512], in0=bank[pu][:, :], in1=sgt[sj][:, :], op=ALU.mult), r=[bankb[pu], sgtb[sj]], w=[actTb[aj][cc][tc]])
            wrel(2, newest=True)
            wdt, wdb = wget()
            dheld.append((aj, wdt, wdb))
            if jb % 2 == 0:
                continue
            for tt in range(8):
                for cb4 in range(4):
                    psi = rot_proj.next()
                    n_ = 0
                    for (a_, wd_, wdb_) in dheld:
                        for cc in range(2):
                            op(PE, lambda: nc.tensor.matmul(bank[psi][:, :], lhsT=actT[a_][:, cc, tt * 128:(tt + 1) * 128], rhs=wd_[:, cc, cb4 * 512:(cb4 + 1) * 512], start=(n_ == 0), stop=(n_ == 3)), r=[wdb_, actTb[a_][cc][tt // 4]], w=[bankb[psi]])
                            n_ += 1
                    dst = xres[:, tt, cb4 * 512:(cb4 + 1) * 512]
                    op(DVE, lambda: nc.vector.tensor_tensor(out=dst, in0=bank[psi][:, :], in1=dst, op=ALU.add), r=[bankb[psi]], w=[xrb[tt][2 * cb4], xrb[tt][2 * cb4 + 1]])
            dheld = []
            wrel()
        self.mark("out")
        for tt in range(8):
            ob_ = self.buf(f"out{tt}")
            dma(SP, self.y[tt * 128:(tt + 1) * 128, :], xres[:, tt, :], r=xrb[tt], owner=ob_)
            self.outtoks.append((ob_.dsem, ob_.dcnt))
        for sem, val in self.outtoks:
            SP.eng.wait_ge(sem.h, val)
        return nc


def _bf(a):
    return np.ascontiguousarray(a.astype(np.float32)).astype(NPBF)


def _consts(s):
    first_real = 0 if s == 1 else 1024
    k = np.arange(TC)
    lbs = np.zeros((128, TC), np.float32)
    lbs[k // 64, k] = BIG
    lbs[32, :] = 16 * (k // 16)
    lbs[33, :] = k % 16
    lbs[34, :] = 1.0
    lbs[35, :] = 1.0
    lbs[36, :] = -BIG
    lbs[37, :] = np.where(k < first_real, -BIG, 0.0)
    lbw = lbs.copy()
    lbw[0:32, :] = 0.0
    lbw[36, :] = 0.0
    c = np.arange(128)
    lbc = np.zeros((128, 128), np.float32)
    lbc[32, :] = 16 * c
    lbc[33, :] = 15.5
    lbc[34, :] = 1.0
    lbc[35, :] = 1.0
    kk = np.arange(128)[:, None]
    x = np.arange(896)[None, :]
    cs = np.where((x - 384) < kk, -BIG, 0.0)
    ws = np.where((x - 384) >= kk, -BIG, 0.0)
    t_ctx = 1024 + np.arange(TQ)[None, :]
    cc = np.arange(128)[:, None]
    cm = np.where((16 * cc + 31 > t_ctx) | (cc >= 127) | (16 * cc < first_real), -BIG, 0.0)
    cs_ = np.arange(128)[:, None] * 16
    bs_ = np.arange(32)[None, :] * 64
    ov = np.clip(np.minimum(cs_ + 32, bs_ + 64) - np.maximum(cs_, bs_), 0, None).astype(np.float32) / 32.0
    ovaug = np.concatenate([np.ones((128, 1), np.float32), ov], axis=1)
    ovaug[127, :] = 0.0
    ident = np.eye(128, dtype=np.float32)
    ones = np.ones((128, 128), np.float32)
    cbf = np.concatenate([lbs, lbw, lbc, cs, ws, cm, ovaug, ident, ones], axis=1)
    assert cbf.shape[1] == NCB
    rb = np.zeros((128, 16, 512), np.float32)
    slopes = 2.0 ** (-(np.arange(NH) + 1.0))
    for h in range(NH):
        for tc in range(2):
            t = 1024 + tc * 512 + np.arange(512)
            i = h * 2 + tc
            rb[0:32, i, :] = 1.0
            rb[32, i, :] = slopes[h]
            rb[33, i, :] = slopes[h]
            rb[34, i, :] = -slopes[h] * (16 * (t // 16))
            rb[35, i, :] = -slopes[h] * (t % 16)
            rb[36, i, :] = 1.0
            rb[37, i, :] = 1.0
    tq = 1024 + np.arange(TQ)[:, None]
    blk = np.arange(32)[None, :]
    cur = tq // 64
    valid = (blk * 64 <= tq) & (blk * 64 >= first_real)
    forced = (blk == first_real // 64) | (blk == cur) | (blk == cur - 1)
    bon = np.where(valid, 1000.0 * forced, -1e30).astype(np.float32)
    bon = bon.reshape(8, 128, 32).transpose(1, 0, 2).reshape(128, 256)
    return _bf(cbf), _bf(rb.reshape(128, 16 * 512)), np.ascontiguousarray(bon)


def _pvec(I):
    pv = np.zeros((128, NPV), np.float32)
    f = lambda a: np.asarray(a, np.float32)
    pv[:, PV_NMIX:PV_NMIX + 16] = f(I["norm_mix"])[0].reshape(16, 128).T
    pv[:, PV_NMQ:PV_NMQ + 16] = f(I["norm_mem_q"])[0].reshape(16, 128).T
    pv[:, PV_NMKV:PV_NMKV + 16] = f(I["norm_mem_kv"])[0].reshape(16, 128).T
    pv[:, PV_NFFN:PV_NFFN + 16] = f(I["norm_ffn"])[0].reshape(16, 128).T
    pv[:, PV_QN] = f(I["q_norm"])[0]
    pv[:, PV_KNC] = f(I["k_norm_cmp"])[0]
    pv[:, PV_KNS] = f(I["k_norm_slc"])[0]
    pv[:, PV_KNW] = f(I["k_norm_win"])[0]
    pv[:, PV_MQN] = f(I["mq_norm"])[0]
    pv[:, PV_MKN] = f(I["mk_norm"])[0]
    pv[:, PV_CB:PV_CB + 8] = f(I["conv_b"])[0].reshape(8, 128).T
    pv[:, PV_LNG:PV_LNG + 8] = f(I["conv_ln_g"])[0].reshape(8, 128).T
    pv[:, PV_LNB:PV_LNB + 8] = f(I["conv_ln_b"])[0].reshape(8, 128).T
    cw = f(I["conv_w"])[0]
    pv[:, PV_CW:PV_CW + 248] = cw.reshape(31, 8, 128).transpose(2, 1, 0).reshape(128, 248)
    pv[:, PV_GB:PV_GB + 24] = f(I["gate_b"])[0][None, :]
    pv[:, PV_POSK:PV_POSK + 32] = f(I["cmp_pos_k"])[0].T
    pv[:, PV_POSV:PV_POSV + 32] = f(I["cmp_pos_v"])[0].T
    return pv


def make_in_maps(I):
    f = lambda a: np.ascontiguousarray(np.asarray(a, np.float32))
    x = f(I["x"])
    mem = f(I["mem"])
    w_in = f(I["w_in"])[0]
    shared = {
        "w_in": w_in, "w_g": np.ascontiguousarray(w_in[:, C_G:C_G + 24]),
        "w_out": f(I["w_out"])[0], "w_mq": f(I["w_mq"])[0], "w_mk": f(I["w_mk"])[0], "w_mv": f(I["w_mv"])[0],
        "w_mo": f(I["w_mo"])[0], "w_gate": f(I["w_gate"])[0], "w_up": f(I["w_up"])[0], "w_down": f(I["w_down"])[0],
        "w1k": f(I["cmp_k_w1"])[0], "w1v": f(I["cmp_v_w1"])[0], "w2k": f(I["cmp_k_w2"])[0], "w2v": f(I["cmp_v_w2"])[0],
        "pvec": _pvec(I),
    }
    cs = [_consts(0), _consts(1)]
    maps = []
    for core in range(8):
        b, s = core // 2, core % 2
        if s == 1:
            xc = x[b]
        else:
            xc = np.concatenate([np.zeros((TQ, D), np.float32), x[b, :TQ]], axis=0)
        m = dict(shared)
        m["xc"] = np.ascontiguousarray(xc)
        m["memb"] = mem[b]
        m["cbf"], m["rb"], m["bonus"] = cs[s]
        maps.append(m)
    return maps


_CACHE = {}


def kernel(**inputs):
    if "k" not in _CACHE:
        K = Kern()
        K.build()
        _CACHE["k"] = K
    K = _CACHE["k"]
    maps = make_in_maps(inputs)
    res = run_bass_kernel_spmd(K.nc, maps, core_ids=list(range(8)))
    out = np.zeros((4, 2048, D), np.float32)
    for core in range(8):
        b, s = core // 2, core % 2
        out[b, s * TQ:(s + 1) * TQ] = np.asarray(res.results[core]["y"], np.float32).reshape(TQ, D)
    return out
```

```python
import numpy as np
import ml_dtypes
from contextlib import ExitStack
import concourse.bass as bass
import concourse.mybir as mybir
from concourse.bass_utils import run_bass_kernel_spmd

F32 = mybir.dt.float32
BF16 = mybir.dt.bfloat16
AF = mybir.ActivationFunctionType
ALU = mybir.AluOpType
NPBF = ml_dtypes.bfloat16

D = 2048
TQ = 1024
TC = 2048
NH = 8
DH = 128
IN_W = 4632
FFN = 5632
BIG = 32768.0
NV = 129
NVC = 161

C_Q, C_KC, C_VC, C_KSL, C_VSL, C_KW, C_VW, C_G, C_U = 0, 1024, 1280, 1536, 1792, 2048, 2304, 2560, 2584

PV_NMIX, PV_NMQ, PV_NMKV, PV_NFFN = 0, 16, 32, 48
PV_QN, PV_KNC, PV_KNS, PV_KNW, PV_MQN, PV_MKN = 64, 65, 66, 67, 68, 69
PV_CB, PV_LNG, PV_LNB, PV_CW = 70, 78, 86, 94
PV_GB = 94 + 248
PV_POSK = PV_GB + 24
PV_POSV = PV_POSK + 32
NPV = PV_POSV + 32

CB_LBS, CB_LBW, CB_LBC, CB_CS, CB_WS, CB_CM, CB_OV, CB_ID, CB_ONE = 0, 2048, 4096, 4224, 5120, 6016, 7040, 7073, 7201
NCB = 7329


class Sem:
    n = 0

    def __init__(self, h):
        self.h = h
        Sem.n += 1
        self.id = Sem.n


class Tok:
    __slots__ = ("sem", "val", "eng")

    def __init__(self, sem, val, eng):
        self.sem, self.val, self.eng = sem, val, eng


def _add(d, tok):
    cur = d.get(tok.sem.id)
    if cur is None or cur.val < tok.val:
        d[tok.sem.id] = tok


class Buf:
    def __init__(self, name, fence):
        self.name = name
        self.w = {}
        self.wf = {}
        self.r = dict(fence)
        self.dsem = None
        self.dcnt = 0
        self.excl = False


class Eng:
    EPOCH = 12000

    def __init__(self, K, eng, name, is_pe=False):
        self.K, self.eng, self.name, self.is_pe = K, eng, name, is_pe
        self.seen = {}
        self.ep = 0
        self.nwait = 0
        self.nins = 0
        self._new()

    def _new(self):
        self.sem = self.K.new_sem(f"e{self.name}{self.ep}")
        self.ep += 1
        self.cnt = 0

    def wait(self, tok):
        if tok is None:
            return
        if self.is_pe and tok.eng is self:
            return
        if self.seen.get(tok.sem.id, 0) >= tok.val:
            return
        self.eng.wait_ge(tok.sem.h, tok.val)
        self.nwait += 1
        self.seen[tok.sem.id] = tok.val

    def issue(self, ins):
        if self.cnt >= self.EPOCH:
            self._new()
        ins.then_inc(self.sem.h, 1)
        self.cnt += 1
        self.nins += 1
        return Tok(self.sem, self.cnt, self)


class Scope:
    def __init__(self, K):
        self.K = K
        self.blocks = []

    def close(self):
        for off, n in self.blocks:
            self.K.arena_release(off, n)
        self.blocks = []


class Rot:
    def __init__(self, items):
        self.items = list(items)
        self.i = 0

    def next(self):
        it = self.items[self.i % len(self.items)]
        self.i += 1
        return it


class Kern:
    def __init__(self, taps=(), stop=None):
        self.stop = stop
        self.taps = list(taps)
        self.tap_tensors = {}
        self.nc = bass.Bass("TRN2", target_bir_lowering=False)
        self.nsem = 0
        self.fence = {}
        self.es = ExitStack()

    def new_sem(self, name):
        self.nsem += 1
        return Sem(self.nc.alloc_semaphore(name=f"s{self.nsem}_{name}"))

    def buf(self, name):
        return Buf(name, self.fence)

    def free(self, bufs):
        for b in bufs:
            for t in b.w.values():
                _add(self.fence, t)
            for t in b.r.values():
                _add(self.fence, t)

    ARENA_BYTES = 198 * 1024

    def arena_init(self):
        self.arena = self.es.enter_context(self.nc.sbuf_tensor("arena", [128, self.ARENA_BYTES // 2], BF16))
        self.afree = [(0, self.ARENA_BYTES)]
        self.apeak = 0

    def arena_release(self, off, n):
        self.afree.append((off, n))
        self.afree.sort()
        m = []
        for o, l in self.afree:
            if m and m[-1][0] + m[-1][1] == o:
                m[-1] = (m[-1][0], m[-1][1] + l)
            else:
                m.append((o, l))
        self.afree = m

    def scope(self):
        return Scope(self)

    def tile(self, S, name, shape, dt):
        esz = 4 if dt == F32 else 2
        nel = int(np.prod(shape[1:]))
        nb = (nel * esz + 63) // 64 * 64
        for i, (o, l) in enumerate(self.afree):
            if l >= nb:
                self.afree[i] = (o + nb, l - nb)
                if l == nb:
                    self.afree.pop(i)
                break
        else:
            raise RuntimeError(f"arena OOM allocating {name} {shape} ({nb}B); free={self.afree}")
        S.blocks.append((o, nb))
        self.apeak = max(self.apeak, o + nb)
        v = self.arena[:, o // 2:o // 2 + nel * esz // 2]
        if dt == F32:
            v = v.bitcast(F32)
        if len(shape) == 3:
            v = v.rearrange("p (a b) -> p a b", a=shape[1])
        elif len(shape) == 4:
            v = v.rearrange("p (a b c) -> p a b c", a=shape[1], b=shape[2])
        return v

    def _waits(self, E, r, w, pw):
        for b in r:
            for t in b.w.values():
                E.wait(t)
            if b.excl:
                for t in b.r.values():
                    if t.eng is not E:
                        E.wait(t)
        for b in w:
            for t in b.w.values():
                E.wait(t)
            for t in b.r.values():
                E.wait(t)
        for b in pw:
            for t in b.wf.values():
                E.wait(t)
            for t in b.r.values():
                E.wait(t)

    def _upd(self, tok, r, w, pw):
        for b in w:
            b.w = {tok.sem.id: tok}
            b.wf = {tok.sem.id: tok}
            b.r = {}
        for b in pw:
            _add(b.w, tok)
        for b in r:
            if b not in w and b not in pw:
                _add(b.r, tok)

    def op(self, E, fn, r=(), w=(), pw=()):
        self._waits(E, r, w, pw)
        ins = fn()
        tok = E.issue(ins)
        self._upd(tok, r, w, pw)
        return tok

    def dma(self, Q, out, in_, r=(), w=(), pw=(), owner=None, **kw):
        self._waits(Q, r, w, pw)
        if owner is None:
            owner = (list(w) + list(pw) + list(r))[0]
        if owner.dsem is None:
            owner.dsem = self.new_sem("d" + owner.name)
        ins = Q.eng.dma_start(out=out, in_=in_, **kw)
        owner.dcnt += 16
        ins.then_inc(owner.dsem.h, 16)
        tok = Tok(owner.dsem, owner.dcnt, None)
        self._upd(tok, r, w, pw)
        return tok

    def tap(self, name, ap, bufs, shape, dt=F32):
        if name not in self.taps:
            return
        t = self.nc.dram_tensor("tap_" + name, list(shape), dt, kind="ExternalOutput").ap()
        self.tap_tensors[name] = t
        tb = self.buf("tap" + name)
        self.dma(self.SP, t, ap, r=list(bufs), owner=tb)
        self.outtoks.append((tb.dsem, tb.dcnt))

    def mark(self, name):
        self.marks.append((name, self.PE.nins, self.ACT.nins, self.DVE.nins))

    def fin(self):
        for sem, val in self.outtoks:
            self.SP.eng.wait_ge(sem.h, val)
        return self.nc

    def build(self):
        nc = self.nc
        es = self.es
        dr = lambda n, s, dt=F32: nc.dram_tensor(n, list(s), dt, kind="ExternalInput").ap()
        self.xc = dr("xc", [TC, D])
        self.memb = dr("memb", [256, D])
        self.w_in = dr("w_in", [D, IN_W])
        self.w_g = dr("w_g", [D, 24])
        self.w_out = dr("w_out", [D, D])
        self.w_mq = dr("w_mq", [D, 512])
        self.w_mk = dr("w_mk", [D, 512])
        self.w_mv = dr("w_mv", [D, 512])
        self.w_mo = dr("w_mo", [512, D])
        self.w_gate = dr("w_gate", [D, FFN])
        self.w_up = dr("w_up", [D, FFN])
        self.w_down = dr("w_down", [FFN, D])
        self.w1k = dr("w1k", [4096, 128])
        self.w1v = dr("w1v", [4096, 128])
        self.w2k = dr("w2k", [128, 128])
        self.w2v = dr("w2v", [128, 128])
        self.pvec_d = dr("pvec", [128, NPV])
        self.cbf_d = dr("cbf", [128, NCB], BF16)
        self.rb_d = dr("rb", [128, 16 * 512], BF16)
        self.bonus_d = dr("bonus", [128, 256])
        self.y = nc.dram_tensor("y", [TQ, D], F32, kind="ExternalOutput").ap()
        self.outtoks = []
        self.marks = []

        self.PE = Eng(self, nc.tensor, "pe", is_pe=True)
        self.ACT = Eng(self, nc.scalar, "act")
        self.DVE = Eng(self, nc.vector, "dve")
        self.POOL = Eng(self, nc.gpsimd, "pool")
        self.SP = Eng(self, nc.sync, "sp")
        PE, ACT, DVE, POOL, SP = self.PE, self.ACT, self.DVE, self.POOL, self.SP
        op, dma = self.op, self.dma

        self.arena_init()
        self.bank = []
        self.bankb = []
        for i in range(8):
            self.bank.append(es.enter_context(nc.psum_tensor(f"bank{i}", [128, 512], F32)))
            self.bankb.append(self.buf(f"bank{i}"))
            self.bankb[-1].excl = True
        bank, bankb = self.bank, self.bankb

        G = self.scope()
        pvec = self.tile(G, "pvec", [128, NPV], F32)
        pvb = self.buf("pvec")
        cid = self.tile(G, "cid", [128, 256], BF16)
        cidb = self.buf("cid")
        eps = self.tile(G, "eps", [128, 4], F32)
        epsb = self.buf("eps")
        dma(SP, pvec[:, :], self.pvec_d, w=[pvb])
        dma(SP, cid[:, :], self.cbf_d[:, CB_ID:CB_ID + 256], w=[cidb])
        op(DVE, lambda: nc.vector.memset(eps[:, 0:1], 1e-6), w=[epsb])
        op(DVE, lambda: nc.vector.memset(eps[:, 1:2], 1e-5), pw=[epsb])
        op(DVE, lambda: nc.vector.tensor_scalar(out=eps[:, 2:3], in0=pvec[:, PV_QN:PV_QN + 1], scalar1=DH ** -0.5, scalar2=None, op0=ALU.mult), r=[pvb], pw=[epsb])
        op(DVE, lambda: nc.vector.tensor_scalar(out=eps[:, 3:4], in0=pvec[:, PV_MQN:PV_MQN + 1], scalar1=DH ** -0.5, scalar2=None, op0=ALU.mult), r=[pvb], pw=[epsb])
        ident = cid[:, 0:128]
        ones = cid[:, 128:256]

        NSLOT = 6
        wslot = [self.tile(G, f"wslot{i}", [128, 4096], BF16) for i in range(NSLOT)]
        wslotb = [self.buf(f"wslot{i}") for i in range(NSLOT)]
        self.wq = []
        self.wnext_issue = 0

        def wcol(w, c0, n=256):
            return w[:, c0:c0 + n].rearrange("(k p) n -> p k n", p=128)

        def wrow(w, r0, n=256):
            return w[r0:r0 + n, :].rearrange("(k p) n -> p k n", p=128)

        def w1v_(w):
            return w.rearrange("(l d) o -> d l o", d=128)

        self.wreleased = set()

        def wissue():
            while self.wnext_issue < len(self.wq):
                i = self.wnext_issue
                if i >= NSLOT and (i - NSLOT) not in self.wreleased:
                    break
                src, shp = self.wq[i]
                s_ = i % NSLOT
                dst = wslot[s_][:, :].rearrange("p (k n) -> p k n", k=shp[0])
                dma(POOL, dst, src, w=[wslotb[s_]])
                self.wnext_issue += 1

        self.wcur = 0
        self.wheld = []

        def wget():
            i = self.wcur
            self.wcur += 1
            wissue()
            assert i < self.wnext_issue, "weight block not issuable: too many blocks held"
            self.wheld.append(i)
            s_ = i % NSLOT
            shp = self.wq[i][1]
            return wslot[s_][:, :].rearrange("p (k n) -> p k n", k=shp[0]), wslotb[s_]

        def wrel(n=None, newest=False):
            n = len(self.wheld) if n is None else n
            for _ in range(n):
                self.wreleased.add(self.wheld.pop(-1 if newest else 0))
            wissue()

        kvblocks = [(C_KC, "kc"), (C_VC, "vc"), (C_KSL, "ksl"), (C_KW, "kw"), (C_VSL, "vsl"), (C_VW, "vw")]
        for c0, _ in kvblocks:
            self.wq.append((wcol(self.w_in, c0), (16, 256)))
        for i in range(4):
            self.wq.append((wcol(self.w_in, C_Q + 256 * i), (16, 256)))
        for j in range(4):
            self.wq.append((wcol(self.w_in, C_U + 256 * j), (16, 256)))
            self.wq.append((wcol(self.w_in, C_U + 1024 + 256 * j), (16, 256)))
        self.wq.append((w1v_(self.w1k), (32, 128)))
        self.wq.append((w1v_(self.w1v), (32, 128)))
        for w in (self.w_mk, self.w_mv):
            for i in range(2):
                self.wq.append((wcol(w, 256 * i), (16, 256)))
        for i in range(8):
            self.wq.append((wcol(self.w_out, 256 * i), (16, 256)))
        for i in range(2):
            self.wq.append((wcol(self.w_mq, 256 * i), (16, 256)))
        for i in range(2):
            self.wq.append((wrow(self.w_mo, 256 * i), (2, 2048)))
        for j in range(22):
            self.wq.append((wcol(self.w_gate, 256 * j), (16, 256)))
            self.wq.append((wcol(self.w_up, 256 * j), (16, 256)))
            self.wq.append((wrow(self.w_down, 256 * j), (2, 2048)))

        wg = self.tile(G, "wg", [128, 16, 24], BF16)
        wgb = self.buf("wg")
        w2 = self.tile(G, "w2", [128, 2, 128], BF16)
        w2b = self.buf("w2")
        wissue()
        dma(POOL, wg[:, :, :], self.w_g.rearrange("(k p) n -> p k n", p=128), w=[wgb])
        dma(POOL, w2[:, 0, :], self.w2k, w=[w2b])
        dma(POOL, w2[:, 1, :], self.w2v, pw=[w2b])
        if self.stop == "init":
            self.tap("w0", wslot[0][:, :], [wslotb[0]], [128, 4096], BF16)
            self.tap("w4", wslot[4][:, :], [wslotb[4]], [128, 4096], BF16)
            self.tap("wg", wg[:, :, :], [wgb], [128, 16, 24], BF16)
            self.tap("pvec", pvec[:, :], [pvb], [128, NPV], F32)
            self.tap("cid", cid[:, :], [cidb], [128, 256], BF16)
            self.tap("eps", eps[:, :], [epsb], [128, 4], F32)
            return self.fin()

        rot_proj = Rot([0, 1, 2, 3])
        rot_ssq = Rot([4, 5])
        rot_tr = Rot([6, 7])
        self.flip = 0

        def evac_engine():
            self.flip ^= 1
            return ACT if self.flip else DVE

        def copy_op(E, out, in_, r, w=(), pw=()):
            if E is ACT:
                return op(ACT, lambda: nc.scalar.copy(out=out, in_=in_), r=r, w=w, pw=pw)
            return op(E, lambda: E.eng.tensor_copy(out=out, in_=in_), r=r, w=w, pw=pw)

        def norm_tile_a(xap, xbufs, xs_, xsb_, ss_, ssb_):
            op(ACT, lambda: nc.scalar.activation(out=xs_[:, :], in_=xap, func=AF.Square, accum_out=ss_[:, 0:1]), r=xbufs, w=[xsb_, ssb_])
            op(ACT, lambda: nc.scalar.activation(out=ss_[:, 1:2], in_=ss_[:, 0:1], func=AF.Sqrt, scale=1.0 / D, bias=eps[:, 0:1]), r=[epsb], w=[ssb_])
            op(DVE, lambda: nc.vector.reciprocal(out=ss_[:, 2:3], in_=ss_[:, 1:2]), w=[ssb_])
            op(DVE, lambda: nc.vector.tensor_scalar(out=xs_[:, :], in0=xap, scalar1=ss_[:, 2:3], scalar2=None, op0=ALU.mult), r=list(xbufs) + [ssb_], w=[xsb_])

        def norm_tile_b(gcol, dstT, t0, dstbuf, xs_, xsb_, tail=None):
            for half in range(2):
                bi = rot_tr.next()
                bv = bank[bi][:, :].bitcast(BF16)
                for c8 in range(8):
                    c = half * 8 + c8
                    op(PE, lambda: nc.tensor.transpose(out=bv[:, c8 * 128:(c8 + 1) * 128], in_=xs_[:, c * 128:(c + 1) * 128], identity=ident), r=[xsb_, cidb], w=[bankb[bi]])
                src = bv[:, 0:1024].rearrange("p (c t) -> p c t", c=8)
                dst = dstT[:, half * 8:(half + 1) * 8, t0:t0 + 128]
                g = pvec[:, gcol + half * 8:gcol + half * 8 + 8].unsqueeze(2).broadcast_to([128, 8, 128])
                op(DVE, lambda: nc.vector.tensor_tensor(out=dst, in0=src, in1=g, op=ALU.mult), r=[bankb[bi], pvb], pw=[dstbuf])
                if tail is not None:
                    tl, tlb = tail
                    op(POOL, lambda: nc.gpsimd.tensor_copy(out=tl[:, half * 8:(half + 1) * 8, :], in_=dstT[:, half * 8:(half + 1) * 8, t0 + 96:t0 + 128]), r=[dstbuf], pw=[tlb])

        def norm_tile(xap, xbufs, gcol, dstT, t0, dstbuf, xs_, xsb_, ss_, ssb_, tail=None):
            norm_tile_a(xap, xbufs, xs_, xsb_, ss_, ssb_)
            norm_tile_b(gcol, dstT, t0, dstbuf, xs_, xsb_, tail=tail)

        def rmsnorm_T(S, get_x, ntiles, gcol, dstT, dstbuf_of, tcol_of, tail=None):
            xs = [self.tile(S, f"xs{j}", [128, D], BF16) for j in range(2)]
            xsb = [self.buf(f"xs{j}") for j in range(2)]
            ss = [self.tile(S, f"ss{j}", [128, 4], F32) for j in range(2)]
            ssb = [self.buf(f"ss{j}") for j in range(2)]
            for i in range(ntiles):
                xap, xbufs = get_x(i)
                j = i % 2
                norm_tile(xap, xbufs, gcol, dstT, tcol_of(i), dstbuf_of(i), xs[j], xsb[j], ss[j], ssb[j], tail=tail if i == ntiles - 1 else None)
            self.free(xsb + ssb)

        P1 = self.scope()
        sqt = [self.tile(G, f"sqt{j}", [128, 512], BF16) for j in range(2)]
        sqtb = [self.buf(f"sqt{j}") for j in range(2)]
        rtt = [self.tile(G, f"rtt{j}", [128, 512], F32) for j in range(2)]
        rttb = [self.buf(f"rtt{j}") for j in range(2)]
        rot_sq = Rot([0, 1])

        def pnorm(psi, n, gain_ap, gain_bufs, dst, dstb):
            j = rot_sq.next()
            ps = bank[psi]
            op(ACT, lambda: nc.scalar.activation(out=sqt[j][:, 0:n], in_=ps[:, 0:n], func=AF.Square), r=[bankb[psi]], w=[sqtb[j]])

            def tail():
                si = rot_ssq.next()
                op(PE, lambda: nc.tensor.matmul(bank[si][:, 0:n], lhsT=ones, rhs=sqt[j][:, 0:n], start=True, stop=True), r=[sqtb[j], cidb], w=[bankb[si]])
                op(ACT, lambda: nc.scalar.activation(out=rtt[j][:, 0:n], in_=bank[si][:, 0:n], func=AF.Sqrt, scale=1.0 / DH, bias=eps[:, 0:1]), r=[bankb[si], epsb], w=[rttb[j]])
                op(DVE, lambda: nc.vector.reciprocal(out=rtt[j][:, 0:n], in_=rtt[j][:, 0:n]), w=[rttb[j]])
                op(DVE, lambda: nc.vector.scalar_tensor_tensor(out=dst, in0=ps[:, 0:n], scalar=gain_ap, in1=rtt[j][:, 0:n], op0=ALU.mult, op1=ALU.mult), r=[bankb[psi], rttb[j]] + list(gain_bufs), pw=[dstb])
            self.deferred.append(tail)

        self.deferred = []

        def flush(keep=0):
            while len(self.deferred) > keep:
                self.deferred.pop(0)()

        def proj_fm(wt, wb, cc, actT, actbufs, t0, n):
            psi = rot_proj.next()
            for kc in range(16):
                op(PE, lambda: nc.tensor.matmul(bank[psi][:, 0:n], lhsT=wt[:, kc, cc * 128:(cc + 1) * 128], rhs=actT[:, kc, t0:t0 + n], start=(kc == 0), stop=(kc == 15)), r=[wb] + list(actbufs), w=[bankb[psi]])
            flush(0)
            return psi

        def proj_tm(wt, wb, actT, actbufs, t0, ncols, kcs=16):
            psi = rot_proj.next()
            for kc in range(kcs):
                op(PE, lambda: nc.tensor.matmul(bank[psi][:, 0:ncols], lhsT=actT[:, kc, t0:t0 + 128], rhs=wt[:, kc, 0:ncols], start=(kc == 0), stop=(kc == kcs - 1)), r=[wb] + list(actbufs), w=[bankb[psi]])
            return psi

        A = self.scope()
        cbf = self.tile(A, "cbf", [128, CB_ID], BF16)
        cbb = self.buf("cbf")
        dma(SP, cbf[:, :], self.cbf_d[:, 0:CB_ID], w=[cbb])
        KnT = self.tile(A, "KnT", [128, 4, TC], BF16)
        KnTb = [[self.buf(f"KnT{i}_{c}") for c in range(4)] for i in range(4)]
        Vaug = self.tile(A, "Vaug", [128, 16, 4, NV], BF16)
        Vaugb = [self.buf(f"Vaug{i}") for i in range(16)]
        CK = self.scope()
        kcvcT = self.tile(CK, "kcvcT", [128, 4, TC], BF16)
        kcvcb = [self.buf(f"kcvc{i}") for i in range(4)]
        vab = self.buf("vaug_init")
        op(POOL, lambda: nc.gpsimd.memset(Vaug[:, :, :, :], 1.0), w=Vaugb)

        if self.stop == "p0a":
            self.tap("Vaug", Vaug[:, :, :, :], Vaugb, [128, 16, 4, NV], BF16)
            self.tap("cbf", cbf[:, :], [cbb], [128, CB_ID], BF16)
            return self.fin()
        hTs = [self.tile(P1, f"hT{c}", [128, 16, 512], BF16) for c in range(2)]
        hTb = [self.buf(f"hT{c}") for c in range(2)]
        htail = self.tile(P1, "htail", [128, 16, 32], BF16)
        htailb = self.buf("htail")

        S0 = self.scope()
        xt = [self.tile(S0, f"xt{j}", [128, D], F32) for j in range(2)]
        xtb = [self.buf(f"xt{j}") for j in range(2)]
        xs0 = [self.tile(S0, f"xs{j}", [128, D], BF16) for j in range(2)]
        xs0b = [self.buf(f"xs{j}") for j in range(2)]
        ss0 = [self.tile(S0, f"ss{j}", [128, 4], F32) for j in range(2)]
        ss0b = [self.buf(f"ss{j}") for j in range(2)]

        def xload(gt):
            dma(SP, xt[gt % 2][:, :], self.xc[gt * 128:(gt + 1) * 128, :], w=[xtb[gt % 2]])

        def norm_a(gt):
            if gt + 1 < 16:
                xload(gt + 1)
            j = gt % 2
            norm_tile_a(xt[j][:, :], [xtb[j]], xs0[j], xs0b[j], ss0[j], ss0b[j])

        def norm_b(gt):
            q_, i_ = gt // 4, gt % 4
            j = gt % 2
            norm_tile_b(PV_NMIX, hTs[q_ % 2], i_ * 128, hTb[q_ % 2], xs0[j], xs0b[j], tail=(htail, htailb) if gt == 7 else None)

        kvw = []

        def kv_block(q_, c0, kind, k_):
            hT_, hb_ = hTs[q_ % 2], hTb[q_ % 2]
            if q_ == 0:
                kvw.append(wget())
            wt, wb = kvw[k_]
            tg = q_ * 512
            if kind in ("kc", "vc", "ksl", "kw"):
                for cc in range(2):
                    psi = proj_fm(wt, wb, cc, hT_, [hb_], 0, 512)
                    if kind in ("kc", "vc"):
                        idx = (0 if kind == "kc" else 2) + cc
                        copy_op(evac_engine(), kcvcT[:, idx, tg:tg + 512], bank[psi][:, :], r=[bankb[psi]], pw=[kcvcb[idx]])
                    else:
                        idx = (0 if kind == "ksl" else 2) + cc
                        gcol = PV_KNS if kind == "ksl" else PV_KNW
                        pnorm(psi, 512, pvec[:, gcol:gcol + 1], [pvb], KnT[:, idx, tg:tg + 512], KnTb[idx][q_])
                flush(0)
            else:
                vk = 0 if kind == "vsl" else 2
                for tt in range(4):
                    psi = proj_tm(wt, wb, hT_, [hb_], tt * 128, 256)
                    kt = q_ * 4 + tt
                    E = evac_engine()
                    src = bank[psi][:, 0:256].rearrange("p (g d) -> p g d", g=2)
                    copy_op(E, Vaug[:, kt, vk:vk + 2, 0:128], src, r=[bankb[psi]], pw=[Vaugb[kt]])
            if q_ == 3:
                wrel(1)

        self.mark("p1_start")
        xload(0)
        norm_a(0)
        for gt in range(4):
            norm_b(gt)
            norm_a(gt + 1)
        for q_ in range(4):
            for k_, (c0, kind) in enumerate(kvblocks):
                kv_block(q_, c0, kind, k_)
                if q_ < 3 and k_ < 4:
                    gt = (q_ + 1) * 4 + k_
                    norm_b(gt)
                    if gt + 1 < 16 and k_ < 3:
                        norm_a(gt + 1)
            if q_ < 2:
                norm_a((q_ + 2) * 4)
            flush(0)
            if self.stop == "p0" and q_ == 0:
                return self.fin()
        S0.close()
        self.free(xtb + xs0b + ss0b)
        qnT = self.tile(A, "qnT", [128, NH, TQ], BF16)
        qnTb = [[self.buf(f"qnT{h}_{c}") for c in range(2)] for h in range(NH)]
        gates = self.tile(A, "gates", [128, 8, 24], F32)
        gatesb = self.buf("gates")
        HG = self.scope()
        hglu = self.tile(HG, "hglu", [128, 8, 1056], BF16)
        hglub = [self.buf(f"hglu{c}") for c in range(8)]

        self.mark("q_proj")
        for blk in range(4):
            wt, wb = wget()
            for cc in range(2):
                h = blk * 2 + cc
                for tc in range(2):
                    psi = proj_fm(wt, wb, cc, hTs[tc], [hTb[tc]], 0, 512)
                    pnorm(psi, 512, eps[:, 2:3], [epsb], qnT[:, h, tc * 512:(tc + 1) * 512], qnTb[h][tc])
            flush(0)
            wrel()
        gtmp = self.tile(P1, "gtmp", [128, 24], F32)
        gtmpb = self.buf("gtmp")
        for tt in range(8):
            psi = rot_proj.next()
            for kc in range(16):
                op(PE, lambda: nc.tensor.matmul(bank[psi][:, 0:24], lhsT=hTs[tt // 4][:, kc, (tt % 4) * 128:(tt % 4 + 1) * 128], rhs=wg[:, kc, :], start=(kc == 0), stop=(kc == 15)), r=[wgb, hTb[tt // 4]], w=[bankb[psi]])
            op(DVE, lambda: nc.vector.tensor_tensor(out=gtmp[:, :], in0=bank[psi][:, 0:24], in1=pvec[:, PV_GB:PV_GB + 24], op=ALU.add), r=[bankb[psi], pvb], w=[gtmpb])
            op(ACT, lambda: nc.scalar.activation(out=gates[:, tt, :], in_=gtmp[:, :], func=AF.Sigmoid), r=[gtmpb], pw=[gatesb])
        self.mark("u_proj")
        sg = [self.tile(P1, f"sg{j}", [128, 512], F32) for j in range(2)]
        sgb = [self.buf(f"sg{j}") for j in range(2)]
        rot_sg = Rot([0, 1])
        for j4 in range(4):
            wa, wab = wget()
            wbt, wbb = wget()
            for cc in range(2):
                ch = j4 * 2 + cc
                for seg in range(3):
                    if seg == 0:
                        actT, actb, t0, n, off = htail, [htailb], 0, 32, 0
                    else:
                        actT, actb, t0, n, off = hTs[seg - 1], [hTb[seg - 1]], 0, 512, 32 + (seg - 1) * 512
                    pa = proj_fm(wa, wab, cc, actT, actb, t0, n)
                    pb = proj_fm(wbt, wbb, cc, actT, actb, t0, n)
                    j = rot_sg.next()
                    op(ACT, lambda: nc.scalar.activation(out=sg[j][:, 0:n], in_=bank[pb][:, 0:n], func=AF.Sigmoid), r=[bankb[pb]], w=[sgb[j]])
                    op(DVE, lambda: nc.vector.tensor_tensor(out=hglu[:, ch, off:off + n], in0=bank[pa][:, 0:n], in1=sg[j][:, 0:n], op=ALU.mult), r=[bankb[pa], sgb[j]], pw=[hglub[ch]])
            wrel()
        self.tap("KnT", KnT[:, :, :], [b for l in KnTb for b in l], [128, 4, TC], BF16)
        self.tap("qnT", qnT[:, :, :], [b for l in qnTb for b in l], [128, NH, TQ], BF16)
        self.tap("kcvcT", kcvcT[:, :, :], kcvcb, [128, 4, TC], BF16)
        self.tap("Vaug", Vaug[:, :, :, :], Vaugb, [128, 16, 4, NV], BF16)
        self.tap("gates", gates[:, :, :], [gatesb], [128, 8, 24], F32)
        self.tap("hglu", hglu[:, :, :], hglub, [128, 8, 1056], BF16)
        P1.close()
        self.free(hTb + [htailb, gtmpb] + sgb)
        if self.stop == "p1":
            return self.fin()

        self.mark("compress")
        kcmpT = self.tile(A, "kcmpT", [128, 2, 128], BF16)
        kcmpb = self.buf("kcmpT")
        vcaug = self.tile(A, "vcaug", [128, 2, NVC], BF16)
        vcaugb = self.buf("vcaug")
        C1 = self.scope()
        posbf = self.tile(C1, "posbf", [128, 64], BF16)
        posbfb = self.buf("posbf")
        posb = self.tile(C1, "posb", [128, 2], F32)
        posbb = self.buf("posb")
        h1 = [self.tile(C1, f"h1_{j}", [128, 128], BF16) for j in range(2)]
        h1b = [self.buf(f"h1_{j}") for j in range(2)]
        op(DVE, lambda: nc.vector.memset(kcmpT[:, :, :], 0.0), w=[kcmpb])
        op(DVE, lambda: nc.vector.memset(vcaug[:, :, :], 0.0), w=[vcaugb])
        for g in range(2):
            op(DVE, lambda: nc.vector.tensor_copy(out=vcaug[:, g, 128:NVC], in_=cbf[:, CB_OV:CB_OV + 33]), r=[cbb], pw=[vcaugb])
        op(DVE, lambda: nc.vector.tensor_copy(out=posbf[:, :], in_=pvec[:, PV_POSK:PV_POSK + 64]), r=[pvb], w=[posbfb])
        rot_h1 = Rot([0, 1])
        for kind in range(2):
            w1, w1b = wget()
            pbi = rot_proj.next()
            for l in range(32):
                op(PE, lambda: nc.tensor.matmul(bank[pbi][:, 0:1], lhsT=w1[:, l, :], rhs=posbf[:, kind * 32 + l:kind * 32 + l + 1], start=(l == 0), stop=(l == 31)), r=[w1b, posbfb], w=[bankb[pbi]])
            op(DVE, lambda: nc.vector.tensor_copy(out=posb[:, kind:kind + 1], in_=bank[pbi][:, 0:1]), r=[bankb[pbi]], pw=[posbb])
            for g in range(2):
                psi = rot_proj.next()
                src = kcvcT[:, kind * 2 + g, :]
                for l in range(32):
                    op(PE, lambda: nc.tensor.matmul(bank[psi][:, 0:127], lhsT=w1[:, l, :], rhs=src[:, l:l + 16 * 126 + 1:16], start=(l == 0), stop=(l == 31)), r=[w1b, kcvcb[kind * 2 + g]], w=[bankb[psi]])
                j = rot_h1.next()
                op(ACT, lambda: nc.scalar.activation(out=h1[j][:, 0:127], in_=bank[psi][:, 0:127], func=AF.Silu, bias=posb[:, kind:kind + 1]), r=[bankb[psi], posbb], w=[h1b[j]])
                ps2 = rot_proj.next()
                if kind == 0:
                    op(PE, lambda: nc.tensor.matmul(bank[ps2][:, 0:127], lhsT=w2[:, 0, :], rhs=h1[j][:, 0:127], start=True, stop=True), r=[w2b, h1b[j]], w=[bankb[ps2]])
                    pnorm(ps2, 127, pvec[:, PV_KNC:PV_KNC + 1], [pvb], kcmpT[:, g, 0:127], kcmpb)
                    flush(0)
                else:
                    op(PE, lambda: nc.tensor.matmul(bank[ps2][0:127, 0:128], lhsT=h1[j][:, 0:127], rhs=w2[:, 1, :], start=True, stop=True), r=[w2b, h1b[j]], w=[bankb[ps2]])
                    copy_op(evac_engine(), vcaug[0:127, g, 0:128], bank[ps2][0:127, 0:128], r=[bankb[ps2]], pw=[vcaugb])
            wrel()
        self.tap("kcmpT", kcmpT[:, :, :], [kcmpb], [128, 2, 128], BF16)
        self.tap("vcaug", vcaug[:, :, :], [vcaugb], [128, 2, NVC], BF16)
        C1.close()
        self.free([posbfb, posbb] + h1b)
        if self.stop == "p1c":
            return self.fin()
        CK.close()
        self.free(kcvcb)

        self.mark("attn")
        CT = self.scope()
        catA = self.tile(CT, "catA", [128, 8, TQ], BF16)
        catTb = [[self.buf(f"catT{c}_{t}") for t in range(2)] for c in range(16)]
        AT = self.scope()
        RB = self.tile(AT, "RB", [128, 16, 512], BF16)
        RBb = [self.buf(f"RB{i}") for i in range(16)]
        dma(SP, RB[:, :, :], self.rb_d.rearrange("p (i n) -> p i n", i=16), w=RBb, owner=RBb[0])
        bonus = self.tile(AT, "bonus", [128, 8, 32], F32)
        bonusb = self.buf("bonus")
        dma(SP, bonus[:, :, :], self.bonus_d.rearrange("p (i n) -> p i n", i=8), w=[bonusb])
        Pt = [self.tile(AT, f"Pt{j}", [128, 512], BF16) for j in range(3)]
        Ptb = [self.buf(f"Pt{j}") for j in range(3)]
        rot_P = Rot([0, 1, 2])
        rot_S = Rot([0, 1, 6])
        rot_O = Rot([(2, 3), (4, 5)])
        ocomb = [self.tile(AT, f"ocomb{j}", [128, 4, 4, 128], F32) for j in range(1)]
        ocombb = [self.buf(f"ocomb{j}") for j in range(1)]
        ocb = [self.tile(AT, f"ocb{j}", [128, 4, 4, 128], BF16) for j in range(1)]
        ocbb = [self.buf(f"ocb{j}") for j in range(1)]
        imp = self.tile(AT, "imp", [128, 4, 32], F32)
        impb = self.buf("imp")
        sc2 = self.tile(AT, "sc2", [128, 4, 32], F32)
        sc2b = self.buf("sc2")
        m8 = self.tile(AT, "m8", [128, 16], F32)
        m8b = self.buf("m8")
        selbf = self.tile(AT, "selbf", [128, 4, 32], BF16)
        selbfb = self.buf("selbf")
        rdt = [self.tile(AT, f"rd{j}", [128, 8], F32) for j in range(4)]
        rdb = [self.buf(f"rd{j}") for j in range(4)]
        rot_rd = Rot([0, 1, 2, 3])

        def attn_tile(Kt, Kb, Qt, Qb, LBap, RBi, mask_ap):
            si = rot_S.next()
            n_extra = 1 if mask_ap is not None else 0
            op(PE, lambda: nc.tensor.matmul(bank[si][:, :], lhsT=Kt, rhs=Qt, start=True, stop=False), r=list(Kb) + list(Qb), w=[bankb[si]])
            op(PE, lambda: nc.tensor.matmul(bank[si][:, :], lhsT=LBap, rhs=RB[:, RBi, :], start=False, stop=(n_extra == 0)), r=[cbb, RBb[RBi]], w=[bankb[si]])
            if mask_ap is not None:
                op(PE, lambda: nc.tensor.matmul(bank[si][:, :], lhsT=ident, rhs=mask_ap, start=False, stop=True), r=[cbb, cidb], w=[bankb[si]])
            pj = rot_P.next()
            op(ACT, lambda: nc.scalar.activation(out=Pt[pj][:, :], in_=bank[si][:, :], func=AF.Exp), r=[bankb[si]], w=[Ptb[pj]])
            return pj

        def pv(pj, Obanks, Vap, Vb, nv, first, last):
            for j in range(4):
                ob = Obanks[j // 2]
                o0 = (j % 2) * nv
                op(PE, lambda: nc.tensor.matmul(bank[ob][:, o0:o0 + nv], lhsT=Pt[pj][:, j * 128:(j + 1) * 128], rhs=Vap, start=(first and j % 2 == 0), stop=last, skip_group_check=True), r=[Ptb[pj]] + list(Vb), w=[bankb[ob]])

        def finish(Obanks, nv, h, tc, br, oc, mode):
            hl = h % 4
            rj = rot_rd.next()
            rd = rdt[rj]
            for b2 in range(2):
                ob = Obanks[b2]
                op(DVE, lambda: nc.vector.tensor_scalar(out=rd[:, 2 * b2:2 * b2 + 2], in0=bank[ob][:, 128:128 + nv + 1:nv], scalar1=1e-30, scalar2=None, op0=ALU.max), r=[bankb[ob]], pw=[rdb[rj]])
            op(DVE, lambda: nc.vector.reciprocal(out=rd[:, 0:4], in_=rd[:, 0:4]), w=[rdb[rj]])
            gcol = h * 3 + br
            op(DVE, lambda: nc.vector.tensor_tensor(out=rd[:, 4:8], in0=rd[:, 0:4], in1=gates[:, tc * 4:tc * 4 + 4, gcol], op=ALU.mult), r=[gatesb], w=[rdb[rj]])
            for j in range(4):
                ob = Obanks[j // 2]
                o0 = (j % 2) * nv
                src = bank[ob][:, o0:o0 + 128]
                f = rd[:, 4 + j:5 + j]
                if mode == "first":
                    op(DVE, lambda: nc.vector.tensor_scalar(out=ocomb[oc][:, j, hl, :], in0=src, scalar1=f, scalar2=None, op0=ALU.mult), r=[bankb[ob], rdb[rj]], pw=[ocombb[oc]])
                elif mode == "add":
                    op(DVE, lambda: nc.vector.scalar_tensor_tensor(out=ocomb[oc][:, j, hl, :], in0=src, scalar=f, in1=ocomb[oc][:, j, hl, :], op0=ALU.mult, op1=ALU.add), r=[bankb[ob], rdb[rj]], w=[ocombb[oc]])
                else:
                    op(DVE, lambda: nc.vector.scalar_tensor_tensor(out=ocb[oc][:, j, hl, :], in0=src, scalar=f, in1=ocomb[oc][:, j, hl, :], op0=ALU.mult, op1=ALU.add), r=[bankb[ob], rdb[rj], ocombb[oc]], pw=[ocbb[oc]])
            return rj

        LBS = lambda i: cbf[:, CB_LBS + i * 128:CB_LBS + (i + 1) * 128]
        LBW = lambda i: cbf[:, CB_LBW + i * 128:CB_LBW + (i + 1) * 128]
        LBC = cbf[:, CB_LBC:CB_LBC + 128]
        CSm = lambda m: cbf[:, CB_CS + 384 - 128 * m:CB_CS + 384 - 128 * m + 512]
        WSm = lambda m: cbf[:, CB_WS + 384 - 128 * m:CB_WS + 384 - 128 * m + 512]

        def run_stream(jobs, L=2):
            pend = []
            for n in range(len(jobs) + L):
                if n < len(jobs):
                    pend.append((jobs[n], jobs[n][0]()))
                if n >= L:
                    job, pj = pend.pop(0)
                    job[1](pj)

        for g in range(2):
            for tc in range(2):
                oc = 0
                Q = lambda h: qnT[:, h, tc * 512:(tc + 1) * 512]
                jobs = []
                for hl in range(4):
                    h = 4 * g + hl
                    Ob = rot_O.next()

                    def sc(h=h):
                        return attn_tile(kcmpT[:, g, :], [kcmpb], Q(h), [qnTb[h][tc]], LBC, h * 2 + tc, cbf[:, CB_CM + tc * 512:CB_CM + (tc + 1) * 512])

                    def pvf(pj, h=h, hl=hl, Ob=Ob):
                        pv(pj, Ob, vcaug[:, g, :], [vcaugb], NVC, True, True)
                        rj = finish(Ob, NVC, h, tc, 0, oc, "first")
                        rd = rdt[rj]
                        for j in range(4):
                            ob = Ob[j // 2]
                            o0 = (j % 2) * NVC + 129
                            if hl == 0:
                                op(DVE, lambda: nc.vector.tensor_scalar(out=imp[:, j, :], in0=bank[ob][:, o0:o0 + 32], scalar1=rd[:, j:j + 1], scalar2=None, op0=ALU.mult), r=[bankb[ob], rdb[rj]], w=[impb] if j == 0 else [], pw=[impb] if j else [])
                            else:
                                op(DVE, lambda: nc.vector.scalar_tensor_tensor(out=imp[:, j, :], in0=bank[ob][:, o0:o0 + 32], scalar=rd[:, j:j + 1], in1=imp[:, j, :], op0=ALU.mult, op1=ALU.add), r=[bankb[ob], rdb[rj]], w=[impb])
                    jobs.append((sc, pvf))
                for hl in range(4):
                    h = 4 * g + hl
                    Ob = rot_O.next()
                    ks = list(range(4 + 4 * tc, 12 + 4 * tc))
                    for n_, i in enumerate(ks):
                        m = i - (4 + 4 * tc)
                        mask = WSm(m) if m < 4 else CSm(m - 4)

                        def sc(h=h, i=i, mask=mask):
                            return attn_tile(KnT[:, 2 + g, i * 128:(i + 1) * 128], [KnTb[2 + g][i // 4]], Q(h), [qnTb[h][tc]], LBW(i), h * 2 + tc, mask)

                        def pvf(pj, h=h, i=i, n_=n_, Ob=Ob, last=(n_ == len(ks) - 1)):
                            pv(pj, Ob, Vaug[:, i, 2 + g, :], [Vaugb[i]], NV, n_ == 0, last)
                            if last:
                                finish(Ob, NV, h, tc, 2, oc, "add")
                        jobs.append((sc, pvf))
                run_stream(jobs[:4])
                op(DVE, lambda: nc.vector.tensor_tensor(out=imp[:, :, :], in0=imp[:, :, :], in1=bonus[:, tc * 4:tc * 4 + 4, :], op=ALU.add), r=[bonusb], w=[impb])
                tb = rot_tr.next()
                tbv = bank[tb][:, :].bitcast(BF16)
                for j in range(4):
                    op(DVE, lambda: nc.vector.max(out=m8[:, 0:8], in_=imp[:, j, :]), r=[impb], w=[m8b])
                    op(DVE, lambda: nc.vector.match_replace(out=sc2[:, j, :], in_to_replace=m8[:, 0:8], in_values=imp[:, j, :], imm_value=-3.0e38), r=[impb, m8b], w=[sc2b])
                    op(DVE, lambda: nc.vector.max(out=m8[:, 8:16], in_=sc2[:, j, :]), r=[sc2b], w=[m8b])
                    op(DVE, lambda: nc.vector.tensor_scalar(out=selbf[:, j, :], in0=imp[:, j, :], scalar1=m8[:, 15:16], scalar2=None, op0=ALU.is_ge), r=[impb, m8b], w=[selbfb])
                if g == 0 and tc == 1:
                    self.tap("imp", imp[:, :, :], [impb], [128, 4, 32], F32)
                    self.tap("selbf", selbf[:, :, :], [selbfb], [128, 4, 32], BF16)
                run_stream(jobs[4:])
                for j in range(4):
                    op(PE, lambda: nc.tensor.transpose(out=tbv[0:32, j * 128:(j + 1) * 128], in_=selbf[:, j, :], identity=ident), r=[selbfb, cidb], w=[bankb[tb]])
                for hl in range(4):
                    h = 4 * g + hl
                    copy_op(ACT, RB[0:32, h * 2 + tc, :], tbv[0:32, 0:512], r=[bankb[tb]], w=[RBb[h * 2 + tc]])
                jobs = []
                for hl in range(4):
                    h = 4 * g + hl
                    Ob = rot_O.next()
                    ks = list(range(0, 12 + 4 * tc))
                    for n_, i in enumerate(ks):
                        m = i - (8 + 4 * tc)
                        mask = CSm(m) if m >= 0 else None

                        def sc(h=h, i=i, mask=mask):
                            return attn_tile(KnT[:, g, i * 128:(i + 1) * 128], [KnTb[g][i // 4]], Q(h), [qnTb[h][tc]], LBS(i), h * 2 + tc, mask)

                        def pvf(pj, h=h, i=i, n_=n_, Ob=Ob, last=(n_ == len(ks) - 1)):
                            pv(pj, Ob, Vaug[:, i, g, :], [Vaugb[i]], NV, n_ == 0, last)
                            if last:
                                finish(Ob, NV, h, tc, 1, oc, "last")
                        jobs.append((sc, pvf))
                run_stream(jobs)
                for hl in range(4):
                    h = 4 * g + hl
                    tb2 = rot_tr.next()
                    tv = bank[tb2][:, :].bitcast(BF16)
                    for j in range(4):
                        op(PE, lambda: nc.tensor.transpose(out=tv[:, j * 128:(j + 1) * 128], in_=ocb[oc][:, j, hl, :], identity=ident), r=[ocbb[oc], cidb], w=[bankb[tb2]])
                    copy_op(evac_engine(), catA[:, h, tc * 512:(tc + 1) * 512], tv[:, 0:512], r=[bankb[tb2]], w=[catTb[h][tc]])
        self.tap("catT_nsa", catA[:, :, :], [catTb[c][t] for c in range(8) for t in range(2)], [128, 8, TQ], BF16)
        if self.stop == "p2":
            return self.fin()
        AT.close()
        self.free(RBb + [bonusb, impb, sc2b, m8b, selbfb] + Ptb + ocombb + ocbb + rdb)
        A.close()
        self.free([b for l in KnTb for b in l] + Vaugb + [b for l in qnTb for b in l] + [gatesb, kcmpb, vcaugb, cbb])
        catB = self.tile(CT, "catB", [128, 8, TQ], BF16)

        self.mark("conv")
        CV = self.scope()
        dg = [self.tile(CV, f"dg{j}", [128, 31, 128], BF16) for j in range(2)]
        dgb = [self.buf(f"dg{j}") for j in range(2)]
        cv = self.tile(CV, "cv", [128, 8, TQ], F32)
        cvb_ = [[self.buf(f"cv{c}_{t}") for t in range(2)] for c in range(8)]
        cvs = [self.tile(CV, f"cvs{j}", [128, 2, 512], BF16) for j in range(2)]
        cvsb = [self.buf(f"cvs{j}") for j in range(2)]
        rot_cvs = Rot([0, 1])
        stat_bank = {(0, 0): 4, (0, 1): 5, (1, 0): 6, (1, 1): 7}
        mean = self.tile(CV, "mean", [128, 512], F32)
        meanb = self.buf("mean")
        rstd = self.tile(CV, "rstd", [128, 512], F32)
        rstdb = self.buf("rstd")
        t1 = [self.tile(CV, f"t1_{j}", [128, 512], F32) for j in range(2)]
        t1b = [self.buf(f"t1_{j}") for j in range(2)]
        ident_b = ident.unsqueeze(1).broadcast_to([128, 31, 128])

        def ln_prep(tc):
            sb0, sb1 = stat_bank[(tc, 0)], stat_bank[(tc, 1)]
            op(DVE, lambda: nc.vector.tensor_scalar(out=mean[:, :], in0=bank[sb0][:, :], scalar1=1.0 / 1024, scalar2=None, op0=ALU.mult), r=[bankb[sb0]], w=[meanb])
            op(DVE, lambda: nc.vector.tensor_tensor(out=rstd[:, :], in0=mean[:, :], in1=mean[:, :], op=ALU.mult), r=[meanb], w=[rstdb])
            op(DVE, lambda: nc.vector.scalar_tensor_tensor(out=rstd[:, :], in0=bank[sb1][:, :], scalar=1.0 / 1024, in1=rstd[:, :], op0=ALU.mult, op1=ALU.subtract), r=[bankb[sb1]], w=[rstdb])
            op(ACT, lambda: nc.scalar.activation(out=rstd[:, :], in_=rstd[:, :], func=AF.Sqrt, bias=eps[:, 1:2]), r=[epsb], w=[rstdb])
            op(DVE, lambda: nc.vector.reciprocal(out=rstd[:, :], in_=rstd[:, :]), w=[rstdb])

        def ln_chunk(tc, ch):
            j = ch % 2
            op(DVE, lambda: nc.vector.tensor_tensor(out=t1[j][:, :], in0=cv[:, ch, tc * 512:(tc + 1) * 512], in1=mean[:, :], op=ALU.subtract), r=[cvb_[ch][tc], meanb], w=[t1b[j]])
            op(DVE, lambda: nc.vector.tensor_tensor(out=t1[j][:, :], in0=t1[j][:, :], in1=rstd[:, :], op=ALU.mult), r=[rstdb], w=[t1b[j]])
            op(ACT, lambda: nc.scalar.activation(out=catB[:, ch, tc * 512:(tc + 1) * 512], in_=t1[j][:, :], func=AF.Silu, scale=pvec[:, PV_LNG + ch:PV_LNG + ch + 1], bias=pvec[:, PV_LNB + ch:PV_LNB + ch + 1]), r=[t1b[j], pvb], w=[catTb[8 + ch][tc]])

        pend = []

        def build_dg(n):
            ch_ = n % 8
            wv = pvec[:, PV_CW + ch_ * 31:PV_CW + ch_ * 31 + 31].unsqueeze(2).broadcast_to([128, 31, 128])
            op(DVE, lambda: nc.vector.tensor_tensor(out=dg[n % 2][:, :, :], in0=ident_b, in1=wv, op=ALU.mult), r=[cidb, pvb], w=[dgb[n % 2]])

        it_ = 0
        build_dg(0)
        for tc in range(2):
            for ch in range(8):
                dj = it_ % 2
                if it_ + 1 < 16:
                    build_dg(it_ + 1)
                it_ += 1
                psi = rot_proj.next()
                for j in range(31):
                    o = 2 + j + tc * 512
                    op(PE, lambda: nc.tensor.matmul(bank[psi][:, :], lhsT=dg[dj][:, j, :], rhs=hglu[:, ch, o:o + 512], start=(j == 0), stop=(j == 30)), r=[dgb[dj], hglub[ch]], w=[bankb[psi]])
                while pend:
                    pend.pop(0)()
                op(ACT, lambda: nc.scalar.activation(out=cv[:, ch, tc * 512:(tc + 1) * 512], in_=bank[psi][:, :], func=AF.Identity, bias=pvec[:, PV_CB + ch:PV_CB + ch + 1]), r=[bankb[psi], pvb], w=[cvb_[ch][tc]])
                sj = rot_cvs.next()
                op(DVE, lambda: nc.vector.tensor_copy(out=cvs[sj][:, 0, :], in_=cv[:, ch, tc * 512:(tc + 1) * 512]), r=[cvb_[ch][tc]], w=[cvsb[sj]])
                op(ACT, lambda: nc.scalar.activation(out=cvs[sj][:, 1, :], in_=cv[:, ch, tc * 512:(tc + 1) * 512], func=AF.Square), r=[cvb_[ch][tc]], pw=[cvsb[sj]])

                def stats(tc=tc, ch=ch, sj=sj):
                    for kind in range(2):
                        sb_ = stat_bank[(tc, kind)]
                        op(PE, lambda: nc.tensor.matmul(bank[sb_][:, :], lhsT=ones, rhs=cvs[sj][:, kind, :], start=(ch == 0), stop=(ch == 7)), r=[cvsb[sj], cidb], w=[bankb[sb_]])
                pend.append(stats)
                if tc == 1:
                    ln_chunk(0, ch)
            while pend:
                pend.pop(0)()
            if tc == 0:
                ln_prep(0)
        ln_prep(1)
        for ch in range(8):
            ln_chunk(1, ch)
        self.tap("cv", cv[:, :, :], [b for l in cvb_ for b in l], [128, 8, TQ], F32)
        self.tap("catB", catB[:, :, :], [catTb[c][t] for c in range(8, 16) for t in range(2)], [128, 8, TQ], BF16)
        if self.stop == "p3":
            return self.fin()
        CV.close()
        self.free(dgb + [b for l in cvb_ for b in l] + cvsb + [meanb, rstdb] + t1b)
        HG.close()
        self.free(hglub)

        self.mark("memkv")
        MK = self.scope()
        hmT = self.tile(MK, "hmT", [128, 16, 256], BF16)
        hmTb = self.buf("hmT")
        kmnT = self.tile(MK, "kmnT", [128, 4, 256], BF16)
        kmnTb = [self.buf(f"kmnT{h}") for h in range(4)]
        vmaug = self.tile(MK, "vmaug", [128, 2, 4, NV], BF16)
        vmaugb = self.buf("vmaug")
        S1m = self.scope()
        mt_ = [self.tile(S1m, f"memt{j}", [128, D], F32) for j in range(2)]
        mtb = [self.buf(f"memt{j}") for j in range(2)]
        for j in range(2):
            dma(SP, mt_[j][:, :], self.memb[j * 128:(j + 1) * 128, :], w=[mtb[j]])
        rmsnorm_T(S1m, lambda i: (mt_[i][:, :], [mtb[i]]), 2, PV_NMKV, hmT, lambda i: hmTb, lambda i: i * 128)
        S1m.close()
        self.free(mtb)
        op(POOL, lambda: nc.gpsimd.memset(vmaug[:, :, :, :], 1.0), w=[vmaugb])
        for blk in range(2):
            wt, wb = wget()
            for cc in range(2):
                h = blk * 2 + cc
                psi = proj_fm(wt, wb, cc, hmT, [hmTb], 0, 256)
                pnorm(psi, 256, pvec[:, PV_MKN:PV_MKN + 1], [pvb], kmnT[:, h, :], kmnTb[h])
            flush(0)
            wrel()
        for blk in range(2):
            wt, wb = wget()
            for mt in range(2):
                psi = proj_tm(wt, wb, hmT, [hmTb], mt * 128, 256)
                src = bank[psi][:, 0:256].rearrange("p (g d) -> p g d", g=2)
                copy_op(evac_engine(), vmaug[:, mt, blk * 2:blk * 2 + 2, 0:128], src, r=[bankb[psi]], pw=[vmaugb])
            wrel()

        self.mark("w_out")
        xres = self.tile(G, "xres", [128, 8, D], F32)
        xrb = [[self.buf(f"xres{t}_{c}") for c in range(8)] for t in range(8)]
        for tt in range(8):
            dma(SP, xres[:, tt, :], self.xc[TQ + tt * 128:TQ + (tt + 1) * 128, :], w=xrb[tt], owner=xrb[tt][0])
        for cb in range(8):
            wt, wb = wget()
            for tt in range(8):
                psi = rot_proj.next()
                for kc in range(16):
                    cat_ = catA if kc < 8 else catB
                    op(PE, lambda: nc.tensor.matmul(bank[psi][:, 0:256], lhsT=cat_[:, kc % 8, tt * 128:(tt + 1) * 128], rhs=wt[:, kc, 0:256], start=(kc == 0), stop=(kc == 15)), r=[wb, catTb[kc][tt // 4]], w=[bankb[psi]])
                dst = xres[:, tt, cb * 256:(cb + 1) * 256]
                op(DVE, lambda: nc.vector.tensor_tensor(out=dst, in0=bank[psi][:, 0:256], in1=dst, op=ALU.add), r=[bankb[psi]], w=[xrb[tt][cb]])
            wrel()
        self.tap("x1", xres[:, :, :], [b for l in xrb for b in l], [128, 8, D], F32)
        if self.stop == "p4":
            return self.fin()
        CT.close()
        self.free([b for l in catTb for b in l])

        self.mark("mem")
        M = self.scope()
        h2T = self.tile(M, "h2T", [128, 16, TQ], BF16)
        h2Tb = [self.buf(f"h2T{c}") for c in range(2)]
        S1 = self.scope()
        rmsnorm_T(S1, lambda i: (xres[:, i, :], xrb[i]), 8, PV_NMQ, h2T, lambda i: h2Tb[i // 4], lambda i: i * 128)
        S1.close()
        qmnT = self.tile(M, "qmnT", [128, 4, TQ], BF16)
        qmnTb = [[self.buf(f"qmnT{h}_{t}") for t in range(2)] for h in range(4)]
        omT = self.tile(M, "omT", [128, 4, TQ], BF16)
        omTb = [[self.buf(f"omT{h}_{t}") for t in range(2)] for h in range(4)]
        for blk in range(2):
            wt, wb = wget()
            for cc in range(2):
                h = blk * 2 + cc
                for tc in range(2):
                    psi = proj_fm(wt, wb, cc, h2T, [h2Tb[tc]], tc * 512, 512)
                    pnorm(psi, 512, eps[:, 3:4], [epsb], qmnT[:, h, tc * 512:(tc + 1) * 512], qmnTb[h][tc])
            flush(0)
            wrel()
        Pm = [self.tile(M, f"Pm{j}", [128, 512], BF16) for j in range(3)]
        Pmb = [self.buf(f"Pm{j}") for j in range(3)]
        omem = [self.tile(M, f"omem{j}", [128, 4, 128], BF16) for j in range(2)]
        omemb = [self.buf(f"omem{j}") for j in range(2)]
        rdm = [self.tile(M, f"rdm{j}", [128, 4], F32) for j in range(2)]
        rdmb = [self.buf(f"rdm{j}") for j in range(2)]
        jobs = []
        it = 0
        for h in range(4):
            for tc in range(2):
                Ob = rot_O.next()
                oj = it % 2
                it += 1
                for mt in range(2):
                    def sc(h=h, tc=tc, mt=mt):
                        si = rot_S.next()
                        op(PE, lambda: nc.tensor.matmul(bank[si][:, :], lhsT=kmnT[:, h, mt * 128:(mt + 1) * 128], rhs=qmnT[:, h, tc * 512:(tc + 1) * 512], start=True, stop=True), r=[kmnTb[h], qmnTb[h][tc]], w=[bankb[si]])
                        pj = rot_P.next()
                        op(ACT, lambda: nc.scalar.activation(out=Pm[pj][:, :], in_=bank[si][:, :], func=AF.Exp), r=[bankb[si]], w=[Pmb[pj]])
                        return pj

                    def pvf(pj, h=h, tc=tc, mt=mt, Ob=Ob, oj=oj):
                        for j in range(4):
                            ob = Ob[j // 2]
                            o0 = (j % 2) * NV
                            op(PE, lambda: nc.tensor.matmul(bank[ob][:, o0:o0 + NV], lhsT=Pm[pj][:, j * 128:(j + 1) * 128], rhs=vmaug[:, mt, h, :], start=(mt == 0 and j % 2 == 0), stop=(mt == 1), skip_group_check=True), r=[Pmb[pj], vmaugb], w=[bankb[ob]])
                        if mt == 0:
                            return
                        for b2 in range(2):
                            ob = Ob[b2]
                            op(DVE, lambda: nc.vector.tensor_scalar(out=rdm[oj][:, 2 * b2:2 * b2 + 2], in0=bank[ob][:, 128:128 + NV + 1:NV], scalar1=1e-30, scalar2=None, op0=ALU.max), r=[bankb[ob]], pw=[rdmb[oj]] if b2 else [], w=[rdmb[oj]] if not b2 else [])
                        op(DVE, lambda: nc.vector.reciprocal(out=rdm[oj][:, :], in_=rdm[oj][:, :]), w=[rdmb[oj]])
                        for j in range(4):
                            ob = Ob[j // 2]
                            o0 = (j % 2) * NV
                            op(DVE, lambda: nc.vector.tensor_scalar(out=omem[oj][:, j, :], in0=bank[ob][:, o0:o0 + 128], scalar1=rdm[oj][:, j:j + 1], scalar2=None, op0=ALU.mult), r=[bankb[ob], rdmb[oj]], w=[omemb[oj]] if j == 0 else [], pw=[omemb[oj]] if j else [])
                        while pend_tr:
                            pend_tr.pop(0)()

                        def trf(h=h, tc=tc, oj=oj):
                            tb2 = rot_tr.next()
                            tv = bank[tb2][:, :].bitcast(BF16)
                            for j in range(4):
                                op(PE, lambda: nc.tensor.transpose(out=tv[:, j * 128:(j + 1) * 128], in_=omem[oj][:, j, :], identity=ident), r=[omemb[oj], cidb], w=[bankb[tb2]])
                            copy_op(evac_engine(), omT[:, h, tc * 512:(tc + 1) * 512], tv[:, 0:512], r=[bankb[tb2]], w=[omTb[h][tc]])
                        pend_tr.append(trf)
                    jobs.append((sc, pvf))
        pend_tr = []
        run_stream(jobs)
        while pend_tr:
            pend_tr.pop(0)()
        wmo = [wget(), wget()]
        for tt in range(8):
            for cb4 in range(4):
                psi = rot_proj.next()
                for kc in range(4):
                    wt, wb = wmo[kc // 2]
                    op(PE, lambda: nc.tensor.matmul(bank[psi][:, :], lhsT=omT[:, kc, tt * 128:(tt + 1) * 128], rhs=wt[:, kc % 2, cb4 * 512:(cb4 + 1) * 512], start=(kc == 0), stop=(kc == 3)), r=[wb, omTb[kc][tt // 4]], w=[bankb[psi]])
                dst = xres[:, tt, cb4 * 512:(cb4 + 1) * 512]
                op(DVE, lambda: nc.vector.tensor_tensor(out=dst, in0=bank[psi][:, :], in1=dst, op=ALU.add), r=[bankb[psi]], w=[xrb[tt][2 * cb4], xrb[tt][2 * cb4 + 1]])
        wrel()
        self.tap("x2", xres[:, :, :], [b for l in xrb for b in l], [128, 8, D], F32)
        if self.stop == "p5":
            return self.fin()
        M.close()
        self.free(h2Tb + [b for l in qmnTb for b in l] + [b for l in omTb for b in l] + Pmb + omemb + rdmb)
        MK.close()
        self.free([hmTb, vmaugb] + kmnTb)

        self.mark("ffn")
        Fz = self.scope()
        h3T = self.tile(Fz, "h3T", [128, 16, TQ], BF16)
        h3Tb = [self.buf(f"h3T{c}") for c in range(2)]
        S3 = self.scope()
        rmsnorm_T(S3, lambda i: (xres[:, i, :], xrb[i]), 8, PV_NFFN, h3T, lambda i: h3Tb[i // 4], lambda i: i * 128)
        S3.close()
        actT = [self.tile(Fz, f"actT{j}", [128, 2, TQ], BF16) for j in range(2)]
        actTb = [[[self.buf(f"actT{j}_{c}_{t}") for t in range(2)] for c in range(2)] for j in range(2)]
        sgt = [self.tile(Fz, f"sgt{j}", [128, 512], BF16) for j in range(2)]
        sgtb = [self.buf(f"sgt{j}") for j in range(2)]
        rot_g = Rot([4, 5])
        rot_u = Rot([6, 7])
        rot_sgt = Rot([0, 1])
        dheld = []
        for jb in range(22):
            wgt, wgb_ = wget()
            wut, wub = wget()
            aj = jb % 2
            for cc in range(2):
                for tc in range(2):
                    pg = rot_g.next()
                    pu = rot_u.next()
                    for kc in range(16):
                        op(PE, lambda: nc.tensor.matmul(bank[pg][:, :], lhsT=wgt[:, kc, cc * 128:(cc + 1) * 128], rhs=h3T[:, kc, tc * 512:(tc + 1) * 512], start=(kc == 0), stop=(kc == 15)), r=[wgb_, h3Tb[tc]], w=[bankb[pg]])
                    for kc in range(16):
                        op(PE, lambda: nc.tensor.matmul(bank[pu][:, :], lhsT=wut[:, kc, cc * 128:(cc + 1) * 128], rhs=h3T[:, kc, tc * 512:(tc + 1) * 512], start=(kc == 0), stop=(kc == 15)), r=[wub, h3Tb[tc]], w=[bankb[pu]])
                    sj = rot_sgt.next()
                    op(ACT, lambda: nc.scalar.activation(out=sgt[sj][:, :], in_=bank[pg][:, :], func=AF.Silu), r=[bankb[pg]], w=[sgtb[sj]])
                    op(DVE, lambda: nc.vector.tensor_tensor(out=actT[aj][:, cc, tc * 512:(tc + 1) * 512], in0=bank[pu][:, :], in1=sgt[sj][:, :], op=ALU.mult), r=[bankb[pu], sgtb[sj]], w=[actTb[aj][cc][tc]])
            wrel(2, newest=True)
            wdt, wdb = wget()
            dheld.append((aj, wdt, wdb))
            if jb % 2 == 0:
                continue
            for tt in range(8):
                for cb4 in range(4):
                    psi = rot_proj.next()
                    n_ = 0
                    for (a_, wd_, wdb_) in dheld:
                        for cc in range(2):
                            op(PE, lambda: nc.tensor.matmul(bank[psi][:, :], lhsT=actT[a_][:, cc, tt * 128:(tt + 1) * 128], rhs=wd_[:, cc, cb4 * 512:(cb4 + 1) * 512], start=(n_ == 0), stop=(n_ == 3)), r=[wdb_, actTb[a_][cc][tt // 4]], w=[bankb[psi]])
                            n_ += 1
                    dst = xres[:, tt, cb4 * 512:(cb4 + 1) * 512]
                    op(DVE, lambda: nc.vector.tensor_tensor(out=dst, in0=bank[psi][:, :], in1=dst, op=ALU.add), r=[bankb[psi]], w=[xrb[tt][2 * cb4], xrb[tt][2 * cb4 + 1]])
            dheld = []
            wrel()
        self.mark("out")
        for tt in range(8):
            ob_ = self.buf(f"out{tt}")
            dma(SP, self.y[tt * 128:(tt + 1) * 128, :], xres[:, tt, :], r=xrb[tt], owner=ob_)
            self.outtoks.append((ob_.dsem, ob_.dcnt))
        for sem, val in self.outtoks:
            SP.eng.wait_ge(sem.h, val)
        return nc


def _bf(a):
    return np.ascontiguousarray(a.astype(np.float32)).astype(NPBF)


def _consts(s):
    first_real = 0 if s == 1 else 1024
    k = np.arange(TC)
    lbs = np.zeros((128, TC), np.float32)
    lbs[k // 64, k] = BIG
    lbs[32, :] = 16 * (k // 16)
    lbs[33, :] = k % 16
    lbs[34, :] = 1.0
    lbs[35, :] = 1.0
    lbs[36, :] = -BIG
    lbs[37, :] = np.where(k < first_real, -BIG, 0.0)
    lbw = lbs.copy()
    lbw[0:32, :] = 0.0
    lbw[36, :] = 0.0
    c = np.arange(128)
    lbc = np.zeros((128, 128), np.float32)
    lbc[32, :] = 16 * c
    lbc[33, :] = 15.5
    lbc[34, :] = 1.0
    lbc[35, :] = 1.0
    kk = np.arange(128)[:, None]
    x = np.arange(896)[None, :]
    cs = np.where((x - 384) < kk, -BIG, 0.0)
    ws = np.where((x - 384) >= kk, -BIG, 0.0)
    t_ctx = 1024 + np.arange(TQ)[None, :]
    cc = np.arange(128)[:, None]
    cm = np.where((16 * cc + 31 > t_ctx) | (cc >= 127) | (16 * cc < first_real), -BIG, 0.0)
    cs_ = np.arange(128)[:, None] * 16
    bs_ = np.arange(32)[None, :] * 64
    ov = np.clip(np.minimum(cs_ + 32, bs_ + 64) - np.maximum(cs_, bs_), 0, None).astype(np.float32) / 32.0
    ovaug = np.concatenate([np.ones((128, 1), np.float32), ov], axis=1)
    ovaug[127, :] = 0.0
    ident = np.eye(128, dtype=np.float32)
    ones = np.ones((128, 128), np.float32)
    cbf = np.concatenate([lbs, lbw, lbc, cs, ws, cm, ovaug, ident, ones], axis=1)
    assert cbf.shape[1] == NCB
    rb = np.zeros((128, 16, 512), np.float32)
    slopes = 2.0 ** (-(np.arange(NH) + 1.0))
    for h in range(NH):
        for tc in range(2):
            t = 1024 + tc * 512 + np.arange(512)
            i = h * 2 + tc
            rb[0:32, i, :] = 1.0
            rb[32, i, :] = slopes[h]
            rb[33, i, :] = slopes[h]
            rb[34, i, :] = -slopes[h] * (16 * (t // 16))
            rb[35, i, :] = -slopes[h] * (t % 16)
            rb[36, i, :] = 1.0
            rb[37, i, :] = 1.0
    tq = 1024 + np.arange(TQ)[:, None]
    blk = np.arange(32)[None, :]
    cur = tq // 64
    valid = (blk * 64 <= tq) & (blk * 64 >= first_real)
    forced = (blk == first_real // 64) | (blk == cur) | (blk == cur - 1)
    bon = np.where(valid, 1000.0 * forced, -1e30).astype(np.float32)
    bon = bon.reshape(8, 128, 32).transpose(1, 0, 2).reshape(128, 256)
    return _bf(cbf), _bf(rb.reshape(128, 16 * 512)), np.ascontiguousarray(bon)


def _pvec(I):
    pv = np.zeros((128, NPV), np.float32)
    f = lambda a: np.asarray(a, np.float32)
    pv[:, PV_NMIX:PV_NMIX + 16] = f(I["norm_mix"])[0].reshape(16, 128).T
    pv[:, PV_NMQ:PV_NMQ + 16] = f(I["norm_mem_q"])[0].reshape(16, 128).T
    pv[:, PV_NMKV:PV_NMKV + 16] = f(I["norm_mem_kv"])[0].reshape(16, 128).T
    pv[:, PV_NFFN:PV_NFFN + 16] = f(I["norm_ffn"])[0].reshape(16, 128).T
    pv[:, PV_QN] = f(I["q_norm"])[0]
    pv[:, PV_KNC] = f(I["k_norm_cmp"])[0]
    pv[:, PV_KNS] = f(I["k_norm_slc"])[0]
    pv[:, PV_KNW] = f(I["k_norm_win"])[0]
    pv[:, PV_MQN] = f(I["mq_norm"])[0]
    pv[:, PV_MKN] = f(I["mk_norm"])[0]
    pv[:, PV_CB:PV_CB + 8] = f(I["conv_b"])[0].reshape(8, 128).T
    pv[:, PV_LNG:PV_LNG + 8] = f(I["conv_ln_g"])[0].reshape(8, 128).T
    pv[:, PV_LNB:PV_LNB + 8] = f(I["conv_ln_b"])[0].reshape(8, 128).T
    cw = f(I["conv_w"])[0]
    pv[:, PV_CW:PV_CW + 248] = cw.reshape(31, 8, 128).transpose(2, 1, 0).reshape(128, 248)
    pv[:, PV_GB:PV_GB + 24] = f(I["gate_b"])[0][None, :]
    pv[:, PV_POSK:PV_POSK + 32] = f(I["cmp_pos_k"])[0].T
    pv[:, PV_POSV:PV_POSV + 32] = f(I["cmp_pos_v"])[0].T
    return pv


def make_in_maps(I):
    f = lambda a: np.ascontiguousarray(np.asarray(a, np.float32))
    x = f(I["x"])
    mem = f(I["mem"])
    w_in = f(I["w_in"])[0]
    shared = {
        "w_in": w_in, "w_g": np.ascontiguousarray(w_in[:, C_G:C_G + 24]),
        "w_out": f(I["w_out"])[0], "w_mq": f(I["w_mq"])[0], "w_mk": f(I["w_mk"])[0], "w_mv": f(I["w_mv"])[0],
        "w_mo": f(I["w_mo"])[0], "w_gate": f(I["w_gate"])[0], "w_up": f(I["w_up"])[0], "w_down": f(I["w_down"])[0],
        "w1k": f(I["cmp_k_w1"])[0], "w1v": f(I["cmp_v_w1"])[0], "w2k": f(I["cmp_k_w2"])[0], "w2v": f(I["cmp_v_w2"])[0],
        "pvec": _pvec(I),
    }
    cs = [_consts(0), _consts(1)]
    maps = []
    for core in range(8):
        b, s = core // 2, core % 2
        if s == 1:
            xc = x[b]
        else:
            xc = np.concatenate([np.zeros((TQ, D), np.float32), x[b, :TQ]], axis=0)
        m = dict(shared)
        m["xc"] = np.ascontiguousarray(xc)
        m["memb"] = mem[b]
        m["cbf"], m["rb"], m["bonus"] = cs[s]
        maps.append(m)
    return maps


_CACHE = {}


def kernel(**inputs):
    if "k" not in _CACHE:
        K = Kern()
        K.build()
        _CACHE["k"] = K
    K = _CACHE["k"]
    maps = make_in_maps(inputs)
    res = run_bass_kernel_spmd(K.nc, maps, core_ids=list(range(8)))
    out = np.zeros((4, 2048, D), np.float32)
    for core in range(8):
        b, s = core // 2, core % 2
        out[b, s * TQ:(s + 1) * TQ] = np.asarray(res.results[core]["y"], np.float32).reshape(TQ, D)
    return out
```
